# Optimizing a Trainium2 kernel written in Bass

```python
import jax, jax.numpy as jnp
from jax import lax
import numpy as np

D_MODEL = 1024
BATCH = 2
SEQ = 8192
DEPTH = 2

GRID_W = 64
CTX_LEN = 256
N_MIXERS = 2
EXPAND = 2
D_INNER = EXPAND * D_MODEL
LRU_BLOCKS = 16
LRU_BLOCK = D_INNER // LRU_BLOCKS
CONV_W = 4
CONV_LEFT = 2
LRU_C = 8.0
POOL_WINDOWS = (2, 4, 8, 16)
N_POOL_GROUPS = len(POOL_WINDOWS)
POOL_GROUP = D_INNER // N_POOL_GROUPS
N_A_LAYERS = (DEPTH + 1) // 2
N_B_LAYERS = DEPTH // 2
ALPHA = float((2 * DEPTH) ** 0.25)
BETA = float((8 * DEPTH) ** -0.25)
LN_EPS = 1e-5

kernel_name = "hybrid_rglru_pool_deepnorm_prefix"


def _layer_norm(v, g, b):
    vf = v.astype(jnp.float32)
    mu = jnp.mean(vf, axis=-1, keepdims=True)
    var = jnp.mean(jnp.square(vf - mu), axis=-1, keepdims=True)
    y = (vf - mu) * lax.rsqrt(var + LN_EPS) * g.astype(jnp.float32) + b.astype(jnp.float32)
    return y.astype(v.dtype)


def _adaln(cvec, w_mod, b_mod):
    m = jax.nn.silu(cvec) @ w_mod + b_mod
    shift, scale, gate = jnp.split(m, 3, axis=-1)
    return shift, scale, gate


def _centred_dwconv(u, w, b):
    L = u.shape[1]
    up = jnp.pad(u, ((0, 0), (CONV_LEFT, CONV_W - 1 - CONV_LEFT), (0, 0)))
    out = up[:, 0:L] * w[0]
    for k in range(1, CONV_W):
        out = out + up[:, k:k + L] * w[k]
    return out + b


def _lru_coeffs(uf, wa, ba, wx, bx, lam):
    bn, L, _ = uf.shape
    ub = uf.reshape(bn, L, LRU_BLOCKS, LRU_BLOCK)
    r = jax.nn.sigmoid(jnp.einsum('blnh,nhk->blnk', ub, wa.astype(jnp.float32)).reshape(bn, L, D_INNER) + ba.astype(jnp.float32))
    i = jax.nn.sigmoid(jnp.einsum('blnh,nhk->blnk', ub, wx.astype(jnp.float32)).reshape(bn, L, D_INNER) + bx.astype(jnp.float32))
    log_a = LRU_C * r * jax.nn.log_sigmoid(lam.astype(jnp.float32))
    a = jnp.exp(log_a)
    drive = jnp.sqrt(-jnp.expm1(2.0 * log_a)) * (i * uf)
    return a, drive


def _linear_scan(a, b, h0, reverse):
    if h0 is not None:
        if reverse:
            b = b.at[:, -1].add(a[:, -1] * h0)
        else:
            b = b.at[:, 0].add(a[:, 0] * h0)

    def combine(left, right):
        a1, b1 = left
        a2, b2 = right
        return a1 * a2, a2 * b1 + b2

    _, h = lax.associative_scan(combine, (a, b), reverse=reverse, axis=1)
    return h


def _rglru(u, wa, ba, wx, bx, lam, h0_f, h0_b):
    uf = u.astype(jnp.float32)
    af, df = _lru_coeffs(uf, wa[0], ba[0], wx[0], bx[0], lam[0])
    ab, db = _lru_coeffs(uf, wa[1], ba[1], wx[1], bx[1], lam[1])
    h_fwd = _linear_scan(af, df, h0_f, False)
    h_bwd = _linear_scan(ab, db, h0_b, True)
    return h_fwd, h_bwd


def _window_mean(v, w, axis):
    n = v.shape[axis]
    cs = jnp.cumsum(v.astype(jnp.float32), axis=axis)
    pad = [(0, 0)] * v.ndim
    pad[axis] = (1, 0)
    cs = jnp.pad(cs, pad)
    t = np.arange(n)
    lo = np.clip(t - w // 2, 0, n)
    hi = np.clip(t + w // 2, 0, n)
    s = jnp.take(cs, hi, axis=axis) - jnp.take(cs, lo, axis=axis)
    cnt_shape = [1] * v.ndim
    cnt_shape[axis] = n
    cnt = jnp.asarray((hi - lo).astype(np.float32)).reshape(cnt_shape)
    return s / cnt


def _pool_grid(u, w_p, scale, rows):
    bn = u.shape[0]
    ug = u.reshape(bn, rows, GRID_W, D_INNER)
    outs = []
    for k, w in enumerate(POOL_WINDOWS):
        seg = ug[..., k * POOL_GROUP:(k + 1) * POOL_GROUP]
        m = _window_mean(_window_mean(seg, w, 2), w, 1)
        d = (m - seg.astype(jnp.float32)).astype(u.dtype)
        outs.append(jnp.einsum('brcg,gh->brch', d, w_p[k]))
    y = jnp.concatenate(outs, axis=-1).reshape(bn, rows * GRID_W, D_INNER)
    return y * scale


def _pool_seq(u, w_p, scale):
    outs = []
    for k, w in enumerate(POOL_WINDOWS):
        seg = u[..., k * POOL_GROUP:(k + 1) * POOL_GROUP]
        d = (_window_mean(seg, w, 1) - seg.astype(jnp.float32)).astype(u.dtype)
        outs.append(jnp.einsum('blg,gh->blh', d, w_p[k]))
    return jnp.concatenate(outs, axis=-1) * scale


def _ctx_needed_after(i):
    return any(j % N_MIXERS == 0 for j in range(i + 1, DEPTH))


def setup_inputs(seed: int = 0) -> dict:
    key = jax.random.key(seed)
    ks = jax.random.split(key, 20)
    f32 = jnp.float32
    x = jax.random.normal(ks[0], (BATCH, SEQ, D_MODEL), f32)
    c = jax.random.normal(ks[1], (BATCH, D_MODEL), f32)
    ctx = jax.random.normal(ks[2], (BATCH, CTX_LEN, D_MODEL), f32)
    c_ctx = jax.random.normal(ks[3], (D_MODEL,), f32)
    w_mod = jax.random.normal(ks[4], (DEPTH, D_MODEL, 3 * D_MODEL), f32) * (0.5 * D_MODEL ** -0.5)
    b_mod = jax.random.normal(ks[5], (DEPTH, 3 * D_MODEL), f32) * 0.02
    w_in = jax.random.normal(ks[6], (DEPTH, D_MODEL, 2 * D_INNER), f32) * D_MODEL ** -0.5
    w_out = jax.random.normal(ks[7], (DEPTH, D_INNER, D_MODEL), f32) * (D_INNER ** -0.5 * BETA)
    ln_g = 1.0 + 0.02 * jax.random.normal(ks[8], (DEPTH, D_MODEL), f32)
    ln_b = 0.02 * jax.random.normal(ks[9], (DEPTH, D_MODEL), f32)
    conv_w = jax.random.normal(ks[10], (N_A_LAYERS, CONV_W, D_INNER), f32) * CONV_W ** -0.5
    conv_b = 0.02 * jax.random.normal(ks[11], (N_A_LAYERS, D_INNER), f32)
    lru_wa = jax.random.normal(ks[12], (N_A_LAYERS, 2, LRU_BLOCKS, LRU_BLOCK, LRU_BLOCK), f32) * LRU_BLOCK ** -0.5
    lru_ba = 0.02 * jax.random.normal(ks[13], (N_A_LAYERS, 2, D_INNER), f32)
    lru_wx = jax.random.normal(ks[14], (N_A_LAYERS, 2, LRU_BLOCKS, LRU_BLOCK, LRU_BLOCK), f32) * LRU_BLOCK ** -0.5
    lru_bx = 0.02 * jax.random.normal(ks[15], (N_A_LAYERS, 2, D_INNER), f32)
    a_pow_c = jax.random.uniform(ks[16], (N_A_LAYERS, 2, D_INNER), f32, minval=0.9, maxval=0.999)
    a0 = a_pow_c ** (1.0 / LRU_C)
    lru_lam = jnp.log(a0) - jnp.log1p(-a0)
    pool_w = jax.random.normal(ks[17], (N_B_LAYERS, N_POOL_GROUPS, POOL_GROUP, POOL_GROUP), f32) * POOL_GROUP ** -0.5
    pool_scale = 1.0 + 0.02 * jax.random.normal(ks[18], (N_B_LAYERS, D_INNER), f32)
    return {"x": x, "c": c, "ctx": ctx, "c_ctx": c_ctx, "w_mod": w_mod, "b_mod": b_mod,
            "w_in": w_in, "w_out": w_out, "ln_g": ln_g, "ln_b": ln_b,
            "conv_w": conv_w, "conv_b": conv_b, "lru_wa": lru_wa, "lru_ba": lru_ba,
            "lru_wx": lru_wx, "lru_bx": lru_bx, "lru_lam": lru_lam,
            "pool_w": pool_w, "pool_scale": pool_scale}


def reference(x, c, ctx, c_ctx, w_mod, b_mod, w_in, w_out, ln_g, ln_b,
              conv_w, conv_b, lru_wa, lru_ba, lru_wx, lru_bx, lru_lam,
              pool_w, pool_scale):
    rows = x.shape[1] // GRID_W
    xc = ctx
    for i in range(DEPTH):
        kind = i % N_MIXERS
        j = i // N_MIXERS
        ctx_out = _ctx_needed_after(i)
        sh, sc, gt = _adaln(c, w_mod[i], b_mod[i])
        h = x * (1.0 + sc[:, None]) + sh[:, None]
        u, g = jnp.split(h @ w_in[i], 2, axis=-1)
        if kind == 0 or ctx_out:
            shc, scc, gtc = _adaln(c_ctx, w_mod[i], b_mod[i])
            hc = xc * (1.0 + scc) + shc
            if ctx_out:
                uc, gc = jnp.split(hc @ w_in[i], 2, axis=-1)
            else:
                uc = hc @ w_in[i][:, :D_INNER]
        if kind == 0:
            uc = _centred_dwconv(uc, conv_w[j], conv_b[j])
            hcf, hcb = _rglru(uc, lru_wa[j], lru_ba[j], lru_wx[j], lru_bx[j], lru_lam[j], None, None)
            u = _centred_dwconv(u, conv_w[j], conv_b[j])
            hf, hb = _rglru(u, lru_wa[j], lru_ba[j], lru_wx[j], lru_bx[j], lru_lam[j],
                            hcf[:, -1], hcb[:, 0])
            y = (hf + hb).astype(x.dtype)
            if ctx_out:
                yc = (hcf + hcb).astype(xc.dtype)
        else:
            y = _pool_grid(u, pool_w[j], pool_scale[j], rows)
            if ctx_out:
                yc = _pool_seq(uc, pool_w[j], pool_scale[j])
        branch = (y * jax.nn.silu(g)) @ w_out[i]
        x = _layer_norm(ALPHA * x + gt[:, None] * branch, ln_g[i], ln_b[i])
        if ctx_out:
            branch_c = (yc * jax.nn.silu(gc)) @ w_out[i]
            xc = _layer_norm(ALPHA * xc + gtc * branch_c, ln_g[i], ln_b[i])
    return x
```

```python
import contextlib
import numpy as np
import concourse.bass as bass
import concourse.mybir as mybir
from concourse.bass_utils import run_bass_kernel_spmd

F32 = mybir.dt.float32
BF16 = mybir.dt.bfloat16
ALU = mybir.AluOpType
AF = mybir.ActivationFunctionType

D = 1024
DI = 2048
NDC = 8
NCH = 16
TOK = 2048
TT = 1024
CTX = 256
ALPHA = float(4 ** 0.25)
LN_EPS = 1e-5
NCORES = 8
SAME_ENG_SYNC = True


class Buf:
    def __init__(self, name):
        self.name = name
        self.w = None
        self.r = []


class Sched:
    ENGS = ("pe", "act", "dve", "pool", "sp")

    def __init__(self, nc, stack, n_dma=40):
        self.nc = nc
        self.q = {e: [] for e in self.ENGS}
        self.sems = {}
        for e in ("pe", "act", "dve", "pool"):
            self.sems[e] = stack.enter_context(nc.semaphore("sem_" + e))
        self.dma_free = [stack.enter_context(nc.semaphore("sem_dma%d" % i)) for i in range(n_dma)]
        self.cnt = {}
        self.waited = {e: {} for e in self.ENGS}
        self.dma_keys = {}

    def _wait(self, eng, tok):
        if tok is None:
            return
        key, val = tok
        if not SAME_ENG_SYNC and key == eng:
            return
        if self.waited[eng].get(key, 0) >= val:
            return
        self.waited[eng][key] = val
        sem = self.sems[key]
        self.q[eng].append(lambda e, sem=sem, val=val: e.wait_ge(sem, val))

    def _deps(self, eng, reads, writes):
        for b in reads:
            self._wait(eng, b.w)
        for b in writes:
            self._wait(eng, b.w)
            for t in b.r:
                self._wait(eng, t)

    def _mark(self, tok, reads, writes):
        for b in writes:
            b.w = tok
            b.r = []
        for b in reads:
            if b not in writes:
                b.r.append(tok)

    def op(self, eng, fn, reads=(), writes=()):
        self._deps(eng, reads, writes)
        self.cnt[eng] = self.cnt.get(eng, 0) + 1
        tok = (eng, self.cnt[eng])
        sem = self.sems[eng]
        self.q[eng].append(lambda e, fn=fn, sem=sem: fn(e).then_inc(sem, 1))
        self._mark(tok, reads, writes)
        return tok

    def dma(self, eng, key, out, in_, reads=(), writes=()):
        if key not in self.dma_keys:
            self.dma_keys[key] = "dma_" + key
            self.sems["dma_" + key] = self.dma_free.pop()
        k = self.dma_keys[key]
        self._deps(eng, reads, writes)
        self.cnt[k] = self.cnt.get(k, 0) + 16
        tok = (k, self.cnt[k])
        sem = self.sems[k]
        self.q[eng].append(lambda e, out=out, in_=in_, sem=sem: e.dma_start(out=out, in_=in_).then_inc(sem, 16))
        self._mark(tok, reads, writes)
        return tok

    def barrier(self):
        toks = [(k, v) for k, v in self.cnt.items() if v > 0]
        for eng in self.ENGS:
            for t in toks:
                self._wait(eng, t)

    def final_wait(self, eng, toks):
        for t in toks:
            self._wait(eng, t)

    def emit(self, block):
        q = self.q

        @block.tensor
        def _(e):
            for f in q["pe"]:
                f(e)

        @block.scalar
        def _(e):
            for f in q["act"]:
                f(e)

        @block.vector
        def _(e):
            for f in q["dve"]:
                f(e)

        @block.gpsimd
        def _(e):
            for f in q["pool"]:
                f(e)

        @block.sync
        def _(e):
            for f in q["sp"]:
                f(e)


class Arena:
    def __init__(self, nc, nbytes):
        self.t = nc.alloc_sbuf_tensor("arena", [128, nbytes // 4], F32)
        self.off = 0
        self.cap = nbytes

    def f32(self, name, *shape):
        n = int(np.prod(shape))
        ap = self.t[:, self.off // 4:self.off // 4 + n]
        self.off += n * 4
        assert self.off <= self.cap, (name, self.off, self.cap)
        return self._shape(ap, shape)

    def bf16(self, name, *shape):
        n = int(np.prod(shape))
        nb = (n * 2 + 3) // 4 * 4
        ap = self.t[:, self.off // 4:self.off // 4 + nb // 4].bitcast(BF16)[:, 0:n]
        self.off += nb
        assert self.off <= self.cap, (name, self.off, self.cap)
        return self._shape(ap, shape)

    @staticmethod
    def _shape(ap, shape):
        if len(shape) == 1:
            return ap
        if len(shape) == 2:
            return ap.rearrange("p (a b) -> p a b", a=shape[0])
        if len(shape) == 3:
            return ap.rearrange("p (a b c) -> p a b c", a=shape[0], b=shape[1])
        if len(shape) == 4:
            return ap.rearrange("p (a b c d) -> p a b c d", a=shape[0], b=shape[1], c=shape[2])
        raise ValueError


def build(kind):
    nc = bass.Bass("TRN2", target_bir_lowering=False)
    layer = 1 if kind == "L1" else 0

    def din(name, shape, dt=F32):
        return nc.dram_tensor(name, list(shape), dt, kind="ExternalInput").ap()

    def dout(name, shape, dt=F32):
        return nc.dram_tensor(name, list(shape), dt, kind="ExternalOutput").ap()

    if kind == "A":
        d_cv = din("cv", [128, NDC, 2])
        d_wm = [din("wm", [24, 128, NDC, 128]), din("wm1", [24, 128, NDC, 128])]
        d_bm = [din("bm", [128, 24]), din("bm1", [128, 24])]
        o_mod = [dout("mod0", [128, 24, 2]), dout("mod1", [128, 24, 2])]
    else:
        d_modin = din("modin", [128, 24, 2])
    d_win = din("win", [32, 128, NDC, 128])
    if kind in ("A", "B"):
        d_x = din("x", [128, NDC, TOK])
        d_xh = din("xh", [128, NDC, 2, 3])
        d_hmask = din("hmask", [128, 2, 3])
        d_ctx = din("ctx", [128, NDC, CTX])
        d_cw = din("cw", [128, NCH, 4])
        d_cb = din("cb", [128, NCH])
        d_gw = din("gw", [NCH, 128, 4, 128])
        d_gb = din("gb", [128, 4, NCH])
        d_lam = din("lam", [128, 2, NCH])
    NCC = 4
    if kind == "A":
        o_ep = dout("ep", [128, 2, NCH, 4])
        o_hc = dout("hc", [128, NCC, 2])
        d_winc = din("winc", [NCC, 128, NDC, 128])
        d_gwc = din("gwc", [NCC, 128, 4, 128])
        d_cwc = din("cwc", [128, NCC, 4])
        d_cbc = din("cbc", [128, NCC])
        d_gbc = din("gbc", [128, 4, NCC])
        d_lamc = din("lamc", [128, 2, NCC])
    if kind == "B":
        d_epf = din("epf", [128, 2, 7, NCH, 2])
        d_epb = din("epb", [128, 2, 7, NCH, 2])
        d_hc = din("hcin", [128, NCH, 2])
    if kind == "L1":
        d_x = din("x", [128, NDC, TOK])
        d_xhalo = din("xhalo", [128, NDC, 960])
        d_icnt = din("icnt", [4, 128, TOK])
        d_pw = din("pw", [4, 128, 4, 512])
        d_ps = din("ps", [128, NCH])
        d_hm1 = din("hm1", [128, 2])
    if kind in ("B", "L1"):
        d_wout = din("wout", [NCH, 128, D])
        d_lng = din("lng", [128, NDC])
        d_lnb = din("lnb", [128, NDC])
        o_x = dout("xo", [128, NDC, TOK])

    stack = contextlib.ExitStack()
    with stack:
        S = Sched(nc, stack)
        AR = Arena(nc, 206 * 1024)
        psum = nc.alloc_psum_tensor("psum", [128, 8, 512], F32)
        P01 = psum[:, 0:2, :].rearrange("p a b -> p (a b)")
        P23 = psum[:, 2:4, :].rearrange("p a b -> p (a b)")
        P45 = psum[:, 4:6, :].rearrange("p a b -> p (a b)")
        P6 = psum[:, 6, :]
        P7 = psum[:, 7, :]
        bP01, bP23, bP45, bP6, bP7 = (Buf(n) for n in ("P01", "P23", "P45", "P6", "P7"))

        CV = AR.f32("CV", NDC, 2)
        BM = AR.f32("BM", 24)
        MOD = AR.f32("MOD", 24, 2)
        SC1 = AR.f32("SC1", NDC, 2)
        SCV = AR.f32("SCV", NDC, 2)
        TMPC = AR.f32("TMPC", NDC, 2)
        NWM = 6 if kind == "A" else 2
        WM = [AR.f32("WM%d" % i, NDC, 128) for i in range(NWM)] if kind != "B" else []
        bWM = [Buf("WM%d" % i) for i in range(NWM)]
        bCV, bBM, bMOD, bSC1, bSCV, bTMPC = (Buf(n) for n in ("CV", "BM", "MOD", "SC1", "SCV", "TMPC"))
        MOD1 = AR.f32("MOD1", 24, 2); bMOD1 = Buf("MOD1")

        def adaln_step(l, jc):
            sl = jc % NWM
            S.dma("sp", "wm%d" % sl, WM[sl], d_wm[l][jc], writes=[bWM[sl]])

            def mm(e, jc=jc, sl=sl):
                for dc in range(NDC):
                    r = e.matmul(P7[:, 2 * jc:2 * jc + 2], lhsT=WM[sl][:, dc, :], rhs=SCV[:, dc, :],
                                 start=(dc == 0), stop=(dc == NDC - 1))
                return r
            S.op("pe", mm, [bWM[sl], bSCV], [bP7])

        def adaln_finish(l, MODd, bMODd, lo=0, hi=24, load_bm=True):
            if load_bm:
                S.dma("sp", "bm", BM, d_bm[l], writes=[bBM])
            P7m = P7[:, 0:48].rearrange("p (a b) -> p a b", b=2)
            for col in range(2):
                S.op("dve", lambda e, col=col: e.tensor_tensor(out=MODd[:, lo:hi, col], in0=P7m[:, lo:hi, col], in1=BM[:, lo:hi], op=ALU.add),
                     [bP7, bBM], [bMODd])

        def adaln(l, MODd, bMODd):
            for jc in range(24):
                adaln_step(l, jc)
            adaln_finish(l, MODd, bMODd)

        if kind == "A":
            S.dma("sp", "cv", CV, d_cv, writes=[bCV])
            S.op("act", lambda e: e.activation(out=TMPC, in_=CV, func=AF.Tanh, scale=0.5), [bCV], [bTMPC])
            S.op("dve", lambda e: e.scalar_tensor_tensor(out=TMPC, in0=TMPC, scalar=1.0, in1=CV, op0=ALU.add, op1=ALU.mult),
                 [bCV, bTMPC], [bTMPC])
            S.op("dve", lambda e: e.tensor_scalar(out=SCV, in0=TMPC, scalar1=0.5, scalar2=None, op0=ALU.mult), [bTMPC], [bSCV])
            for jc in range(16):
                adaln_step(0, jc)
            adaln_finish(0, MOD, bMOD, 0, 16)
        else:
            S.dma("sp", "modin", MOD, d_modin, writes=[bMOD])
        S.op("dve", lambda e: e.tensor_scalar(out=SC1, in0=MOD[:, 8:16, :], scalar1=1.0, scalar2=None, op0=ALU.add),
             [bMOD], [bSC1])
        SH = MOD[:, 0:8, :]
        GT = MOD[:, 16:24, :]

        XACC = AR.f32("XACC", NDC, TOK if kind != "A" else TT)
        bXc = [[Buf("X%d_%d" % (dc, t)) for t in range(4)] for dc in range(NDC)]

        def bXs(dcs=range(NDC), tcs=range(4)):
            return [bXc[dc][t] for dc in dcs for t in tcs]
        def load_x(j_):
            if kind == "A":
                S.dma("sp", "xload%d" % j_, XACC, d_x[:, :, j_ * TT:(j_ + 1) * TT], writes=bXs(tcs=[0, 1]))
            else:
                S.dma("sp", "xload%d" % j_, XACC[:, :, j_ * TT:(j_ + 1) * TT], d_x[:, :, j_ * TT:(j_ + 1) * TT], writes=bXs(tcs=[2 * j_, 2 * j_ + 1]))
        load_x(0)
        if kind != "A":
            load_x(1)

        out_toks = []

        if kind in ("A", "B"):
            CW = AR.f32("CW", NCH, 4); bCW = Buf("CW")
            CB = AR.f32("CB", NCH); bCB = Buf("CB")
            GB = AR.f32("GB", 4, NCH); bGB = Buf("GB")
            LAM = AR.f32("LAM", 2, NCH); bLAM = Buf("LAM")
            HSs = AR.f32("HS", 2, NCH); bHS = Buf("HSs")
            SS = AR.f32("SS", 2, NCH)
            LT = AR.f32("LT", 2, NCH); LT2 = AR.f32("LT2", 2, NCH)
            HMASK = AR.f32("HMASK", 2, 3); bHM = Buf("HMASK")
            if kind == "A":
                ZERO = AR.f32("ZERO", TT); bZERO = Buf("ZERO")
            S.dma("sp", "c0", CW, d_cw, writes=[bCW])
            S.dma("sp", "c1", CB, d_cb, writes=[bCB])
            S.dma("sp", "c2", GB, d_gb, writes=[bGB])
            S.dma("sp", "c3", LAM, d_lam, writes=[bLAM])
            S.dma("sp", "c4", HMASK, d_hmask, writes=[bHM])
            if kind == "A":
                S.op("pool", lambda e: e.memset(ZERO, 0.0), [], [bZERO])
            def prep_consts(GB_, bGB_, LAM_, bLAM_, LT_, LT2_, SS_, HS_, bHS_):
                S.op("dve", lambda e: e.tensor_scalar(out=GB_, in0=GB_, scalar1=0.5, scalar2=None, op0=ALU.mult), [bGB_], [bGB_])
                S.op("act", lambda e: e.activation(out=LT_, in_=LAM_, func=AF.Exp, scale=-1.0), [bLAM_], [bHS_])
                S.op("dve", lambda e: e.tensor_scalar(out=LT2_, in0=LT_, scalar1=-0.2, scalar2=0.25, op0=ALU.mult, op1=ALU.add), [bHS_], [bHS_])
                for cst in (1.0 / 3.0, 0.5, 1.0):
                    S.op("dve", lambda e: e.tensor_tensor(out=LT2_, in0=LT2_, in1=LT_, op=ALU.mult), [bHS_], [bHS_])
                    S.op("dve", lambda e, cst=cst: e.tensor_scalar(out=LT2_, in0=LT2_, scalar1=-1.0, scalar2=cst, op0=ALU.mult, op1=ALU.add), [bHS_], [bHS_])
                S.op("dve", lambda e: e.tensor_tensor(out=LT2_, in0=LT2_, in1=LT_, op=ALU.mult), [bHS_], [bHS_])
                S.op("dve", lambda e: e.tensor_scalar(out=SS_, in0=LT2_, scalar1=-8.0, scalar2=None, op0=ALU.mult), [bHS_], [bHS_])
                S.op("dve", lambda e: e.tensor_scalar(out=HS_, in0=LT2_, scalar1=-4.0, scalar2=None, op0=ALU.mult), [bHS_], [bHS_])

            prep_consts(GB, bGB, LAM, bLAM, LT, LT2, SS, HSs, bHS)
            main_cs = dict(CW=CW, CB=CB, GB=GB, HSs=HSs, SS=SS, bCW=bCW, bCB=bCB, bGB=bGB, bHS=bHS, win=d_win, gw=d_gw, nch=NCH)
            cur = dict(main_cs)
            if kind == "A":
                CWc = AR.f32("CWc", NCC, 4); CBc = AR.f32("CBc", NCC); GBc = AR.f32("GBc", 4, NCC); LAMc = AR.f32("LAMc", 2, NCC)
                HSc = AR.f32("HSc", 2, NCC); SSc = AR.f32("SSc", 2, NCC); LTc = AR.f32("LTc", 2, NCC); LT2c = AR.f32("LT2c", 2, NCC)
                bCWc, bCBc, bGBc, bLAMc, bHSc = (Buf(x) for x in ("CWc", "CBc", "GBc", "LAMc", "HSc"))
                S.dma("sp", "cc0", CWc, d_cwc, writes=[bCWc])
                S.dma("sp", "cc1", CBc, d_cbc, writes=[bCBc])
                S.dma("sp", "cc2", GBc, d_gbc, writes=[bGBc])
                S.dma("sp", "cc3", LAMc, d_lamc, writes=[bLAMc])
                prep_consts(GBc, bGBc, LAMc, bLAMc, LTc, LT2c, SSc, HSc, bHSc)
                ctx_cs = dict(CW=CWc, CB=CBc, GB=GBc, HSs=HSc, SS=SSc, bCW=bCWc, bCB=bCBc, bGB=bGBc, bHS=bHSc, win=d_winc, gw=d_gwc, nch=NCC)

            work_mark = AR.off
            H = AR.bf16("H", NDC, TT + 3); bH = Buf("H")
            XH = AR.f32("XH", NDC, 2, 3); bXH = Buf("XH")
            S.dma("sp", "c5", XH, d_xh, writes=[bXH])
            WU = [AR.bf16("WU%d" % i, NDC, 128) for i in range(3)]; bWU = [Buf("WU0"), Buf("WU1"), Buf("WU2")]
            GW = [AR.bf16("GW%d" % i, 4, 128) for i in range(2)]; bGW = [Buf("GW0"), Buf("GW1")]
            U = AR.f32("U", TT); bU = Buf("U")
            US = AR.f32("US", 4); bUS = Buf("US")
            UCs = [AR.f32("UC%d" % i, TT) for i in range(2)]; bUCs = [Buf("UC0"), Buf("UC1")]
            UCBs = [AR.bf16("UCB%d" % i, TT) for i in range(2)]; bUCBs = [Buf("UCB0"), Buf("UCB1")]
            THR = [AR.f32("THR%d" % i, TT) for i in range(2)]; bTHR = [Buf("THR0"), Buf("THR1")]
            THIs = [[AR.f32("THI%d%d" % (p, i), TT) for i in range(2)] for p in range(2)]
            bTHIs = [[Buf("THI%d%d" % (p, i)) for i in range(2)] for p in range(2)]
            AAs = [[AR.f32("AA%d%d" % (p, i), TT) for i in range(2)] for p in range(2)]
            bAAs = [[Buf("AA%d%d" % (p, i)) for i in range(2)] for p in range(2)]
            VVall = AR.f32("VV", 2, TT)
            VV = [VVall[:, 0, :], VVall[:, 1, :]]; bVV = [Buf("VV0"), Buf("VV1")]
            HSC = AR.f32("HSC", TT); bHSC = Buf("HSC")
            if kind == "A":
                RSUM = AR.f32("RSUM", 2); bRSUM = [Buf("RSUM0"), Buf("RSUM1")]
                HST = AR.f32("HST", 2, NCH)
                S.op("dve", lambda e: e.tensor_scalar(out=HST, in0=HSs, scalar1=float(TT), scalar2=None, op0=ALU.mult), [bHS], [bHS])

            if kind == "A":
                CTXT = AR.f32("CTXT", NDC, CTX); bCTXT = Buf("CTXT")
                S.dma("sp", "ctxl", CTXT, d_ctx, writes=[bCTXT])
                EP = AR.f32("EP", 2, NCH, 4); bEP = Buf("EP")
                EPP = AR.f32("EPP", 2, NCH, 2); bEPP = Buf("EPP")
                HCO = AR.f32("HCO", NCC, 2); bHCO = Buf("HCO")
            if kind == "B":
                WG = [AR.bf16("WG%d" % i, NDC, 128) for i in range(2)]; bWG = [Buf("WG0"), Buf("WG1")]
                HF = AR.f32("HF", TT); bHF = Buf("HF")
                HB = AR.f32("HB", TT); bHB = Buf("HB")
                TG = AR.f32("TG", TT); bTG = Buf("TG")
                T1 = AR.f32("T1", TT); bT1 = Buf("T1")
                Z = AR.bf16("Z", 4, TT); bZ = Buf("Z")
                WO = AR.bf16("WO", 4, D); bWO = Buf("WO")
                EPF = AR.f32("EPF", 2, 7, NCH, 2); bEPF = Buf("EPF")
                EPB = AR.f32("EPB", 2, 7, NCH, 2); bEPB = Buf("EPB")
                HCI = AR.f32("HCI", NCH, 2); bHCI = Buf("HCI")
                CAR = AR.f32("CAR", 2, 2, NCH); bCAR = Buf("CAR")
                S.dma("sp", "c6", EPF, d_epf, writes=[bEPF])
                S.dma("sp", "c7", EPB, d_epb, writes=[bEPB])
                S.dma("sp", "c8", HCI, d_hc, writes=[bHCI])
                for j in range(2):
                    for dr, EPX, bEPX in ((0, EPF, bEPF), (1, EPB, bEPB)):
                        S.op("dve", lambda e, j=j, dr=dr: e.tensor_copy(out=CAR[:, j, dr, :], in_=HCI[:, :, dr]), [bHCI], [bCAR])
                        for s_ in range(7):
                            S.op("dve", lambda e, j=j, dr=dr, s_=s_, EPX=EPX: e.tensor_tensor(
                                out=CAR[:, j, dr, :], in0=CAR[:, j, dr, :], in1=EPX[:, j, s_, :, 0], op=ALU.mult), [bEPX, bCAR], [bCAR])
                            S.op("dve", lambda e, j=j, dr=dr, s_=s_, EPX=EPX: e.tensor_tensor(
                                out=CAR[:, j, dr, :], in0=CAR[:, j, dr, :], in1=EPX[:, j, s_, :, 1], op=ALU.add), [bEPX, bCAR], [bCAR])

            def load_wu(n):
                S.dma("pool", "wu%d" % (n % 3), WU[n % 3], cur["win"][n], writes=[bWU[n % 3]])

            def load_gw(n, with_g):
                slot = n % 2
                S.dma("pool", "gw%d" % slot, GW[slot], cur["gw"][n], writes=[bGW[slot]])
                if with_g:
                    S.dma("pool", "wg%d" % slot, WG[slot], d_win[NCH + n], writes=[bWG[slot]])

            def run_fronts(Hsrc, bHsrc, T, halo, j, with_g, tail, pre=None, extra=None):
                nch = cur["nch"]
                load_wu(0)
                load_wu(1)
                load_gw(0, with_g)
                front1(0, Hsrc, bHsrc, T, halo, j)
                front1b(0, T)
                for n in range(nch):
                    if n + 2 < nch:
                        load_wu(n + 2)
                    if n + 1 < nch:
                        load_gw(n + 1, with_g)
                        front1(n + 1, Hsrc, bHsrc, T, halo, j)
                    if pre is not None:
                        pre(n)
                    if extra is not None:
                        extra(n)
                    front2(n, T)
                    if n + 1 < nch:
                        front1b(n + 1, T)
                    tail(n)

            def front1(n, Hsrc, bHsrc, T, halo, j):
                nchunks = (T + 511) // 512
                slot = n % 3
                UC, bUC, UCB, bUCB = UCs[n % 2], bUCs[n % 2], UCBs[n % 2], bUCBs[n % 2]

                def mm_u(e):
                    for c in range(nchunks):
                        w = min(512, T - c * 512)
                        for dc in range(NDC):
                            r = e.matmul(P01[:, c * 512:c * 512 + w], lhsT=WU[slot][:, dc, :], rhs=Hsrc[:, dc, c * 512:c * 512 + w],
                                         start=(dc == 0), stop=(dc == NDC - 1))
                    return r
                S.op("pe", mm_u, [bWU[slot], bHsrc], [bP01])
                if halo:
                    def mm_s(e):
                        for dc in range(NDC):
                            r = e.matmul(P6[:, 0:3], lhsT=WU[slot][:, dc, :], rhs=Hsrc[:, dc, T:T + 3],
                                         start=(dc == 0), stop=(dc == NDC - 1))
                        return r
                    S.op("pe", mm_s, [bWU[slot], bHsrc], [bP6])
                    S.op("dve", lambda e: e.tensor_tensor(out=US[:, 0:3], in0=P6[:, 0:3], in1=HMASK[:, j, :], op=ALU.mult),
                         [bP6, bHM], [bUS])
                S.op("act", lambda e: e.activation(out=U[:, 0:T], in_=P01[:, 0:T], func=AF.Identity), [bP01], [bU])
                w0, w1, w2, w3 = (cur["CW"][:, n, k:k + 1] for k in range(4))
                cbn = cur["CB"][:, n:n + 1]
                S.op("dve", lambda e: e.tensor_scalar(out=UC[:, 0:T], in0=U[:, 0:T], scalar1=w2, scalar2=cbn,
                                                      op0=ALU.mult, op1=ALU.add), [bU, cur["bCW"], cur["bCB"]], [bUC])
                S.op("dve", lambda e: e.scalar_tensor_tensor(out=UC[:, 2:T], in0=U[:, 0:T - 2], scalar=w0, in1=UC[:, 2:T],
                                                             op0=ALU.mult, op1=ALU.add), [bU, bUC], [bUC])
                S.op("dve", lambda e: e.scalar_tensor_tensor(out=UC[:, 1:T], in0=U[:, 0:T - 1], scalar=w1, in1=UC[:, 1:T],
                                                             op0=ALU.mult, op1=ALU.add), [bU, bUC], [bUC])
                S.op("dve", lambda e: e.scalar_tensor_tensor(out=UC[:, 0:T - 1], in0=U[:, 1:T], scalar=w3, in1=UC[:, 0:T - 1],
                                                             op0=ALU.mult, op1=ALU.add), [bU, bUC], [bUC])
                if halo:
                    S.op("dve", lambda e: e.scalar_tensor_tensor(out=UC[:, 0:2], in0=US[:, 0:2], scalar=w0, in1=UC[:, 0:2],
                                                                 op0=ALU.mult, op1=ALU.add), [bUS, bUC], [bUC])
                    S.op("dve", lambda e: e.scalar_tensor_tensor(out=UC[:, 0:1], in0=US[:, 1:2], scalar=w1, in1=UC[:, 0:1],
                                                                 op0=ALU.mult, op1=ALU.add), [bUS, bUC], [bUC])
                    S.op("dve", lambda e: e.scalar_tensor_tensor(out=UC[:, T - 1:T], in0=US[:, 2:3], scalar=w3, in1=UC[:, T - 1:T],
                                                                 op0=ALU.mult, op1=ALU.add), [bUS, bUC], [bUC])

            def front1b(n, T):
                UC, bUC, UCB, bUCB = UCs[n % 2], bUCs[n % 2], UCBs[n % 2], bUCBs[n % 2]
                S.op("act", lambda e: e.activation(out=UCB[:, 0:T], in_=UC[:, 0:T], func=AF.Identity), [bUC], [bUCB])

            def front2(n, T):
                nchunks = (T + 511) // 512
                slot = n % 2
                UC, bUC, UCB, bUCB = UCs[n % 2], bUCs[n % 2], UCBs[n % 2], bUCBs[n % 2]
                THI, bTHI, AA, bAA = THIs[n % 2], bTHIs[n % 2], AAs[n % 2], bAAs[n % 2]
                for dr in range(2):
                    gbr, gbi = cur["GB"][:, dr * 2, n:n + 1], cur["GB"][:, dr * 2 + 1, n:n + 1]
                    hsn, ssn = cur["HSs"][:, dr, n:n + 1], cur["SS"][:, dr, n:n + 1]
                    bGBx, bHSx = cur["bGB"], cur["bHS"]

                    def mm_g(e, dr=dr):
                        for gate, PP in ((0, P23), (1, P45)):
                            for c in range(nchunks):
                                w = min(512, T - c * 512)
                                r = e.matmul(PP[:, c * 512:c * 512 + w], lhsT=GW[slot][:, dr * 2 + gate, :],
                                             rhs=UCB[:, c * 512:c * 512 + w], start=True, stop=True)
                        return r
                    S.op("pe", mm_g, [bGW[slot], bUCB], [bP23, bP45])
                    if kind == "A":
                        S.op("act", lambda e, dr=dr, gbr=gbr: e.activation(out=THR[dr][:, 0:T], in_=P23[:, 0:T], func=AF.Tanh,
                                                                           bias=gbr, scale=0.5, accum_out=RSUM[:, dr:dr + 1]),
                             [bP23, bGBx], [bTHR[dr], bRSUM[dr]])
                    else:
                        S.op("act", lambda e, dr=dr, gbr=gbr: e.activation(out=THR[dr][:, 0:T], in_=P23[:, 0:T], func=AF.Tanh,
                                                                           bias=gbr, scale=0.5), [bP23, bGBx], [bTHR[dr]])
                    S.op("act", lambda e, dr=dr, gbi=gbi: e.activation(out=THI[dr][:, 0:T], in_=P45[:, 0:T], func=AF.Tanh,
                                                                       bias=gbi, scale=0.5), [bP45, bGBx], [bTHI[dr]])
                    S.op("act", lambda e, dr=dr, hsn=hsn: e.activation(out=AA[dr][:, 0:T], in_=THR[dr][:, 0:T], func=AF.Exp,
                                                                       bias=hsn, scale=hsn), [bTHR[dr], bHSx], [bAA[dr]])
                    S.op("act", lambda e, dr=dr, ssn=ssn: e.activation(out=VV[dr][:, 0:T], in_=THR[dr][:, 0:T], func=AF.Exp,
                                                                       bias=ssn, scale=ssn), [bTHR[dr], bHSx], [bVV[dr]])
                    S.op("dve", lambda e, dr=dr: e.scalar_tensor_tensor(out=THI[dr][:, 0:T], in0=THI[dr][:, 0:T], scalar=1.0, in1=UC[:, 0:T],
                                                                        op0=ALU.add, op1=ALU.mult), [bTHI[dr], bUC], [bTHI[dr]])
                S.op("dve", lambda e: e.tensor_scalar(out=VVall[:, :, 0:T], in0=VVall[:, :, 0:T], scalar1=1.0, scalar2=None, op0=ALU.min),
                     [bVV[0], bVV[1]], [bVV[0], bVV[1]])
                for dr in range(2):
                    S.op("act", lambda e, dr=dr: e.activation(out=VV[dr][:, 0:T], in_=VV[dr][:, 0:T], func=AF.Sqrt, bias=1.0, scale=-1.0),
                         [bVV[dr]], [bVV[dr]])
                for dr in range(2):
                    S.op("dve", lambda e, dr=dr: e.scalar_tensor_tensor(out=THI[dr][:, 0:T], in0=THI[dr][:, 0:T], scalar=0.5, in1=VV[dr][:, 0:T],
                                                                        op0=ALU.mult, op1=ALU.mult), [bTHI[dr], bVV[dr]], [bTHI[dr]])

            def scan(eng, out, a, d, init, T, rev, reads, writes):
                if rev:
                    o_, a_, d_ = out[:, 0:T][:, ::-1], a[:, 0:T][:, ::-1], d[:, 0:T][:, ::-1]
                else:
                    o_, a_, d_ = out[:, 0:T], a[:, 0:T], d[:, 0:T]
                return S.op(eng, lambda e: e.tensor_tensor_scan(out=o_, data0=a_, data1=d_, initial=init, op0=ALU.mult, op1=ALU.add),
                            reads, writes)

            def make_h(j, Hb=None, bHb=None):
                Hb = H if Hb is None else Hb
                bHb = bH if bHb is None else bHb
                js = 0 if kind == "A" else j
                for dc in range(NDC):
                    S.op("act", lambda e, dc=dc: e.activation(out=Hb[:, dc, 0:TT], in_=XACC[:, dc, js * TT:(js + 1) * TT], func=AF.Identity,
                                                              bias=SH[:, dc, 0:1], scale=SC1[:, dc, 0:1]), bXs([dc], [2 * js, 2 * js + 1]) + [bMOD, bSC1], [bHb])
                    S.op("act", lambda e, dc=dc: e.activation(out=Hb[:, dc, TT:TT + 3], in_=XH[:, dc, j, :], func=AF.Identity,
                                                              bias=SH[:, dc, 0:1], scale=SC1[:, dc, 0:1]), [bXH, bMOD, bSC1], [bHb])

            if kind == "A":
                HC = AR.bf16("HC", NDC, CTX); bHC = Buf("HC")
                for dc in range(NDC):
                    S.op("act", lambda e, dc=dc: e.activation(out=HC[:, dc, :], in_=CTXT[:, dc, :], func=AF.Identity,
                                                              bias=SH[:, dc, 1:2], scale=SC1[:, dc, 1:2]), [bCTXT, bMOD, bSC1], [bHC])
                def tail_ctx(n):
                    THI, bTHI, AA, bAA = THIs[n % 2], bTHIs[n % 2], AAs[n % 2], bAAs[n % 2]
                    scan("dve", HSC, AA[0], THI[0], 0.0, CTX, False, [bAA[0], bTHI[0]], [bHSC])
                    S.op("dve", lambda e, n=n: e.tensor_copy(out=HCO[:, n, 0:1], in_=HSC[:, CTX - 1:CTX]), [bHSC], [bHCO])
                    scan("dve", HSC, AA[1], THI[1], 0.0, CTX, True, [bAA[1], bTHI[1]], [bHSC])
                    S.op("dve", lambda e, n=n: e.tensor_copy(out=HCO[:, n, 1:2], in_=HSC[:, 0:1]), [bHSC], [bHCO])
                cur.update(ctx_cs)
                def extra_ctx(n):
                    adaln_step(0, 16 + 2 * n)
                    adaln_step(0, 16 + 2 * n + 1)
                run_fronts(HC, bHC, CTX, False, 0, False, tail_ctx, extra=extra_ctx)
                adaln_finish(0, MOD, bMOD, 16, 24, load_bm=False)
                cur.update(main_cs)
                H2 = AR.bf16("H2", NDC, TT + 3); bH2 = Buf("H2")
                Hj = [(H, bH), (H2, bH2)]
                make_h(0, H, bH)
                load_x(1)
                for j in range(2):

                    def tail_a(n, j=j):
                        THI, bTHI, AA, bAA = THIs[n % 2], bTHIs[n % 2], AAs[n % 2], bAAs[n % 2]
                        for dr in range(2):
                            rev = dr == 1
                            col = 0 if rev else TT - 1
                            scan("dve", HSC, AA[dr], THI[dr], 0.0, TT, rev, [bAA[dr], bTHI[dr]], [bHSC])
                            S.op("dve", lambda e, n=n, dr=dr, col=col, j=j: e.tensor_copy(out=EP[:, j, n, 2 * dr:2 * dr + 1], in_=HSC[:, col:col + 1]),
                                 [bHSC], [bEP])
                            S.op("act", lambda e, n=n, dr=dr, j=j: e.activation(out=EPP[:, j, n, dr:dr + 1], in_=RSUM[:, dr:dr + 1], func=AF.Exp,
                                                                             bias=HST[:, dr, n:n + 1], scale=HSs[:, dr, n:n + 1]),
                                 [bRSUM[dr], bHS], [bEPP])
                    def extra_a(n, j=j):
                        if n < 12:
                            adaln_step(1, 12 * j + n)
                        if j == 0 and n == 10:
                            make_h(1, H2, bH2)
                    run_fronts(Hj[j][0], Hj[j][1], TT, True, j, False, tail_a, extra=extra_a)
                for dr in range(2):
                    S.op("dve", lambda e, dr=dr: e.tensor_copy(out=EP[:, :, :, 2 * dr + 1], in_=EPP[:, :, :, dr]), [bEPP], [bEP])
                adaln_finish(1, MOD1, bMOD1)
                out_toks.append(S.dma("sp", "omod0", o_mod[0], MOD, reads=[bMOD]))
                out_toks.append(S.dma("sp", "omod1", o_mod[1], MOD1, reads=[bMOD1]))
                out_toks.append(S.dma("sp", "oep", o_ep, EP, reads=[bEP]))
                out_toks.append(S.dma("sp", "ohc", o_hc, HCO, reads=[bHCO]))

            if kind == "B":
                def outproj_burst(G):
                    jp = G // 4
                    for dc in range(NDC):
                        for c in range(2):
                            PP, bPP = (P6, bP6) if c == 0 else (P7, bP7)

                            def mm_o(e, dc=dc, c=c, PP=PP):
                                for i in range(4):
                                    r = e.matmul(PP, lhsT=WO[:, i, dc * 128:(dc + 1) * 128],
                                                 rhs=Z[:, i, c * 512:(c + 1) * 512], start=(i == 0), stop=(i == 3))
                                return r
                            S.op("pe", mm_o, [bWO, bZ], [bPP])
                            cs = slice(jp * TT + c * 512, jp * TT + (c + 1) * 512)
                            S.op("dve", lambda e, dc=dc, PP=PP, cs=cs: e.scalar_tensor_tensor(
                                out=XACC[:, dc, cs], in0=PP, scalar=GT[:, dc, 0:1], in1=XACC[:, dc, cs],
                                op0=ALU.mult, op1=ALU.add), [bPP, bMOD] + bXs([dc], [2 * jp + c]), bXs([dc], [2 * jp + c]))

                for j in range(2):
                    make_h(j)
                    S.op("dve", lambda e, j=j: e.tensor_scalar(out=XACC[:, :, j * TT:(j + 1) * TT], in0=XACC[:, :, j * TT:(j + 1) * TT],
                                                               scalar1=ALPHA, scalar2=None, op0=ALU.mult), bXs(tcs=[2 * j, 2 * j + 1]), bXs(tcs=[2 * j, 2 * j + 1]))

                    def pre_b(n):
                        slot = n % 2

                        def mm_gg(e, slot=slot):
                            for c in range(2):
                                for dc in range(NDC):
                                    r = e.matmul(P01[:, c * 512:(c + 1) * 512], lhsT=WG[slot][:, dc, :], rhs=H[:, dc, c * 512:(c + 1) * 512],
                                                 start=(dc == 0), stop=(dc == NDC - 1))
                            return r
                        S.op("pe", mm_gg, [bWG[slot], bH], [bP01])
                        S.op("act", lambda e: e.activation(out=TG, in_=P01, func=AF.Tanh, scale=0.5), [bP01], [bTG])
                        S.op("dve", lambda e: e.scalar_tensor_tensor(out=T1, in0=TG, scalar=1.0, in1=P01, op0=ALU.add, op1=ALU.mult),
                             [bTG, bP01], [bT1])

                    def tail_b(n, j=j):
                        THI, bTHI, AA, bAA = THIs[n % 2], bTHIs[n % 2], AAs[n % 2], bAAs[n % 2]
                        slot = n % 2
                        gi = n % 4
                        if gi == 0:
                            if j * 4 + n // 4 >= 1:
                                outproj_burst(j * 4 + n // 4 - 1)
                            S.dma("pool", "wo", WO, d_wout[n:n + 4].rearrange("a p d -> p a d"), writes=[bWO])
                        scan("dve", HF, AA[0], THI[0], CAR[:, j, 0, n:n + 1], TT, False, [bAA[0], bTHI[0], bCAR], [bHF])
                        scan("dve", HB, AA[1], THI[1], CAR[:, j, 1, n:n + 1], TT, True, [bAA[1], bTHI[1], bCAR], [bHB])
                        S.op("pool", lambda e: e.tensor_tensor(out=HF, in0=HF, in1=HB, op=ALU.add), [bHF, bHB], [bHF])

                        S.op("dve", lambda e, gi=gi: e.scalar_tensor_tensor(out=Z[:, gi, :], in0=T1, scalar=0.5, in1=HF, op0=ALU.mult, op1=ALU.mult),
                             [bT1, bHF], [bZ])
                    run_fronts(H, bH, TT, True, j, True, tail_b, pre=pre_b)
                outproj_burst(7)


        ln_alias = []
        ln_mark = None

        def L1_body():
            nonlocal ln_mark
            PS_ = AR.f32("PS", NCH); bPS = Buf("PS")
            HM1 = AR.f32("HM1", 2); bHM1 = Buf("HM1")
            S.dma("sp", "c0", PS_, d_ps, writes=[bPS])
            S.dma("sp", "c1", HM1, d_hm1, writes=[bHM1])
            NR = 47
            H1 = AR.bf16("H1", NDC, NR * 64); bH1 = Buf("H1")
            XHS = AR.f32("XHS", 960); bXHS = Buf("XHS")
            UW = 31 * 80 + 16
            ln_mark = AR.off
            U32 = AR.f32("U32", UW); bU32 = Buf("U32")
            ln_alias.append(bU32)
            B1 = AR.f32("B1", UW); bB1 = Buf("B1")
            B2 = AR.f32("B2", UW); bB2 = Buf("B2")
            ICNT = AR.f32("ICNT", TT); bICNT = Buf("ICNT")
            TMP = WM[1].rearrange("p a b -> p (a b)"); bTMP = bWM[1]
            DD = AR.bf16("DD", 4, TT); bDD = Buf("DD")
            WU = [AR.bf16("WU%d" % i, NDC, 128) for i in range(2)]; bWU = [Buf("WU0"), Buf("WU1")]
            WG = [AR.bf16("WG%d" % i, NDC, 128) for i in range(2)]; bWG = [Buf("WG0"), Buf("WG1")]
            PWb = AR.bf16("PWb", 4, 512); bPW = Buf("PWb")
            TG = WM[0].rearrange("p a b -> p (a b)"); bTG = bWM[0]
            Z = AR.bf16("Z", 4, TT); bZ = Buf("Z")
            WO = AR.bf16("WO", 4, D); bWO = Buf("WO")
            S.op("pool", lambda e: e.memset(U32, 0.0), [], [bU32])
            S.op("pool", lambda e: e.memset(B1, 0.0), [], [bB1])
            S.op("pool", lambda e: e.memset(B2, 0.0), [], [bB2])
            for dc in range(NDC):
                S.op("act", lambda e, dc=dc: e.activation(out=H1[:, dc, 512:512 + TOK], in_=XACC[:, dc, :], func=AF.Identity,
                                                          bias=SH[:, dc, 0:1], scale=SC1[:, dc, 0:1]), bXs([dc]) + [bMOD, bSC1], [bH1])
                XHb, bXHb = (XHS, bXHS) if dc % 2 == 0 else (T1[:, 0:960], bT1)
                S.dma("sp", "xhs%d" % (dc % 2), XHb, d_xhalo[:, dc, :], writes=[bXHb])
                S.op("act", lambda e, dc=dc, XHb=XHb: e.activation(out=H1[:, dc, 0:512], in_=XHb[:, 0:512], func=AF.Identity,
                                                                   bias=SH[:, dc, 0:1], scale=SC1[:, dc, 0:1]), [bXHb, bMOD, bSC1], [bH1])
                S.op("act", lambda e, dc=dc, XHb=XHb: e.activation(out=H1[:, dc, 512 + TOK:512 + TOK + 448], in_=XHb[:, 512:960], func=AF.Identity,
                                                                   bias=SH[:, dc, 0:1], scale=SC1[:, dc, 0:1]), [bXHb, bMOD, bSC1], [bH1])
            S.op("dve", lambda e: e.tensor_scalar(out=H1[:, :, 0:512], in0=H1[:, :, 0:512], scalar1=HM1[:, 0:1], scalar2=None, op0=ALU.mult),
                 [bH1, bHM1], [bH1])
            S.op("dve", lambda e: e.tensor_scalar(out=H1[:, :, 512 + TOK:512 + TOK + 448], in0=H1[:, :, 512 + TOK:512 + TOK + 448],
                                                  scalar1=HM1[:, 1:2], scalar2=None, op0=ALU.mult), [bH1, bHM1], [bH1])
            S.op("dve", lambda e: e.tensor_scalar(out=XACC, in0=XACC, scalar1=ALPHA, scalar2=None, op0=ALU.mult), bXs(), bXs())
            P03 = psum[:, 0:4, :].rearrange("p a b -> p (a b)")
            SHIFTS = [(1, 0), (1, 1), (2, 2), (4, 4)]
            P67 = psum[:, 6:8, :].rearrange("p a b -> p (a b)")
            bP67 = [bP6, bP7]
            DDs = [DD, AR.bf16("DD1", 4, TT)]; bDDs = [bDD, Buf("DD1")]

            def stage_a_chunk(j, k, gi, par):
                hw = 1 << k
                R = 15 + 2 * hw
                L = R * 64
                r0 = 8 + 16 * j - hw
                n = 4 * k + gi
                slot = n % 2
                if gi == 0:
                    S.dma("sp", "icnt", ICNT, d_icnt[k][:, j * TT:(j + 1) * TT], writes=[bICNT])
                S.dma("pool", "wu%d" % slot, WU[slot], d_win[n], writes=[bWU[slot]])

                def mm_u(e):
                    for c in range((L + 511) // 512):
                        w = min(512, L - c * 512)
                        for dc in range(NDC):
                            r = e.matmul(P03[:, c * 512:c * 512 + w], lhsT=WU[slot][:, dc, :],
                                         rhs=H1[:, dc, r0 * 64 + c * 512:r0 * 64 + c * 512 + w],
                                         start=(dc == 0), stop=(dc == NDC - 1))
                    return r
                S.op("pe", mm_u, [bWU[slot], bH1], [bP01, bP23])
                Uv = U32[:, 8:8 + R * 80].rearrange("p (r c) -> p r c", c=80)[:, :, 8:72]
                S.op("act", lambda e: e.activation(out=Uv, in_=P03[:, 0:L].rearrange("p (r c) -> p r c", c=64), func=AF.Identity),
                     [bP01, bP23], [bU32])
                src, bsrc = U32, bU32
                dsts = [(B1, bB1), (B2, bB2)]
                rng = [(hw, hw + 16)]
                for l in range(k, 0, -1):
                    a, b_ = SHIFTS[l]
                    rng.insert(0, (rng[0][0] - a, rng[0][1] + b_))
                for l in range(k + 1):
                    a, b_ = SHIFTS[l]
                    rlo, rhi = rng[l]
                    dst, bdst = dsts[l % 2]
                    lo, hi = 8 + rlo * 80, 8 + rhi * 80
                    S.op("dve", lambda e, dst=dst, src=src, a=a, b_=b_, lo=lo, hi=hi: e.tensor_tensor(
                        out=dst[:, lo:hi], in0=src[:, lo - 80 * a:hi - 80 * a], in1=src[:, lo + 80 * b_:hi + 80 * b_], op=ALU.add),
                        [bsrc], [bdst])
                    src, bsrc = dst, bdst
                lo, hi = 8 + hw * 80, 8 + (hw + 16) * 80
                for l in range(k + 1):
                    a, b_ = SHIFTS[l]
                    dst, bdst = dsts[(k + 1 + l) % 2]
                    S.op("dve", lambda e, dst=dst, src=src, a=a, b_=b_, lo=lo, hi=hi: e.tensor_tensor(
                        out=dst[:, lo:hi], in0=src[:, lo - a:hi - a], in1=src[:, lo + b_:hi + b_], op=ALU.add), [bsrc], [bdst])
                    src, bsrc = dst, bdst
                own = slice(8 + hw * 80, 8 + (hw + 16) * 80)
                RSv = src[:, own].rearrange("p (r c) -> p r c", c=80)[:, :, 8:72]
                UOv = U32[:, own].rearrange("p (r c) -> p r c", c=80)[:, :, 8:72]
                S.op("dve", lambda e: e.tensor_tensor(out=TMP.rearrange("p (r c) -> p r c", c=64), in0=RSv,
                                                      in1=ICNT.rearrange("p (r c) -> p r c", c=64), op=ALU.mult),
                     [bsrc, bICNT], [bTMP])
                S.op("dve", lambda e: e.tensor_tensor(out=DDs[par][:, gi, :].rearrange("p (r c) -> p r c", c=64),
                                                      in0=TMP.rearrange("p (r c) -> p r c", c=64), in1=UOv, op=ALU.subtract),
                     [bTMP, bU32], [bDDs[par]])

            def stage_b_chunk(j, k, hc, par):
                n = 4 * k + hc
                slot = n % 2
                if hc == 0:
                    S.dma("pool", "pw", PWb, d_pw[k], writes=[bPW])
                    S.dma("pool", "wo", WO, d_wout[4 * k:4 * k + 4].rearrange("a p d -> p a d"), writes=[bWO])
                S.dma("pool", "wg%d" % slot, WG[slot], d_win[NCH + n], writes=[bWG[slot]])

                def mm_gg(e):
                    for c in range(2):
                        for dc in range(NDC):
                            c0 = 512 + j * TT + c * 512
                            r = e.matmul(P45[:, c * 512:(c + 1) * 512], lhsT=WG[slot][:, dc, :], rhs=H1[:, dc, c0:c0 + 512],
                                         start=(dc == 0), stop=(dc == NDC - 1))
                    return r
                S.op("pe", mm_gg, [bWG[slot], bH1], [bP45])
                S.op("act", lambda e: e.activation(out=T1, in_=P45, func=AF.Silu), [bP45], [bT1])

                def mm_p(e):
                    for c in range(2):
                        for gc in range(4):
                            r = e.matmul(P67[:, c * 512:(c + 1) * 512], lhsT=PWb[:, gc, hc * 128:(hc + 1) * 128],
                                         rhs=DDs[par][:, gc, c * 512:(c + 1) * 512], start=(gc == 0), stop=(gc == 3))
                    return r
                S.op("pe", mm_p, [bPW, bDDs[par]], bP67)
                S.op("dve", lambda e: e.scalar_tensor_tensor(out=Z[:, hc, :], in0=P67, scalar=PS_[:, n:n + 1], in1=T1,
                                                             op0=ALU.mult, op1=ALU.mult), bP67 + [bPS, bT1], [bZ])

            def stage_b_out(j):
                for dc in range(NDC):
                    PP, bPP = (P45, [bP45]) if dc % 2 == 0 else (P67, bP67)

                    def mm_o(e, dc=dc, PP=PP):
                        for c in range(2):
                            for i in range(4):
                                r = e.matmul(PP[:, c * 512:(c + 1) * 512], lhsT=WO[:, i, dc * 128:(dc + 1) * 128],
                                             rhs=Z[:, i, c * 512:(c + 1) * 512], start=(i == 0), stop=(i == 3))
                        return r
                    S.op("pe", mm_o, [bWO, bZ], bPP)
                    S.op("dve", lambda e, dc=dc, PP=PP: e.scalar_tensor_tensor(
                        out=XACC[:, dc, j * TT:(j + 1) * TT], in0=PP, scalar=GT[:, dc, 0:1], in1=XACC[:, dc, j * TT:(j + 1) * TT],
                        op0=ALU.mult, op1=ALU.add), bPP + [bMOD] + bXs([dc], [2 * j, 2 * j + 1]), bXs([dc], [2 * j, 2 * j + 1]))

            groups = [(j, k) for j in range(2) for k in range(4)]
            for gi in range(4):
                stage_a_chunk(groups[0][0], groups[0][1], gi, 0)
            for g, (j, k) in enumerate(groups):
                par = g % 2
                for i in range(4):
                    if g + 1 < len(groups):
                        stage_a_chunk(groups[g + 1][0], groups[g + 1][1], i, 1 - par)
                    stage_b_chunk(j, k, i, par)
                stage_b_out(j)

        if kind == "L1":
            T1 = AR.f32("T1", TT); bT1 = Buf("T1")
            L1_body()

        if kind in ("B", "L1"):
            S.barrier()
            AR.off = ln_mark if kind == "L1" else work_mark
            P67 = psum[:, 6:8, :].rearrange("p a b -> p (a b)")
            ONES = AR.f32("ONES", 128); bONES = Buf("ONES")
            LNG = AR.f32("LNG", NDC); LNB = AR.f32("LNB", NDC); bLN = Buf("LN")
            SQT = [AR.f32("SQT%d" % i, 512) for i in range(2)]; bSQT = [Buf("SQT0"), Buf("SQT1")]
            MEANs = [AR.f32("MEAN%d" % i, 512) for i in range(2)]; bMEANs = [Buf("MEAN0"), Buf("MEAN1")]
            RSTDs = [AR.f32("RSTD%d" % i, 512) for i in range(2)]; bRSTDs = [Buf("RSTD0"), Buf("RSTD1")]
            S.op("pool", lambda e: e.memset(ONES, 1.0), [], [bONES])
            S.dma("sp", "c9", LNG, d_lng, writes=[bLN])
            S.dma("sp", "c10", LNB, d_lnb, writes=[bLN])
            for tc_ in range(4):
                cs = slice(tc_ * 512, (tc_ + 1) * 512)
                par = tc_ % 2
                MEAN, bMEAN, RSTD, bRSTD = MEANs[par], bMEANs[par], RSTDs[par], bRSTDs[par]
                PS1, bPS1, PS2, bPS2 = (P01, [bP01], P23, [bP23]) if par == 0 else (P45, [bP45], P67, [bP6, bP7])

                def mm_sum(e, cs=cs, PS1=PS1):
                    for dc in range(NDC):
                        r = e.matmul(PS1[:, 0:512], lhsT=ONES, rhs=XACC[:, dc, cs], start=(dc == 0), stop=(dc == NDC - 1))
                    return r
                S.op("pe", mm_sum, [bONES] + bXs(tcs=[tc_]), bPS1)
                for dc in range(NDC):
                    S.op("act", lambda e, dc=dc, cs=cs: e.activation(out=SQT[dc % 2], in_=XACC[:, dc, cs], func=AF.Square),
                         bXs([dc], [tc_]), [bSQT[dc % 2]])
                    S.op("pe", lambda e, dc=dc, PS2=PS2: e.matmul(PS2[:, 0:512], lhsT=ONES, rhs=SQT[dc % 2],
                                                                  start=(dc == 0), stop=(dc == NDC - 1)), [bONES, bSQT[dc % 2]], bPS2)
                S.op("dve", lambda e, MEAN=MEAN, PS1=PS1: e.tensor_scalar(out=MEAN, in0=PS1[:, 0:512], scalar1=1.0 / D, scalar2=None, op0=ALU.mult),
                     bPS1, [bMEAN])
                S.op("dve", lambda e, MEAN=MEAN, RSTD=RSTD: e.tensor_tensor(out=RSTD, in0=MEAN, in1=MEAN, op=ALU.mult), [bMEAN], [bRSTD])
                S.op("dve", lambda e, RSTD=RSTD, PS2=PS2: e.scalar_tensor_tensor(out=RSTD, in0=PS2[:, 0:512], scalar=1.0 / D, in1=RSTD,
                                                                                 op0=ALU.mult, op1=ALU.subtract), bPS2 + [bRSTD], [bRSTD])
                S.op("dve", lambda e, RSTD=RSTD: e.tensor_scalar(out=RSTD, in0=RSTD, scalar1=LN_EPS, scalar2=None, op0=ALU.add), [bRSTD], [bRSTD])
                S.op("act", lambda e, RSTD=RSTD: e.activation(out=RSTD, in_=RSTD, func=AF.Sqrt), [bRSTD], [bRSTD])
                S.op("dve", lambda e, RSTD=RSTD: e.reciprocal(out=RSTD, in_=RSTD), [bRSTD], [bRSTD])
                for dc in range(NDC):
                    eng = "dve" if dc % 3 != 2 else "pool"
                    bx = bXs([dc], [tc_])
                    S.op(eng, lambda e, dc=dc, cs=cs, MEAN=MEAN: e.tensor_tensor(out=XACC[:, dc, cs], in0=XACC[:, dc, cs], in1=MEAN, op=ALU.subtract),
                         bx + [bMEAN], bx)
                    S.op(eng, lambda e, dc=dc, cs=cs, RSTD=RSTD: e.tensor_tensor(out=XACC[:, dc, cs], in0=XACC[:, dc, cs], in1=RSTD, op=ALU.mult),
                         bx + [bRSTD], bx)
                    S.op("act", lambda e, dc=dc, cs=cs: e.activation(out=XACC[:, dc, cs], in_=XACC[:, dc, cs], func=AF.Identity,
                                                                     bias=LNB[:, dc:dc + 1], scale=LNG[:, dc:dc + 1]), bx + [bLN], bx)
                out_toks.append(S.dma("sp", "ox%d" % tc_, o_x[:, :, cs], XACC[:, :, cs], reads=bXs(tcs=[tc_])))


        S.final_wait("sp", out_toks)
        with nc.Block() as block:
            S.emit(block)
    return nc


def _fm(v):
    v = np.asarray(v, np.float32)
    n = v.shape[-1] // 128
    lead = v.shape[:-1]
    v = v.reshape(*lead, n, 128)
    perm = (len(lead) + 1, len(lead)) + tuple(range(len(lead)))
    return np.ascontiguousarray(v.transpose(perm))


_PROGS = {}


def _prog(kind):
    if kind not in _PROGS:
        _PROGS[kind] = build(kind)
    return _PROGS[kind]


def _common_maps(layer, c, c_ctx, w_mod, b_mod, w_in, with_mod=False):
    win = np.ascontiguousarray(np.asarray(w_in[layer], np.float32).reshape(NDC, 128, 32, 128).transpose(2, 1, 0, 3))
    maps = []
    extra = {}
    if with_mod:
        for l, sfx in ((0, ""), (1, "1")):
            extra["wm" + sfx] = np.ascontiguousarray(np.asarray(w_mod[l], np.float32).reshape(NDC, 128, 24, 128).transpose(2, 1, 0, 3))
            extra["bm" + sfx] = _fm(b_mod[l])
    for core in range(NCORES):
        b = core // 4
        m = {"win": win}
        if with_mod:
            m["cv"] = np.ascontiguousarray(np.stack([_fm(c[b]), _fm(c_ctx)], axis=-1))
            m.update(extra)
        maps.append(m)
    return maps


def layer0_inputs(x, ctx, conv_w, conv_b, lru_wa, lru_ba, lru_wx, lru_bx, lru_lam):
    cw = np.ascontiguousarray(_fm(conv_w[0]))
    cb = _fm(conv_b[0])
    gw = np.stack([lru_wa[0, 0], lru_wx[0, 0], lru_wa[0, 1], lru_wx[0, 1]], axis=0)
    gw = np.ascontiguousarray(np.asarray(gw, np.float32).transpose(1, 2, 0, 3))
    gb = np.stack([lru_ba[0, 0], lru_bx[0, 0], lru_ba[0, 1], lru_bx[0, 1]], axis=0)
    gb = np.ascontiguousarray(_fm(gb).transpose(0, 2, 1))
    lam = np.ascontiguousarray(_fm(lru_lam[0]).transpose(0, 2, 1))
    xs = np.asarray(x, np.float32)
    S_ = xs.shape[1]
    maps = []
    for core in range(NCORES):
        b, q = core // 4, core % 4
        t0 = q * TOK
        xt = _fm(xs[b, t0:t0 + TOK])
        xh = np.zeros((128, NDC, 2, 3), np.float32)
        hm = np.zeros((128, 2, 3), np.float32)
        for j in range(2):
            s = t0 + j * TT
            for k, t in enumerate((s - 2, s - 1, s + TT)):
                if 0 <= t < S_:
                    xh[:, :, j, k] = _fm(xs[b, t])
                    hm[:, j, k] = 1.0
        maps.append({"x": xt, "xh": xh, "hmask": hm, "ctx": _fm(np.asarray(ctx, np.float32)[b]),
                     "cw": cw, "cb": cb, "gw": gw, "gb": gb, "lam": lam})
    return maps


def run_A(inputs):
    cm = _common_maps(0, inputs["c"], inputs["c_ctx"], inputs["w_mod"], inputs["b_mod"], inputs["w_in"], with_mod=True)
    lm = layer0_inputs(inputs["x"], inputs["ctx"], inputs["conv_w"], inputs["conv_b"], inputs["lru_wa"], inputs["lru_ba"],
                       inputs["lru_wx"], inputs["lru_bx"], inputs["lru_lam"])
    maps = [dict(a, **b) for a, b in zip(cm, lm)]
    for core, m in enumerate(maps):
        q = core % 4
        sl = slice(4 * q, 4 * q + 4)
        m["winc"] = np.ascontiguousarray(m["win"][sl])
        m["gwc"] = np.ascontiguousarray(m["gw"][sl])
        m["cwc"] = np.ascontiguousarray(m["cw"][:, sl, :])
        m["cbc"] = np.ascontiguousarray(m["cb"][:, sl])
        m["gbc"] = np.ascontiguousarray(m["gb"][:, :, sl])
        m["lamc"] = np.ascontiguousarray(m["lam"][:, :, sl])
    res = run_bass_kernel_spmd(_prog("A"), maps, core_ids=list(range(NCORES)))
    mods = [(r["mod0"], r["mod1"]) for r in res.results]
    return maps, [r["ep"] for r in res.results], [r["hc"] for r in res.results], mods


def carries_layout(eps, hcs):
    outs = []
    for core in range(NCORES):
        b, q = core // 4, core % 4
        epf = np.zeros((128, 2, 7, NCH, 2), np.float32)
        epb = np.zeros((128, 2, 7, NCH, 2), np.float32)
        epf[..., 0] = 1.0
        epb[..., 0] = 1.0
        for j in range(2):
            v = 2 * q + j
            for s_, r in enumerate(range(0, v)):
                src = eps[b * 4 + r // 2][:, r % 2]
                epf[:, j, s_, :, 0] = src[:, :, 1]
                epf[:, j, s_, :, 1] = src[:, :, 0]
            for s_, r in enumerate(range(7, v, -1)):
                src = eps[b * 4 + r // 2][:, r % 2]
                epb[:, j, s_, :, 0] = src[:, :, 3]
                epb[:, j, s_, :, 1] = src[:, :, 2]
        hcin = np.concatenate([hcs[b * 4 + qq] for qq in range(4)], axis=1)
        outs.append({"epf": epf, "epb": epb, "hcin": np.ascontiguousarray(hcin)})
    return outs


def run_B(inputs, maps, eps, hcs, mods):
    wout = np.ascontiguousarray(np.asarray(inputs["w_out"][0], np.float32).reshape(NCH, 128, D))
    lng, lnb = _fm(inputs["ln_g"][0]), _fm(inputs["ln_b"][0])
    cl = carries_layout(eps, hcs)
    m2 = []
    for core in range(NCORES):
        m = {k: v for k, v in maps[core].items() if k not in ("cv", "wm", "wm1", "bm", "bm1", "winc", "gwc", "cwc", "cbc", "gbc", "lamc")}
        m["modin"] = np.ascontiguousarray(mods[core][0])
        m.update(cl[core])
        m.update({"wout": wout, "lng": lng, "lnb": lnb})
        m2.append(m)
    res = run_bass_kernel_spmd(_prog("B"), m2, core_ids=list(range(NCORES)))
    return [r["xo"] for r in res.results]


def run_L1(inputs, xo, mods):
    cm = _common_maps(1, inputs["c"], inputs["c_ctx"], inputs["w_mod"], inputs["b_mod"], inputs["w_in"])
    wout = np.ascontiguousarray(np.asarray(inputs["w_out"][1], np.float32).reshape(NCH, 128, D))
    lng, lnb = _fm(inputs["ln_g"][1]), _fm(inputs["ln_b"][1])
    pw = np.ascontiguousarray(np.asarray(inputs["pool_w"][0], np.float32).reshape(4, 4, 128, 512).transpose(0, 2, 1, 3))
    ps = _fm(inputs["pool_scale"][0])
    cols = np.arange(64)
    maps = []
    for core in range(NCORES):
        b, q = core // 4, core % 4
        xhalo = np.zeros((128, NDC, 960), np.float32)
        hm1 = np.zeros((128, 2), np.float32)
        if q > 0:
            xhalo[:, :, 0:512] = xo[core - 1][:, :, TOK - 512:TOK]
            hm1[:, 0] = 1.0
        if q < 3:
            xhalo[:, :, 512:960] = xo[core + 1][:, :, 0:448]
            hm1[:, 1] = 1.0
        icnt = np.zeros((4, 128, TOK), np.float32)
        rows = q * 32 + np.arange(32)
        for k, w in enumerate((2, 4, 8, 16)):
            cc = (np.minimum(cols + w // 2, 64) - np.maximum(cols - w // 2, 0)).astype(np.float32)
            cr = (np.minimum(rows + w // 2, 128) - np.maximum(rows - w // 2, 0)).astype(np.float32)
            icnt[k] = (np.float32(1.0) / (cr[:, None] * cc[None, :])).reshape(-1)[None, :]
        m = dict(cm[core])
        m["modin"] = np.ascontiguousarray(mods[core][1])
        m.update({"x": np.ascontiguousarray(xo[core]), "xhalo": xhalo, "hm1": hm1, "icnt": icnt, "pw": pw, "ps": ps,
                  "wout": wout, "lng": lng, "lnb": lnb})
        maps.append(m)
    res = run_bass_kernel_spmd(_prog("L1"), maps, core_ids=list(range(NCORES)))
    return [r["xo"] for r in res.results]


def kernel(**inputs):
    maps, eps, hcs, mods = run_A(inputs)
    xo = run_B(inputs, maps, eps, hcs, mods)
    xo = run_L1(inputs, xo, mods)
    x = np.asarray(inputs["x"])
    out = np.zeros(x.shape, np.float32)
    for core in range(NCORES):
        b, q = core // 4, core % 4
        out[b, q * TOK:(q + 1) * TOK] = xo[core].transpose(2, 1, 0).reshape(TOK, D)
    return out
```

```python
import contextlib
import numpy as np
import concourse.bass as bass
import concourse.mybir as mybir
from concourse.bass_utils import run_bass_kernel_spmd

F32 = mybir.dt.float32
BF16 = mybir.dt.bfloat16
ALU = mybir.AluOpType
AF = mybir.ActivationFunctionType

D = 1024
DI = 2048
NDC = 8
NCH = 16
TOK = 2048
TT = 1024
CTX = 256
ALPHA = float(4 ** 0.25)
LN_EPS = 1e-5
NCORES = 8
SAME_ENG_SYNC = True


class Buf:
    def __init__(self, name):
        self.name = name
        self.w = None
        self.r = []


class Sched:
    ENGS = ("pe", "act", "dve", "pool", "sp")

    def __init__(self, nc, stack, n_dma=40):
        self.nc = nc
        self.q = {e: [] for e in self.ENGS}
        self.sems = {}
        for e in ("pe", "act", "dve", "pool"):
            self.sems[e] = stack.enter_context(nc.semaphore("sem_" + e))
        self.dma_free = [stack.enter_context(nc.semaphore("sem_dma%d" % i)) for i in range(n_dma)]
        self.cnt = {}
        self.waited = {e: {} for e in self.ENGS}
        self.dma_keys = {}

    def _wait(self, eng, tok):
        if tok is None:
            return
        key, val = tok
        if not SAME_ENG_SYNC and key == eng:
            return
        if self.waited[eng].get(key, 0) >= val:
            return
        self.waited[eng][key] = val
        sem = self.sems[key]
        self.q[eng].append(lambda e, sem=sem, val=val: e.wait_ge(sem, val))

    def _deps(self, eng, reads, writes):
        for b in reads:
            self._wait(eng, b.w)
        for b in writes:
            self._wait(eng, b.w)
            for t in b.r:
                self._wait(eng, t)

    def _mark(self, tok, reads, writes):
        for b in writes:
            b.w = tok
            b.r = []
        for b in reads:
            if b not in writes:
                b.r.append(tok)

    def op(self, eng, fn, reads=(), writes=()):
        self._deps(eng, reads, writes)
        self.cnt[eng] = self.cnt.get(eng, 0) + 1
        tok = (eng, self.cnt[eng])
        sem = self.sems[eng]
        self.q[eng].append(lambda e, fn=fn, sem=sem: fn(e).then_inc(sem, 1))
        self._mark(tok, reads, writes)
        return tok

    def dma(self, eng, key, out, in_, reads=(), writes=()):
        if key not in self.dma_keys:
            self.dma_keys[key] = "dma_" + key
            self.sems["dma_" + key] = self.dma_free.pop()
        k = self.dma_keys[key]
        self._deps(eng, reads, writes)
        self.cnt[k] = self.cnt.get(k, 0) + 16
        tok = (k, self.cnt[k])
        sem = self.sems[k]
        self.q[eng].append(lambda e, out=out, in_=in_, sem=sem: e.dma_start(out=out, in_=in_).then_inc(sem, 16))
        self._mark(tok, reads, writes)
        return tok

    def barrier(self):
        toks = [(k, v) for k, v in self.cnt.items() if v > 0]
        for eng in self.ENGS:
            for t in toks:
                self._wait(eng, t)

    def final_wait(self, eng, toks):
        for t in toks:
            self._wait(eng, t)

    def emit(self, block):
        q = self.q

        @block.tensor
        def _(e):
            for f in q["pe"]:
                f(e)

        @block.scalar
        def _(e):
            for f in q["act"]:
                f(e)

        @block.vector
        def _(e):
            for f in q["dve"]:
                f(e)

        @block.gpsimd
        def _(e):
            for f in q["pool"]:
                f(e)

        @block.sync
        def _(e):
            for f in q["sp"]:
                f(e)


class Arena:
    def __init__(self, nc, nbytes):
        self.t = nc.alloc_sbuf_tensor("arena", [128, nbytes // 4], F32)
        self.off = 0
        self.cap = nbytes

    def f32(self, name, *shape):
        n = int(np.prod(shape))
        ap = self.t[:, self.off // 4:self.off // 4 + n]
        self.off += n * 4
        assert self.off <= self.cap, (name, self.off, self.cap)
        return self._shape(ap, shape)

    def bf16(self, name, *shape):
        n = int(np.prod(shape))
        nb = (n * 2 + 3) // 4 * 4
        ap = self.t[:, self.off // 4:self.off // 4 + nb // 4].bitcast(BF16)[:, 0:n]
        self.off += nb
        assert self.off <= self.cap, (name, self.off, self.cap)
        return self._shape(ap, shape)

    @staticmethod
    def _shape(ap, shape):
        if len(shape) == 1:
            return ap
        if len(shape) == 2:
            return ap.rearrange("p (a b) -> p a b", a=shape[0])
        if len(shape) == 3:
            return ap.rearrange("p (a b c) -> p a b c", a=shape[0], b=shape[1])
        if len(shape) == 4:
            return ap.rearrange("p (a b c d) -> p a b c d", a=shape[0], b=shape[1], c=shape[2])
        raise ValueError


def build(kind):
    nc = bass.Bass("TRN2", target_bir_lowering=False)
    layer = 1 if kind == "L1" else 0

    def din(name, shape, dt=F32):
        return nc.dram_tensor(name, list(shape), dt, kind="ExternalInput").ap()

    def dout(name, shape, dt=F32):
        return nc.dram_tensor(name, list(shape), dt, kind="ExternalOutput").ap()

    if kind == "A":
        d_cv = din("cv", [128, NDC, 2])
        d_wm = [din("wm", [24, 128, NDC, 128]), din("wm1", [24, 128, NDC, 128])]
        d_bm = [din("bm", [128, 24]), din("bm1", [128, 24])]
        o_mod = [dout("mod0", [128, 24, 2]), dout("mod1", [128, 24, 2])]
    else:
        d_modin = din("modin", [128, 24, 2])
    d_win = din("win", [32, 128, NDC, 128])
    if kind in ("A", "B"):
        d_x = din("x", [128, NDC, TOK])
        d_xh = din("xh", [128, NDC, 2, 3])
        d_hmask = din("hmask", [128, 2, 3])
        d_ctx = din("ctx", [128, NDC, CTX])
        d_cw = din("cw", [128, NCH, 4])
        d_cb = din("cb", [128, NCH])
        d_gw = din("gw", [NCH, 128, 4, 128])
        d_gb = din("gb", [128, 4, NCH])
        d_lam = din("lam", [128, 2, NCH])
    NCC = 4
    if kind == "A":
        o_ep = dout("ep", [128, 2, NCH, 4])
        o_hc = dout("hc", [128, NCC, 2])
        d_winc = din("winc", [NCC, 128, NDC, 128])
        d_gwc = din("gwc", [NCC, 128, 4, 128])
        d_cwc = din("cwc", [128, NCC, 4])
        d_cbc = din("cbc", [128, NCC])
        d_gbc = din("gbc", [128, 4, NCC])
        d_lamc = din("lamc", [128, 2, NCC])
    if kind == "B":
        d_epf = din("epf", [128, 2, 7, NCH, 2])
        d_epb = din("epb", [128, 2, 7, NCH, 2])
        d_hc = din("hcin", [128, NCH, 2])
    if kind == "L1":
        d_x = din("x", [128, NDC, TOK])
        d_xhalo = din("xhalo", [128, NDC, 960])
        d_icnt = din("icnt", [4, 128, TOK])
        d_pw = din("pw", [4, 128, 4, 512])
        d_ps = din("ps", [128, NCH])
        d_hm1 = din("hm1", [128, 2])
    if kind in ("B", "L1"):
        d_wout = din("wout", [NCH, 128, D])
        d_lng = din("lng", [128, NDC])
        d_lnb = din("lnb", [128, NDC])
        o_x = dout("xo", [128, NDC, TOK])

    stack = contextlib.ExitStack()
    with stack:
        S = Sched(nc, stack)
        AR = Arena(nc, 206 * 1024)
        psum = nc.alloc_psum_tensor("psum", [128, 8, 512], F32)
        P01 = psum[:, 0:2, :].rearrange("p a b -> p (a b)")
        P23 = psum[:, 2:4, :].rearrange("p a b -> p (a b)")
        P45 = psum[:, 4:6, :].rearrange("p a b -> p (a b)")
        P6 = psum[:, 6, :]
        P7 = psum[:, 7, :]
        bP01, bP23, bP45, bP6, bP7 = (Buf(n) for n in ("P01", "P23", "P45", "P6", "P7"))

        CV = AR.f32("CV", NDC, 2)
        BM = AR.f32("BM", 24)
        MOD = AR.f32("MOD", 24, 2)
        SC1 = AR.f32("SC1", NDC, 2)
        SCV = AR.f32("SCV", NDC, 2)
        TMPC = AR.f32("TMPC", NDC, 2)
        NWM = 6 if kind == "A" else 2
        WM = [AR.f32("WM%d" % i, NDC, 128) for i in range(NWM)] if kind != "B" else []
        bWM = [Buf("WM%d" % i) for i in range(NWM)]
        bCV, bBM, bMOD, bSC1, bSCV, bTMPC = (Buf(n) for n in ("CV", "BM", "MOD", "SC1", "SCV", "TMPC"))
        MOD1 = AR.f32("MOD1", 24, 2); bMOD1 = Buf("MOD1")

        def adaln_step(l, jc):
            sl = jc % NWM
            S.dma("sp", "wm%d" % sl, WM[sl], d_wm[l][jc], writes=[bWM[sl]])

            def mm(e, jc=jc, sl=sl):
                for dc in range(NDC):
                    r = e.matmul(P7[:, 2 * jc:2 * jc + 2], lhsT=WM[sl][:, dc, :], rhs=SCV[:, dc, :],
                                 start=(dc == 0), stop=(dc == NDC - 1))
                return r
            S.op("pe", mm, [bWM[sl], bSCV], [bP7])

        def adaln_finish(l, MODd, bMODd, lo=0, hi=24, load_bm=True):
            if load_bm:
                S.dma("sp", "bm", BM, d_bm[l], writes=[bBM])
            P7m = P7[:, 0:48].rearrange("p (a b) -> p a b", b=2)
            for col in range(2):
                S.op("dve", lambda e, col=col: e.tensor_tensor(out=MODd[:, lo:hi, col], in0=P7m[:, lo:hi, col], in1=BM[:, lo:hi], op=ALU.add),
                     [bP7, bBM], [bMODd])

        def adaln(l, MODd, bMODd):
            for jc in range(24):
                adaln_step(l, jc)
            adaln_finish(l, MODd, bMODd)

        if kind == "A":
            S.dma("sp", "cv", CV, d_cv, writes=[bCV])
            S.op("act", lambda e: e.activation(out=TMPC, in_=CV, func=AF.Tanh, scale=0.5), [bCV], [bTMPC])
            S.op("dve", lambda e: e.scalar_tensor_tensor(out=TMPC, in0=TMPC, scalar=1.0, in1=CV, op0=ALU.add, op1=ALU.mult),
                 [bCV, bTMPC], [bTMPC])
            S.op("dve", lambda e: e.tensor_scalar(out=SCV, in0=TMPC, scalar1=0.5, scalar2=None, op0=ALU.mult), [bTMPC], [bSCV])
            for jc in range(16):
                adaln_step(0, jc)
            adaln_finish(0, MOD, bMOD, 0, 16)
        else:
            S.dma("sp", "modin", MOD, d_modin, writes=[bMOD])
        S.op("dve", lambda e: e.tensor_scalar(out=SC1, in0=MOD[:, 8:16, :], scalar1=1.0, scalar2=None, op0=ALU.add),
             [bMOD], [bSC1])
        SH = MOD[:, 0:8, :]
        GT = MOD[:, 16:24, :]

        XACC = AR.f32("XACC", NDC, TOK if kind != "A" else TT)
        bXc = [[Buf("X%d_%d" % (dc, t)) for t in range(4)] for dc in range(NDC)]

        def bXs(dcs=range(NDC), tcs=range(4)):
            return [bXc[dc][t] for dc in dcs for t in tcs]
        def load_x(j_):
            if kind == "A":
                S.dma("sp", "xload%d" % j_, XACC, d_x[:, :, j_ * TT:(j_ + 1) * TT], writes=bXs(tcs=[0, 1]))
            else:
                S.dma("sp", "xload%d" % j_, XACC[:, :, j_ * TT:(j_ + 1) * TT], d_x[:, :, j_ * TT:(j_ + 1) * TT], writes=bXs(tcs=[2 * j_, 2 * j_ + 1]))
        load_x(0)
        if kind != "A":
            load_x(1)

        out_toks = []

        if kind in ("A", "B"):
            CW = AR.f32("CW", NCH, 4); bCW = Buf("CW")
            CB = AR.f32("CB", NCH); bCB = Buf("CB")
            GB = AR.f32("GB", 4, NCH); bGB = Buf("GB")
            LAM = AR.f32("LAM", 2, NCH); bLAM = Buf("LAM")
            HSs = AR.f32("HS", 2, NCH); bHS = Buf("HSs")
            SS = AR.f32("SS", 2, NCH)
            LT = AR.f32("LT", 2, NCH); LT2 = AR.f32("LT2", 2, NCH)
            HMASK = AR.f32("HMASK", 2, 3); bHM = Buf("HMASK")
            if kind == "A":
                ZERO = AR.f32("ZERO", TT); bZERO = Buf("ZERO")
            S.dma("sp", "c0", CW, d_cw, writes=[bCW])
            S.dma("sp", "c1", CB, d_cb, writes=[bCB])
            S.dma("sp", "c2", GB, d_gb, writes=[bGB])
            S.dma("sp", "c3", LAM, d_lam, writes=[bLAM])
            S.dma("sp", "c4", HMASK, d_hmask, writes=[bHM])
            if kind == "A":
                S.op("pool", lambda e: e.memset(ZERO, 0.0), [], [bZERO])
            def prep_consts(GB_, bGB_, LAM_, bLAM_, LT_, LT2_, SS_, HS_, bHS_):
                S.op("dve", lambda e: e.tensor_scalar(out=GB_, in0=GB_, scalar1=0.5, scalar2=None, op0=ALU.mult), [bGB_], [bGB_])
                S.op("act", lambda e: e.activation(out=LT_, in_=LAM_, func=AF.Exp, scale=-1.0), [bLAM_], [bHS_])
                S.op("dve", lambda e: e.tensor_scalar(out=LT2_, in0=LT_, scalar1=-0.2, scalar2=0.25, op0=ALU.mult, op1=ALU.add), [bHS_], [bHS_])
                for cst in (1.0 / 3.0, 0.5, 1.0):
                    S.op("dve", lambda e: e.tensor_tensor(out=LT2_, in0=LT2_, in1=LT_, op=ALU.mult), [bHS_], [bHS_])
                    S.op("dve", lambda e, cst=cst: e.tensor_scalar(out=LT2_, in0=LT2_, scalar1=-1.0, scalar2=cst, op0=ALU.mult, op1=ALU.add), [bHS_], [bHS_])
                S.op("dve", lambda e: e.tensor_tensor(out=LT2_, in0=LT2_, in1=LT_, op=ALU.mult), [bHS_], [bHS_])
                S.op("dve", lambda e: e.tensor_scalar(out=SS_, in0=LT2_, scalar1=-8.0, scalar2=None, op0=ALU.mult), [bHS_], [bHS_])
                S.op("dve", lambda e: e.tensor_scalar(out=HS_, in0=LT2_, scalar1=-4.0, scalar2=None, op0=ALU.mult), [bHS_], [bHS_])

            prep_consts(GB, bGB, LAM, bLAM, LT, LT2, SS, HSs, bHS)
            main_cs = dict(CW=CW, CB=CB, GB=GB, HSs=HSs, SS=SS, bCW=bCW, bCB=bCB, bGB=bGB, bHS=bHS, win=d_win, gw=d_gw, nch=NCH)
            cur = dict(main_cs)
            if kind == "A":
                CWc = AR.f32("CWc", NCC, 4); CBc = AR.f32("CBc", NCC); GBc = AR.f32("GBc", 4, NCC); LAMc = AR.f32("LAMc", 2, NCC)
                HSc = AR.f32("HSc", 2, NCC); SSc = AR.f32("SSc", 2, NCC); LTc = AR.f32("LTc", 2, NCC); LT2c = AR.f32("LT2c", 2, NCC)
                bCWc, bCBc, bGBc, bLAMc, bHSc = (Buf(x) for x in ("CWc", "CBc", "GBc", "LAMc", "HSc"))
                S.dma("sp", "cc0", CWc, d_cwc, writes=[bCWc])
                S.dma("sp", "cc1", CBc, d_cbc, writes=[bCBc])
                S.dma("sp", "cc2", GBc, d_gbc, writes=[bGBc])
                S.dma("sp", "cc3", LAMc, d_lamc, writes=[bLAMc])
                prep_consts(GBc, bGBc, LAMc, bLAMc, LTc, LT2c, SSc, HSc, bHSc)
                ctx_cs = dict(CW=CWc, CB=CBc, GB=GBc, HSs=HSc, SS=SSc, bCW=bCWc, bCB=bCBc, bGB=bGBc, bHS=bHSc, win=d_winc, gw=d_gwc, nch=NCC)

            work_mark = AR.off
            H = AR.bf16("H", NDC, TT + 3); bH = Buf("H")
            XH = AR.f32("XH", NDC, 2, 3); bXH = Buf("XH")
            S.dma("sp", "c5", XH, d_xh, writes=[bXH])
            WU = [AR.bf16("WU%d" % i, NDC, 128) for i in range(3)]; bWU = [Buf("WU0"), Buf("WU1"), Buf("WU2")]
            GW = [AR.bf16("GW%d" % i, 4, 128) for i in range(2)]; bGW = [Buf("GW0"), Buf("GW1")]
            U = AR.f32("U", TT); bU = Buf("U")
            US = AR.f32("US", 4); bUS = Buf("US")
            UCs = [AR.f32("UC%d" % i, TT) for i in range(2)]; bUCs = [Buf("UC0"), Buf("UC1")]
            UCBs = [AR.bf16("UCB%d" % i, TT) for i in range(2)]; bUCBs = [Buf("UCB0"), Buf("UCB1")]
            THR = [AR.f32("THR%d" % i, TT) for i in range(2)]; bTHR = [Buf("THR0"), Buf("THR1")]
            THIs = [[AR.f32("THI%d%d" % (p, i), TT) for i in range(2)] for p in range(2)]
            bTHIs = [[Buf("THI%d%d" % (p, i)) for i in range(2)] for p in range(2)]
            AAs = [[AR.f32("AA%d%d" % (p, i), TT) for i in range(2)] for p in range(2)]
            bAAs = [[Buf("AA%d%d" % (p, i)) for i in range(2)] for p in range(2)]
            VV = [AR.f32("VV%d" % i, TT) for i in range(2)]; bVV = [Buf("VV0"), Buf("VV1")]
            HSC = AR.f32("HSC", TT); bHSC = Buf("HSC")
            if kind == "A":
                RSUM = AR.f32("RSUM", 2); bRSUM = [Buf("RSUM0"), Buf("RSUM1")]
                HST = AR.f32("HST", 2, NCH)
                S.op("dve", lambda e: e.tensor_scalar(out=HST, in0=HSs, scalar1=float(TT), scalar2=None, op0=ALU.mult), [bHS], [bHS])

            if kind == "A":
                CTXT = AR.f32("CTXT", NDC, CTX); bCTXT = Buf("CTXT")
                S.dma("sp", "ctxl", CTXT, d_ctx, writes=[bCTXT])
                EP = AR.f32("EP", 2, NCH, 4); bEP = Buf("EP")
                EPP = AR.f32("EPP", 2, NCH, 2); bEPP = Buf("EPP")
                HCO = AR.f32("HCO", NCC, 2); bHCO = Buf("HCO")
            if kind == "B":
                WG = [AR.bf16("WG%d" % i, NDC, 128) for i in range(2)]; bWG = [Buf("WG0"), Buf("WG1")]
                HF = AR.f32("HF", TT); bHF = Buf("HF")
                HB = AR.f32("HB", TT); bHB = Buf("HB")
                TG = AR.f32("TG", TT); bTG = Buf("TG")
                T1 = AR.f32("T1", TT); bT1 = Buf("T1")
                Z = AR.bf16("Z", 4, TT); bZ = Buf("Z")
                WO = AR.bf16("WO", 4, D); bWO = Buf("WO")
                EPF = AR.f32("EPF", 2, 7, NCH, 2); bEPF = Buf("EPF")
                EPB = AR.f32("EPB", 2, 7, NCH, 2); bEPB = Buf("EPB")
                HCI = AR.f32("HCI", NCH, 2); bHCI = Buf("HCI")
                CAR = AR.f32("CAR", 2, 2, NCH); bCAR = Buf("CAR")
                S.dma("sp", "c6", EPF, d_epf, writes=[bEPF])
                S.dma("sp", "c7", EPB, d_epb, writes=[bEPB])
                S.dma("sp", "c8", HCI, d_hc, writes=[bHCI])
                for j in range(2):
                    for dr, EPX, bEPX in ((0, EPF, bEPF), (1, EPB, bEPB)):
                        S.op("dve", lambda e, j=j, dr=dr: e.tensor_copy(out=CAR[:, j, dr, :], in_=HCI[:, :, dr]), [bHCI], [bCAR])
                        for s_ in range(7):
                            S.op("dve", lambda e, j=j, dr=dr, s_=s_, EPX=EPX: e.tensor_tensor(
                                out=CAR[:, j, dr, :], in0=CAR[:, j, dr, :], in1=EPX[:, j, s_, :, 0], op=ALU.mult), [bEPX, bCAR], [bCAR])
                            S.op("dve", lambda e, j=j, dr=dr, s_=s_, EPX=EPX: e.tensor_tensor(
                                out=CAR[:, j, dr, :], in0=CAR[:, j, dr, :], in1=EPX[:, j, s_, :, 1], op=ALU.add), [bEPX, bCAR], [bCAR])

            def load_wu(n):
                S.dma("pool", "wu%d" % (n % 3), WU[n % 3], cur["win"][n], writes=[bWU[n % 3]])

            def load_gw(n, with_g):
                slot = n % 2
                S.dma("pool", "gw%d" % slot, GW[slot], cur["gw"][n], writes=[bGW[slot]])
                if with_g:
                    S.dma("pool", "wg%d" % slot, WG[slot], d_win[NCH + n], writes=[bWG[slot]])

            def run_fronts(Hsrc, bHsrc, T, halo, j, with_g, tail, pre=None, extra=None):
                nch = cur["nch"]
                load_wu(0)
                load_wu(1)
                load_gw(0, with_g)
                front1(0, Hsrc, bHsrc, T, halo, j)
                front1b(0, T)
                for n in range(nch):
                    if n + 2 < nch:
                        load_wu(n + 2)
                    if n + 1 < nch:
                        load_gw(n + 1, with_g)
                        front1(n + 1, Hsrc, bHsrc, T, halo, j)
                    if pre is not None:
                        pre(n)
                    if extra is not None:
                        extra(n)
                    front2(n, T)
                    if n + 1 < nch:
                        front1b(n + 1, T)
                    tail(n)

            def front1(n, Hsrc, bHsrc, T, halo, j):
                nchunks = (T + 511) // 512
                slot = n % 3
                UC, bUC, UCB, bUCB = UCs[n % 2], bUCs[n % 2], UCBs[n % 2], bUCBs[n % 2]

                def mm_u(e):
                    for c in range(nchunks):
                        w = min(512, T - c * 512)
                        for dc in range(NDC):
                            r = e.matmul(P01[:, c * 512:c * 512 + w], lhsT=WU[slot][:, dc, :], rhs=Hsrc[:, dc, c * 512:c * 512 + w],
                                         start=(dc == 0), stop=(dc == NDC - 1))
                    return r
                S.op("pe", mm_u, [bWU[slot], bHsrc], [bP01])
                if halo:
                    def mm_s(e):
                        for dc in range(NDC):
                            r = e.matmul(P6[:, 0:3], lhsT=WU[slot][:, dc, :], rhs=Hsrc[:, dc, T:T + 3],
                                         start=(dc == 0), stop=(dc == NDC - 1))
                        return r
                    S.op("pe", mm_s, [bWU[slot], bHsrc], [bP6])
                    S.op("dve", lambda e: e.tensor_tensor(out=US[:, 0:3], in0=P6[:, 0:3], in1=HMASK[:, j, :], op=ALU.mult),
                         [bP6, bHM], [bUS])
                S.op("act", lambda e: e.activation(out=U[:, 0:T], in_=P01[:, 0:T], func=AF.Identity), [bP01], [bU])
                w0, w1, w2, w3 = (cur["CW"][:, n, k:k + 1] for k in range(4))
                cbn = cur["CB"][:, n:n + 1]
                S.op("dve", lambda e: e.tensor_scalar(out=UC[:, 0:T], in0=U[:, 0:T], scalar1=w2, scalar2=cbn,
                                                      op0=ALU.mult, op1=ALU.add), [bU, cur["bCW"], cur["bCB"]], [bUC])
                S.op("dve", lambda e: e.scalar_tensor_tensor(out=UC[:, 2:T], in0=U[:, 0:T - 2], scalar=w0, in1=UC[:, 2:T],
                                                             op0=ALU.mult, op1=ALU.add), [bU, bUC], [bUC])
                S.op("dve", lambda e: e.scalar_tensor_tensor(out=UC[:, 1:T], in0=U[:, 0:T - 1], scalar=w1, in1=UC[:, 1:T],
                                                             op0=ALU.mult, op1=ALU.add), [bU, bUC], [bUC])
                S.op("dve", lambda e: e.scalar_tensor_tensor(out=UC[:, 0:T - 1], in0=U[:, 1:T], scalar=w3, in1=UC[:, 0:T - 1],
                                                             op0=ALU.mult, op1=ALU.add), [bU, bUC], [bUC])
                if halo:
                    S.op("dve", lambda e: e.scalar_tensor_tensor(out=UC[:, 0:2], in0=US[:, 0:2], scalar=w0, in1=UC[:, 0:2],
                                                                 op0=ALU.mult, op1=ALU.add), [bUS, bUC], [bUC])
                    S.op("dve", lambda e: e.scalar_tensor_tensor(out=UC[:, 0:1], in0=US[:, 1:2], scalar=w1, in1=UC[:, 0:1],
                                                                 op0=ALU.mult, op1=ALU.add), [bUS, bUC], [bUC])
                    S.op("dve", lambda e: e.scalar_tensor_tensor(out=UC[:, T - 1:T], in0=US[:, 2:3], scalar=w3, in1=UC[:, T - 1:T],
                                                                 op0=ALU.mult, op1=ALU.add), [bUS, bUC], [bUC])

            def front1b(n, T):
                UC, bUC, UCB, bUCB = UCs[n % 2], bUCs[n % 2], UCBs[n % 2], bUCBs[n % 2]
                S.op("act", lambda e: e.activation(out=UCB[:, 0:T], in_=UC[:, 0:T], func=AF.Identity), [bUC], [bUCB])

            def front2(n, T):
                nchunks = (T + 511) // 512
                slot = n % 2
                UC, bUC, UCB, bUCB = UCs[n % 2], bUCs[n % 2], UCBs[n % 2], bUCBs[n % 2]
                THI, bTHI, AA, bAA = THIs[n % 2], bTHIs[n % 2], AAs[n % 2], bAAs[n % 2]
                for dr in range(2):
                    gbr, gbi = cur["GB"][:, dr * 2, n:n + 1], cur["GB"][:, dr * 2 + 1, n:n + 1]
                    hsn, ssn = cur["HSs"][:, dr, n:n + 1], cur["SS"][:, dr, n:n + 1]
                    bGBx, bHSx = cur["bGB"], cur["bHS"]

                    def mm_g(e, dr=dr):
                        for gate, PP in ((0, P23), (1, P45)):
                            for c in range(nchunks):
                                w = min(512, T - c * 512)
                                r = e.matmul(PP[:, c * 512:c * 512 + w], lhsT=GW[slot][:, dr * 2 + gate, :],
                                             rhs=UCB[:, c * 512:c * 512 + w], start=True, stop=True)
                        return r
                    S.op("pe", mm_g, [bGW[slot], bUCB], [bP23, bP45])
                    if kind == "A":
                        S.op("act", lambda e, dr=dr, gbr=gbr: e.activation(out=THR[dr][:, 0:T], in_=P23[:, 0:T], func=AF.Tanh,
                                                                           bias=gbr, scale=0.5, accum_out=RSUM[:, dr:dr + 1]),
                             [bP23, bGBx], [bTHR[dr], bRSUM[dr]])
                    else:
                        S.op("act", lambda e, dr=dr, gbr=gbr: e.activation(out=THR[dr][:, 0:T], in_=P23[:, 0:T], func=AF.Tanh,
                                                                           bias=gbr, scale=0.5), [bP23, bGBx], [bTHR[dr]])
                    S.op("act", lambda e, dr=dr, gbi=gbi: e.activation(out=THI[dr][:, 0:T], in_=P45[:, 0:T], func=AF.Tanh,
                                                                       bias=gbi, scale=0.5), [bP45, bGBx], [bTHI[dr]])
                    S.op("act", lambda e, dr=dr, hsn=hsn: e.activation(out=AA[dr][:, 0:T], in_=THR[dr][:, 0:T], func=AF.Exp,
                                                                       bias=hsn, scale=hsn), [bTHR[dr], bHSx], [bAA[dr]])
                    S.op("act", lambda e, dr=dr, ssn=ssn: e.activation(out=VV[dr][:, 0:T], in_=THR[dr][:, 0:T], func=AF.Exp,
                                                                       bias=ssn, scale=ssn), [bTHR[dr], bHSx], [bVV[dr]])
                    S.op("dve", lambda e, dr=dr: e.tensor_scalar(out=VV[dr][:, 0:T], in0=VV[dr][:, 0:T], scalar1=1.0, scalar2=None, op0=ALU.min),
                         [bVV[dr]], [bVV[dr]])
                    S.op("dve", lambda e, dr=dr: e.scalar_tensor_tensor(out=THI[dr][:, 0:T], in0=THI[dr][:, 0:T], scalar=1.0, in1=UC[:, 0:T],
                                                                        op0=ALU.add, op1=ALU.mult), [bTHI[dr], bUC], [bTHI[dr]])
                for dr in range(2):
                    S.op("act", lambda e, dr=dr: e.activation(out=VV[dr][:, 0:T], in_=VV[dr][:, 0:T], func=AF.Sqrt, bias=1.0, scale=-1.0),
                         [bVV[dr]], [bVV[dr]])
                for dr in range(2):
                    S.op("dve", lambda e, dr=dr: e.scalar_tensor_tensor(out=THI[dr][:, 0:T], in0=THI[dr][:, 0:T], scalar=0.5, in1=VV[dr][:, 0:T],
                                                                        op0=ALU.mult, op1=ALU.mult), [bTHI[dr], bVV[dr]], [bTHI[dr]])

            def scan(eng, out, a, d, init, T, rev, reads, writes):
                if rev:
                    o_, a_, d_ = out[:, 0:T][:, ::-1], a[:, 0:T][:, ::-1], d[:, 0:T][:, ::-1]
                else:
                    o_, a_, d_ = out[:, 0:T], a[:, 0:T], d[:, 0:T]
                return S.op(eng, lambda e: e.tensor_tensor_scan(out=o_, data0=a_, data1=d_, initial=init, op0=ALU.mult, op1=ALU.add),
                            reads, writes)

            def make_h(j, Hb=None, bHb=None):
                Hb = H if Hb is None else Hb
                bHb = bH if bHb is None else bHb
                js = 0 if kind == "A" else j
                for dc in range(NDC):
                    S.op("act", lambda e, dc=dc: e.activation(out=Hb[:, dc, 0:TT], in_=XACC[:, dc, js * TT:(js + 1) * TT], func=AF.Identity,
                                                              bias=SH[:, dc, 0:1], scale=SC1[:, dc, 0:1]), bXs([dc], [2 * js, 2 * js + 1]) + [bMOD, bSC1], [bHb])
                    S.op("act", lambda e, dc=dc: e.activation(out=Hb[:, dc, TT:TT + 3], in_=XH[:, dc, j, :], func=AF.Identity,
                                                              bias=SH[:, dc, 0:1], scale=SC1[:, dc, 0:1]), [bXH, bMOD, bSC1], [bHb])

            if kind == "A":
                HC = AR.bf16("HC", NDC, CTX); bHC = Buf("HC")
                for dc in range(NDC):
                    S.op("act", lambda e, dc=dc: e.activation(out=HC[:, dc, :], in_=CTXT[:, dc, :], func=AF.Identity,
                                                              bias=SH[:, dc, 1:2], scale=SC1[:, dc, 1:2]), [bCTXT, bMOD, bSC1], [bHC])
                def tail_ctx(n):
                    THI, bTHI, AA, bAA = THIs[n % 2], bTHIs[n % 2], AAs[n % 2], bAAs[n % 2]
                    scan("dve", HSC, AA[0], THI[0], 0.0, CTX, False, [bAA[0], bTHI[0]], [bHSC])
                    S.op("dve", lambda e, n=n: e.tensor_copy(out=HCO[:, n, 0:1], in_=HSC[:, CTX - 1:CTX]), [bHSC], [bHCO])
                    scan("dve", HSC, AA[1], THI[1], 0.0, CTX, True, [bAA[1], bTHI[1]], [bHSC])
                    S.op("dve", lambda e, n=n: e.tensor_copy(out=HCO[:, n, 1:2], in_=HSC[:, 0:1]), [bHSC], [bHCO])
                cur.update(ctx_cs)
                def extra_ctx(n):
                    adaln_step(0, 16 + 2 * n)
                    adaln_step(0, 16 + 2 * n + 1)
                run_fronts(HC, bHC, CTX, False, 0, False, tail_ctx, extra=extra_ctx)
                adaln_finish(0, MOD, bMOD, 16, 24, load_bm=False)
                cur.update(main_cs)
                H2 = AR.bf16("H2", NDC, TT + 3); bH2 = Buf("H2")
                Hj = [(H, bH), (H2, bH2)]
                make_h(0, H, bH)
                load_x(1)
                for j in range(2):

                    def tail_a(n, j=j):
                        THI, bTHI, AA, bAA = THIs[n % 2], bTHIs[n % 2], AAs[n % 2], bAAs[n % 2]
                        for dr in range(2):
                            rev = dr == 1
                            col = 0 if rev else TT - 1
                            scan("dve", HSC, AA[dr], THI[dr], 0.0, TT, rev, [bAA[dr], bTHI[dr]], [bHSC])
                            S.op("dve", lambda e, n=n, dr=dr, col=col, j=j: e.tensor_copy(out=EP[:, j, n, 2 * dr:2 * dr + 1], in_=HSC[:, col:col + 1]),
                                 [bHSC], [bEP])
                            S.op("act", lambda e, n=n, dr=dr, j=j: e.activation(out=EPP[:, j, n, dr:dr + 1], in_=RSUM[:, dr:dr + 1], func=AF.Exp,
                                                                             bias=HST[:, dr, n:n + 1], scale=HSs[:, dr, n:n + 1]),
                                 [bRSUM[dr], bHS], [bEPP])
                    def extra_a(n, j=j):
                        if n < 12:
                            adaln_step(1, 12 * j + n)
                        if j == 0 and n == 10:
                            make_h(1, H2, bH2)
                    run_fronts(Hj[j][0], Hj[j][1], TT, True, j, False, tail_a, extra=extra_a)
                for dr in range(2):
                    S.op("dve", lambda e, dr=dr: e.tensor_copy(out=EP[:, :, :, 2 * dr + 1], in_=EPP[:, :, :, dr]), [bEPP], [bEP])
                adaln_finish(1, MOD1, bMOD1)
                out_toks.append(S.dma("sp", "omod0", o_mod[0], MOD, reads=[bMOD]))
                out_toks.append(S.dma("sp", "omod1", o_mod[1], MOD1, reads=[bMOD1]))
                out_toks.append(S.dma("sp", "oep", o_ep, EP, reads=[bEP]))
                out_toks.append(S.dma("sp", "ohc", o_hc, HCO, reads=[bHCO]))

            if kind == "B":
                def outproj_burst(G):
                    jp = G // 4
                    for dc in range(NDC):
                        for c in range(2):
                            PP, bPP = (P6, bP6) if c == 0 else (P7, bP7)

                            def mm_o(e, dc=dc, c=c, PP=PP):
                                for i in range(4):
                                    r = e.matmul(PP, lhsT=WO[:, i, dc * 128:(dc + 1) * 128],
                                                 rhs=Z[:, i, c * 512:(c + 1) * 512], start=(i == 0), stop=(i == 3))
                                return r
                            S.op("pe", mm_o, [bWO, bZ], [bPP])
                            cs = slice(jp * TT + c * 512, jp * TT + (c + 1) * 512)
                            S.op("dve", lambda e, dc=dc, PP=PP, cs=cs: e.scalar_tensor_tensor(
                                out=XACC[:, dc, cs], in0=PP, scalar=GT[:, dc, 0:1], in1=XACC[:, dc, cs],
                                op0=ALU.mult, op1=ALU.add), [bPP, bMOD] + bXs([dc], [2 * jp + c]), bXs([dc], [2 * jp + c]))

                for j in range(2):
                    make_h(j)
                    S.op("dve", lambda e, j=j: e.tensor_scalar(out=XACC[:, :, j * TT:(j + 1) * TT], in0=XACC[:, :, j * TT:(j + 1) * TT],
                                                               scalar1=ALPHA, scalar2=None, op0=ALU.mult), bXs(tcs=[2 * j, 2 * j + 1]), bXs(tcs=[2 * j, 2 * j + 1]))

                    def pre_b(n):
                        slot = n % 2

                        def mm_gg(e, slot=slot):
                            for c in range(2):
                                for dc in range(NDC):
                                    r = e.matmul(P01[:, c * 512:(c + 1) * 512], lhsT=WG[slot][:, dc, :], rhs=H[:, dc, c * 512:(c + 1) * 512],
                                                 start=(dc == 0), stop=(dc == NDC - 1))
                            return r
                        S.op("pe", mm_gg, [bWG[slot], bH], [bP01])
                        S.op("act", lambda e: e.activation(out=TG, in_=P01, func=AF.Tanh, scale=0.5), [bP01], [bTG])
                        S.op("dve", lambda e: e.scalar_tensor_tensor(out=T1, in0=TG, scalar=1.0, in1=P01, op0=ALU.add, op1=ALU.mult),
                             [bTG, bP01], [bT1])

                    def tail_b(n, j=j):
                        THI, bTHI, AA, bAA = THIs[n % 2], bTHIs[n % 2], AAs[n % 2], bAAs[n % 2]
                        slot = n % 2
                        gi = n % 4
                        if gi == 0:
                            if j * 4 + n // 4 >= 1:
                                outproj_burst(j * 4 + n // 4 - 1)
                            S.dma("pool", "wo", WO, d_wout[n:n + 4].rearrange("a p d -> p a d"), writes=[bWO])
                        scan("dve", HF, AA[0], THI[0], CAR[:, j, 0, n:n + 1], TT, False, [bAA[0], bTHI[0], bCAR], [bHF])
                        scan("dve", HB, AA[1], THI[1], CAR[:, j, 1, n:n + 1], TT, True, [bAA[1], bTHI[1], bCAR], [bHB])
                        S.op("pool", lambda e: e.tensor_tensor(out=HF, in0=HF, in1=HB, op=ALU.add), [bHF, bHB], [bHF])

                        S.op("dve", lambda e, gi=gi: e.scalar_tensor_tensor(out=Z[:, gi, :], in0=T1, scalar=0.5, in1=HF, op0=ALU.mult, op1=ALU.mult),
                             [bT1, bHF], [bZ])
                    run_fronts(H, bH, TT, True, j, True, tail_b, pre=pre_b)
                outproj_burst(7)


        ln_alias = []
        ln_mark = None

        def L1_body():
            nonlocal ln_mark
            PS_ = AR.f32("PS", NCH); bPS = Buf("PS")
            HM1 = AR.f32("HM1", 2); bHM1 = Buf("HM1")
            S.dma("sp", "c0", PS_, d_ps, writes=[bPS])
            S.dma("sp", "c1", HM1, d_hm1, writes=[bHM1])
            NR = 47
            H1 = AR.bf16("H1", NDC, NR * 64); bH1 = Buf("H1")
            XHS = AR.f32("XHS", 960); bXHS = Buf("XHS")
            UW = 31 * 80 + 16
            ln_mark = AR.off
            U32 = AR.f32("U32", UW); bU32 = Buf("U32")
            ln_alias.append(bU32)
            B1 = AR.f32("B1", UW); bB1 = Buf("B1")
            B2 = AR.f32("B2", UW); bB2 = Buf("B2")
            ICNT = AR.f32("ICNT", TT); bICNT = Buf("ICNT")
            TMP = WM[1].rearrange("p a b -> p (a b)"); bTMP = bWM[1]
            DD = AR.bf16("DD", 4, TT); bDD = Buf("DD")
            WU = [AR.bf16("WU%d" % i, NDC, 128) for i in range(2)]; bWU = [Buf("WU0"), Buf("WU1")]
            WG = [AR.bf16("WG%d" % i, NDC, 128) for i in range(2)]; bWG = [Buf("WG0"), Buf("WG1")]
            PWb = AR.bf16("PWb", 4, 512); bPW = Buf("PWb")
            TG = WM[0].rearrange("p a b -> p (a b)"); bTG = bWM[0]
            Z = AR.bf16("Z", 4, TT); bZ = Buf("Z")
            WO = AR.bf16("WO", 4, D); bWO = Buf("WO")
            S.op("pool", lambda e: e.memset(U32, 0.0), [], [bU32])
            S.op("pool", lambda e: e.memset(B1, 0.0), [], [bB1])
            S.op("pool", lambda e: e.memset(B2, 0.0), [], [bB2])
            for dc in range(NDC):
                S.op("act", lambda e, dc=dc: e.activation(out=H1[:, dc, 512:512 + TOK], in_=XACC[:, dc, :], func=AF.Identity,
                                                          bias=SH[:, dc, 0:1], scale=SC1[:, dc, 0:1]), bXs([dc]) + [bMOD, bSC1], [bH1])
                XHb, bXHb = (XHS, bXHS) if dc % 2 == 0 else (T1[:, 0:960], bT1)
                S.dma("sp", "xhs%d" % (dc % 2), XHb, d_xhalo[:, dc, :], writes=[bXHb])
                S.op("act", lambda e, dc=dc, XHb=XHb: e.activation(out=H1[:, dc, 0:512], in_=XHb[:, 0:512], func=AF.Identity,
                                                                   bias=SH[:, dc, 0:1], scale=SC1[:, dc, 0:1]), [bXHb, bMOD, bSC1], [bH1])
                S.op("act", lambda e, dc=dc, XHb=XHb: e.activation(out=H1[:, dc, 512 + TOK:512 + TOK + 448], in_=XHb[:, 512:960], func=AF.Identity,
                                                                   bias=SH[:, dc, 0:1], scale=SC1[:, dc, 0:1]), [bXHb, bMOD, bSC1], [bH1])
            S.op("dve", lambda e: e.tensor_scalar(out=H1[:, :, 0:512], in0=H1[:, :, 0:512], scalar1=HM1[:, 0:1], scalar2=None, op0=ALU.mult),
                 [bH1, bHM1], [bH1])
            S.op("dve", lambda e: e.tensor_scalar(out=H1[:, :, 512 + TOK:512 + TOK + 448], in0=H1[:, :, 512 + TOK:512 + TOK + 448],
                                                  scalar1=HM1[:, 1:2], scalar2=None, op0=ALU.mult), [bH1, bHM1], [bH1])
            S.op("dve", lambda e: e.tensor_scalar(out=XACC, in0=XACC, scalar1=ALPHA, scalar2=None, op0=ALU.mult), bXs(), bXs())
            P03 = psum[:, 0:4, :].rearrange("p a b -> p (a b)")
            SHIFTS = [(1, 0), (1, 1), (2, 2), (4, 4)]
            P67 = psum[:, 6:8, :].rearrange("p a b -> p (a b)")
            bP67 = [bP6, bP7]
            DDs = [DD, AR.bf16("DD1", 4, TT)]; bDDs = [bDD, Buf("DD1")]

            pf = {"a": 0, "b": 0}

            def stage_a_chunk(j, k, gi, par):
                hw = 1 << k
                R = 15 + 2 * hw
                L = R * 64
                r0 = 8 + 16 * j - hw
                n = 4 * k + gi
                slot = n % 2
                if gi == 0:
                    S.dma("sp", "icnt", ICNT, d_icnt[k][:, j * TT:(j + 1) * TT], writes=[bICNT])
                if pf["a"] == 0:
                    S.dma("pool", "wu%d" % slot, WU[slot], d_win[n], writes=[bWU[slot]])
                pf["a"] += 1
                if pf["a"] < 32:
                    n2 = (n + 1) % NCH
                    S.dma("pool", "wu%d" % (n2 % 2), WU[n2 % 2], d_win[n2], writes=[bWU[n2 % 2]])

                def mm_u(e):
                    for c in range((L + 511) // 512):
                        w = min(512, L - c * 512)
                        for dc in range(NDC):
                            r = e.matmul(P03[:, c * 512:c * 512 + w], lhsT=WU[slot][:, dc, :],
                                         rhs=H1[:, dc, r0 * 64 + c * 512:r0 * 64 + c * 512 + w],
                                         start=(dc == 0), stop=(dc == NDC - 1))
                    return r
                S.op("pe", mm_u, [bWU[slot], bH1], [bP01, bP23])
                Uv = U32[:, 8:8 + R * 80].rearrange("p (r c) -> p r c", c=80)[:, :, 8:72]
                S.op("act", lambda e: e.activation(out=Uv, in_=P03[:, 0:L].rearrange("p (r c) -> p r c", c=64), func=AF.Identity),
                     [bP01, bP23], [bU32])
                src, bsrc = U32, bU32
                dsts = [(B1, bB1), (B2, bB2)]
                rng = [(hw, hw + 16)]
                for l in range(k, 0, -1):
                    a, b_ = SHIFTS[l]
                    rng.insert(0, (rng[0][0] - a, rng[0][1] + b_))
                for l in range(k + 1):
                    a, b_ = SHIFTS[l]
                    rlo, rhi = rng[l]
                    dst, bdst = dsts[l % 2]
                    lo, hi = 8 + rlo * 80, 8 + rhi * 80
                    S.op("dve", lambda e, dst=dst, src=src, a=a, b_=b_, lo=lo, hi=hi: e.tensor_tensor(
                        out=dst[:, lo:hi], in0=src[:, lo - 80 * a:hi - 80 * a], in1=src[:, lo + 80 * b_:hi + 80 * b_], op=ALU.add),
                        [bsrc], [bdst])
                    src, bsrc = dst, bdst
                lo, hi = 8 + hw * 80, 8 + (hw + 16) * 80
                for l in range(k + 1):
                    a, b_ = SHIFTS[l]
                    dst, bdst = dsts[(k + 1 + l) % 2]
                    S.op("dve", lambda e, dst=dst, src=src, a=a, b_=b_, lo=lo, hi=hi: e.tensor_tensor(
                        out=dst[:, lo:hi], in0=src[:, lo - a:hi - a], in1=src[:, lo + b_:hi + b_], op=ALU.add), [bsrc], [bdst])
                    src, bsrc = dst, bdst
                own = slice(8 + hw * 80, 8 + (hw + 16) * 80)
                RSv = src[:, own].rearrange("p (r c) -> p r c", c=80)[:, :, 8:72]
                UOv = U32[:, own].rearrange("p (r c) -> p r c", c=80)[:, :, 8:72]
                S.op("dve", lambda e: e.tensor_tensor(out=TMP.rearrange("p (r c) -> p r c", c=64), in0=RSv,
                                                      in1=ICNT.rearrange("p (r c) -> p r c", c=64), op=ALU.mult),
                     [bsrc, bICNT], [bTMP])
                S.op("dve", lambda e: e.tensor_tensor(out=DDs[par][:, gi, :].rearrange("p (r c) -> p r c", c=64),
                                                      in0=TMP.rearrange("p (r c) -> p r c", c=64), in1=UOv, op=ALU.subtract),
                     [bTMP, bU32], [bDDs[par]])

            def stage_b_chunk(j, k, hc, par):
                n = 4 * k + hc
                slot = n % 2
                if hc == 0:
                    S.dma("pool", "pw", PWb, d_pw[k], writes=[bPW])
                    S.dma("pool", "wo", WO, d_wout[4 * k:4 * k + 4].rearrange("a p d -> p a d"), writes=[bWO])
                if pf["b"] == 0:
                    S.dma("pool", "wg%d" % slot, WG[slot], d_win[NCH + n], writes=[bWG[slot]])
                pf["b"] += 1
                if pf["b"] < 32:
                    n2 = (n + 1) % NCH
                    S.dma("pool", "wg%d" % (n2 % 2), WG[n2 % 2], d_win[NCH + n2], writes=[bWG[n2 % 2]])

                def mm_gg(e):
                    for c in range(2):
                        for dc in range(NDC):
                            c0 = 512 + j * TT + c * 512
                            r = e.matmul(P45[:, c * 512:(c + 1) * 512], lhsT=WG[slot][:, dc, :], rhs=H1[:, dc, c0:c0 + 512],
                                         start=(dc == 0), stop=(dc == NDC - 1))
                    return r
                S.op("pe", mm_gg, [bWG[slot], bH1], [bP45])
                S.op("act", lambda e: e.activation(out=T1, in_=P45, func=AF.Silu), [bP45], [bT1])

                def mm_p(e):
                    for c in range(2):
                        for gc in range(4):
                            r = e.matmul(P67[:, c * 512:(c + 1) * 512], lhsT=PWb[:, gc, hc * 128:(hc + 1) * 128],
                                         rhs=DDs[par][:, gc, c * 512:(c + 1) * 512], start=(gc == 0), stop=(gc == 3))
                    return r
                S.op("pe", mm_p, [bPW, bDDs[par]], bP67)
                S.op("dve", lambda e: e.scalar_tensor_tensor(out=Z[:, hc, :], in0=P67, scalar=PS_[:, n:n + 1], in1=T1,
                                                             op0=ALU.mult, op1=ALU.mult), bP67 + [bPS, bT1], [bZ])

            def stage_b_out(j):
                for dc in range(NDC):
                    PP, bPP = (P45, [bP45]) if dc % 2 == 0 else (P67, bP67)

                    def mm_o(e, dc=dc, PP=PP):
                        for c in range(2):
                            for i in range(4):
                                r = e.matmul(PP[:, c * 512:(c + 1) * 512], lhsT=WO[:, i, dc * 128:(dc + 1) * 128],
                                             rhs=Z[:, i, c * 512:(c + 1) * 512], start=(i == 0), stop=(i == 3))
                        return r
                    S.op("pe", mm_o, [bWO, bZ], bPP)
                    S.op("dve", lambda e, dc=dc, PP=PP: e.scalar_tensor_tensor(
                        out=XACC[:, dc, j * TT:(j + 1) * TT], in0=PP, scalar=GT[:, dc, 0:1], in1=XACC[:, dc, j * TT:(j + 1) * TT],
                        op0=ALU.mult, op1=ALU.add), bPP + [bMOD] + bXs([dc], [2 * j, 2 * j + 1]), bXs([dc], [2 * j, 2 * j + 1]))

            groups = [(j, k) for j in range(2) for k in range(4)]
            for gi in range(4):
                stage_a_chunk(groups[0][0], groups[0][1], gi, 0)
            for g, (j, k) in enumerate(groups):
                par = g % 2
                for i in range(4):
                    if g + 1 < len(groups):
                        stage_a_chunk(groups[g + 1][0], groups[g + 1][1], i, 1 - par)
                    stage_b_chunk(j, k, i, par)
                stage_b_out(j)

        if kind == "L1":
            T1 = AR.f32("T1", TT); bT1 = Buf("T1")
            L1_body()

        if kind in ("B", "L1"):
            S.barrier()
            AR.off = ln_mark if kind == "L1" else work_mark
            P67 = psum[:, 6:8, :].rearrange("p a b -> p (a b)")
            ONES = AR.f32("ONES", 128); bONES = Buf("ONES")
            LNG = AR.f32("LNG", NDC); LNB = AR.f32("LNB", NDC); bLN = Buf("LN")
            SQT = [AR.f32("SQT%d" % i, 512) for i in range(2)]; bSQT = [Buf("SQT0"), Buf("SQT1")]
            MEANs = [AR.f32("MEAN%d" % i, 512) for i in range(2)]; bMEANs = [Buf("MEAN0"), Buf("MEAN1")]
            RSTDs = [AR.f32("RSTD%d" % i, 512) for i in range(2)]; bRSTDs = [Buf("RSTD0"), Buf("RSTD1")]
            S.op("pool", lambda e: e.memset(ONES, 1.0), [], [bONES])
            S.dma("sp", "c9", LNG, d_lng, writes=[bLN])
            S.dma("sp", "c10", LNB, d_lnb, writes=[bLN])
            for tc_ in range(4):
                cs = slice(tc_ * 512, (tc_ + 1) * 512)
                par = tc_ % 2
                MEAN, bMEAN, RSTD, bRSTD = MEANs[par], bMEANs[par], RSTDs[par], bRSTDs[par]
                PS1, bPS1, PS2, bPS2 = (P01, [bP01], P23, [bP23]) if par == 0 else (P45, [bP45], P67, [bP6, bP7])

                def mm_sum(e, cs=cs, PS1=PS1):
                    for dc in range(NDC):
                        r = e.matmul(PS1[:, 0:512], lhsT=ONES, rhs=XACC[:, dc, cs], start=(dc == 0), stop=(dc == NDC - 1))
                    return r
                S.op("pe", mm_sum, [bONES] + bXs(tcs=[tc_]), bPS1)
                for dc in range(NDC):
                    S.op("act", lambda e, dc=dc, cs=cs: e.activation(out=SQT[dc % 2], in_=XACC[:, dc, cs], func=AF.Square),
                         bXs([dc], [tc_]), [bSQT[dc % 2]])
                    S.op("pe", lambda e, dc=dc, PS2=PS2: e.matmul(PS2[:, 0:512], lhsT=ONES, rhs=SQT[dc % 2],
                                                                  start=(dc == 0), stop=(dc == NDC - 1)), [bONES, bSQT[dc % 2]], bPS2)
                S.op("dve", lambda e, MEAN=MEAN, PS1=PS1: e.tensor_scalar(out=MEAN, in0=PS1[:, 0:512], scalar1=1.0 / D, scalar2=None, op0=ALU.mult),
                     bPS1, [bMEAN])
                S.op("dve", lambda e, MEAN=MEAN, RSTD=RSTD: e.tensor_tensor(out=RSTD, in0=MEAN, in1=MEAN, op=ALU.mult), [bMEAN], [bRSTD])
                S.op("dve", lambda e, RSTD=RSTD, PS2=PS2: e.scalar_tensor_tensor(out=RSTD, in0=PS2[:, 0:512], scalar=1.0 / D, in1=RSTD,
                                                                                 op0=ALU.mult, op1=ALU.subtract), bPS2 + [bRSTD], [bRSTD])
                S.op("dve", lambda e, RSTD=RSTD: e.tensor_scalar(out=RSTD, in0=RSTD, scalar1=LN_EPS, scalar2=None, op0=ALU.add), [bRSTD], [bRSTD])
                S.op("act", lambda e, RSTD=RSTD: e.activation(out=RSTD, in_=RSTD, func=AF.Sqrt), [bRSTD], [bRSTD])
                S.op("dve", lambda e, RSTD=RSTD: e.reciprocal(out=RSTD, in_=RSTD), [bRSTD], [bRSTD])
                for dc in range(NDC):
                    eng = "dve" if dc % 3 != 2 else "pool"
                    bx = bXs([dc], [tc_])
                    S.op(eng, lambda e, dc=dc, cs=cs, MEAN=MEAN: e.tensor_tensor(out=XACC[:, dc, cs], in0=XACC[:, dc, cs], in1=MEAN, op=ALU.subtract),
                         bx + [bMEAN], bx)
                    S.op(eng, lambda e, dc=dc, cs=cs, RSTD=RSTD: e.tensor_tensor(out=XACC[:, dc, cs], in0=XACC[:, dc, cs], in1=RSTD, op=ALU.mult),
                         bx + [bRSTD], bx)
                    S.op("act", lambda e, dc=dc, cs=cs: e.activation(out=XACC[:, dc, cs], in_=XACC[:, dc, cs], func=AF.Identity,
                                                                     bias=LNB[:, dc:dc + 1], scale=LNG[:, dc:dc + 1]), bx + [bLN], bx)
                out_toks.append(S.dma("sp", "ox%d" % tc_, o_x[:, :, cs], XACC[:, :, cs], reads=bXs(tcs=[tc_])))


        S.final_wait("sp", out_toks)
        with nc.Block() as block:
            S.emit(block)
    return nc


def _fm(v):
    v = np.asarray(v, np.float32)
    n = v.shape[-1] // 128
    lead = v.shape[:-1]
    v = v.reshape(*lead, n, 128)
    perm = (len(lead) + 1, len(lead)) + tuple(range(len(lead)))
    return np.ascontiguousarray(v.transpose(perm))


_PROGS = {}


def _prog(kind):
    if kind not in _PROGS:
        _PROGS[kind] = build(kind)
    return _PROGS[kind]


def _common_maps(layer, c, c_ctx, w_mod, b_mod, w_in, with_mod=False):
    win = np.ascontiguousarray(np.asarray(w_in[layer], np.float32).reshape(NDC, 128, 32, 128).transpose(2, 1, 0, 3))
    maps = []
    extra = {}
    if with_mod:
        for l, sfx in ((0, ""), (1, "1")):
            extra["wm" + sfx] = np.ascontiguousarray(np.asarray(w_mod[l], np.float32).reshape(NDC, 128, 24, 128).transpose(2, 1, 0, 3))
            extra["bm" + sfx] = _fm(b_mod[l])
    for core in range(NCORES):
        b = core // 4
        m = {"win": win}
        if with_mod:
            m["cv"] = np.ascontiguousarray(np.stack([_fm(c[b]), _fm(c_ctx)], axis=-1))
            m.update(extra)
        maps.append(m)
    return maps


def layer0_inputs(x, ctx, conv_w, conv_b, lru_wa, lru_ba, lru_wx, lru_bx, lru_lam):
    cw = np.ascontiguousarray(_fm(conv_w[0]))
    cb = _fm(conv_b[0])
    gw = np.stack([lru_wa[0, 0], lru_wx[0, 0], lru_wa[0, 1], lru_wx[0, 1]], axis=0)
    gw = np.ascontiguousarray(np.asarray(gw, np.float32).transpose(1, 2, 0, 3))
    gb = np.stack([lru_ba[0, 0], lru_bx[0, 0], lru_ba[0, 1], lru_bx[0, 1]], axis=0)
    gb = np.ascontiguousarray(_fm(gb).transpose(0, 2, 1))
    lam = np.ascontiguousarray(_fm(lru_lam[0]).transpose(0, 2, 1))
    xs = np.asarray(x, np.float32)
    S_ = xs.shape[1]
    maps = []
    for core in range(NCORES):
        b, q = core // 4, core % 4
        t0 = q * TOK
        xt = _fm(xs[b, t0:t0 + TOK])
        xh = np.zeros((128, NDC, 2, 3), np.float32)
        hm = np.zeros((128, 2, 3), np.float32)
        for j in range(2):
            s = t0 + j * TT
            for k, t in enumerate((s - 2, s - 1, s + TT)):
                if 0 <= t < S_:
                    xh[:, :, j, k] = _fm(xs[b, t])
                    hm[:, j, k] = 1.0
        maps.append({"x": xt, "xh": xh, "hmask": hm, "ctx": _fm(np.asarray(ctx, np.float32)[b]),
                     "cw": cw, "cb": cb, "gw": gw, "gb": gb, "lam": lam})
    return maps


def run_A(inputs):
    cm = _common_maps(0, inputs["c"], inputs["c_ctx"], inputs["w_mod"], inputs["b_mod"], inputs["w_in"], with_mod=True)
    lm = layer0_inputs(inputs["x"], inputs["ctx"], inputs["conv_w"], inputs["conv_b"], inputs["lru_wa"], inputs["lru_ba"],
                       inputs["lru_wx"], inputs["lru_bx"], inputs["lru_lam"])
    maps = [dict(a, **b) for a, b in zip(cm, lm)]
    for core, m in enumerate(maps):
        q = core % 4
        sl = slice(4 * q, 4 * q + 4)
        m["winc"] = np.ascontiguousarray(m["win"][sl])
        m["gwc"] = np.ascontiguousarray(m["gw"][sl])
        m["cwc"] = np.ascontiguousarray(m["cw"][:, sl, :])
        m["cbc"] = np.ascontiguousarray(m["cb"][:, sl])
        m["gbc"] = np.ascontiguousarray(m["gb"][:, :, sl])
        m["lamc"] = np.ascontiguousarray(m["lam"][:, :, sl])
    res = run_bass_kernel_spmd(_prog("A"), maps, core_ids=list(range(NCORES)))
    mods = [(r["mod0"], r["mod1"]) for r in res.results]
    return maps, [r["ep"] for r in res.results], [r["hc"] for r in res.results], mods


def carries_layout(eps, hcs):
    outs = []
    for core in range(NCORES):
        b, q = core // 4, core % 4
        epf = np.zeros((128, 2, 7, NCH, 2), np.float32)
        epb = np.zeros((128, 2, 7, NCH, 2), np.float32)
        epf[..., 0] = 1.0
        epb[..., 0] = 1.0
        for j in range(2):
            v = 2 * q + j
            for s_, r in enumerate(range(0, v)):
                src = eps[b * 4 + r // 2][:, r % 2]
                epf[:, j, s_, :, 0] = src[:, :, 1]
                epf[:, j, s_, :, 1] = src[:, :, 0]
            for s_, r in enumerate(range(7, v, -1)):
                src = eps[b * 4 + r // 2][:, r % 2]
                epb[:, j, s_, :, 0] = src[:, :, 3]
                epb[:, j, s_, :, 1] = src[:, :, 2]
        hcin = np.concatenate([hcs[b * 4 + qq] for qq in range(4)], axis=1)
        outs.append({"epf": epf, "epb": epb, "hcin": np.ascontiguousarray(hcin)})
    return outs


def run_B(inputs, maps, eps, hcs, mods):
    wout = np.ascontiguousarray(np.asarray(inputs["w_out"][0], np.float32).reshape(NCH, 128, D))
    lng, lnb = _fm(inputs["ln_g"][0]), _fm(inputs["ln_b"][0])
    cl = carries_layout(eps, hcs)
    m2 = []
    for core in range(NCORES):
        m = {k: v for k, v in maps[core].items() if k not in ("cv", "wm", "wm1", "bm", "bm1", "winc", "gwc", "cwc", "cbc", "gbc", "lamc")}
        m["modin"] = np.ascontiguousarray(mods[core][0])
        m.update(cl[core])
        m.update({"wout": wout, "lng": lng, "lnb": lnb})
        m2.append(m)
    res = run_bass_kernel_spmd(_prog("B"), m2, core_ids=list(range(NCORES)))
    return [r["xo"] for r in res.results]


def run_L1(inputs, xo, mods):
    cm = _common_maps(1, inputs["c"], inputs["c_ctx"], inputs["w_mod"], inputs["b_mod"], inputs["w_in"])
    wout = np.ascontiguousarray(np.asarray(inputs["w_out"][1], np.float32).reshape(NCH, 128, D))
    lng, lnb = _fm(inputs["ln_g"][1]), _fm(inputs["ln_b"][1])
    pw = np.ascontiguousarray(np.asarray(inputs["pool_w"][0], np.float32).reshape(4, 4, 128, 512).transpose(0, 2, 1, 3))
    ps = _fm(inputs["pool_scale"][0])
    cols = np.arange(64)
    maps = []
    for core in range(NCORES):
        b, q = core // 4, core % 4
        xhalo = np.zeros((128, NDC, 960), np.float32)
        hm1 = np.zeros((128, 2), np.float32)
        if q > 0:
            xhalo[:, :, 0:512] = xo[core - 1][:, :, TOK - 512:TOK]
            hm1[:, 0] = 1.0
        if q < 3:
            xhalo[:, :, 512:960] = xo[core + 1][:, :, 0:448]
            hm1[:, 1] = 1.0
        icnt = np.zeros((4, 128, TOK), np.float32)
        rows = q * 32 + np.arange(32)
        for k, w in enumerate((2, 4, 8, 16)):
            cc = (np.minimum(cols + w // 2, 64) - np.maximum(cols - w // 2, 0)).astype(np.float32)
            cr = (np.minimum(rows + w // 2, 128) - np.maximum(rows - w // 2, 0)).astype(np.float32)
            icnt[k] = (np.float32(1.0) / (cr[:, None] * cc[None, :])).reshape(-1)[None, :]
        m = dict(cm[core])
        m["modin"] = np.ascontiguousarray(mods[core][1])
        m.update({"x": np.ascontiguousarray(xo[core]), "xhalo": xhalo, "hm1": hm1, "icnt": icnt, "pw": pw, "ps": ps,
                  "wout": wout, "lng": lng, "lnb": lnb})
        maps.append(m)
    res = run_bass_kernel_spmd(_prog("L1"), maps, core_ids=list(range(NCORES)))
    return [r["xo"] for r in res.results]


def kernel(**inputs):
    maps, eps, hcs, mods = run_A(inputs)
    xo = run_B(inputs, maps, eps, hcs, mods)
    xo = run_L1(inputs, xo, mods)
    x = np.asarray(inputs["x"])
    out = np.zeros(x.shape, np.float32)
    for core in range(NCORES):
        b, q = core // 4, core % 4
        out[b, q * TOK:(q + 1) * TOK] = xo[core].transpose(2, 1, 0).reshape(TOK, D)
    return out
```

```python
import contextlib
import numpy as np
import concourse.bass as bass
import concourse.mybir as mybir
from concourse.bass_utils import run_bass_kernel_spmd

F32 = mybir.dt.float32
BF16 = mybir.dt.bfloat16
ALU = mybir.AluOpType
AF = mybir.ActivationFunctionType

D = 1024
DI = 2048
NDC = 8
NCH = 16
TOK = 2048
TT = 1024
CTX = 256
ALPHA = float(4 ** 0.25)
LN_EPS = 1e-5
NCORES = 8
SAME_ENG_SYNC = True


class Buf:
    def __init__(self, name):
        self.name = name
        self.w = None
        self.r = []


class Sched:
    ENGS = ("pe", "act", "dve", "pool", "sp")

    def __init__(self, nc, stack, n_dma=40):
        self.nc = nc
        self.q = {e: [] for e in self.ENGS}
        self.sems = {}
        for e in ("pe", "act", "dve", "pool"):
            self.sems[e] = stack.enter_context(nc.semaphore("sem_" + e))
        self.dma_free = [stack.enter_context(nc.semaphore("sem_dma%d" % i)) for i in range(n_dma)]
        self.cnt = {}
        self.waited = {e: {} for e in self.ENGS}
        self.dma_keys = {}

    def _wait(self, eng, tok):
        if tok is None:
            return
        key, val = tok
        if not SAME_ENG_SYNC and key == eng:
            return
        if self.waited[eng].get(key, 0) >= val:
            return
        self.waited[eng][key] = val
        sem = self.sems[key]
        self.q[eng].append(lambda e, sem=sem, val=val: e.wait_ge(sem, val))

    def _deps(self, eng, reads, writes):
        for b in reads:
            self._wait(eng, b.w)
        for b in writes:
            self._wait(eng, b.w)
            for t in b.r:
                self._wait(eng, t)

    def _mark(self, tok, reads, writes):
        for b in writes:
            b.w = tok
            b.r = []
        for b in reads:
            if b not in writes:
                b.r.append(tok)

    def op(self, eng, fn, reads=(), writes=()):
        self._deps(eng, reads, writes)
        self.cnt[eng] = self.cnt.get(eng, 0) + 1
        tok = (eng, self.cnt[eng])
        sem = self.sems[eng]
        self.q[eng].append(lambda e, fn=fn, sem=sem: fn(e).then_inc(sem, 1))
        self._mark(tok, reads, writes)
        return tok

    def dma(self, eng, key, out, in_, reads=(), writes=()):
        if key not in self.dma_keys:
            self.dma_keys[key] = "dma_" + key
            self.sems["dma_" + key] = self.dma_free.pop()
        k = self.dma_keys[key]
        self._deps(eng, reads, writes)
        self.cnt[k] = self.cnt.get(k, 0) + 16
        tok = (k, self.cnt[k])
        sem = self.sems[k]
        self.q[eng].append(lambda e, out=out, in_=in_, sem=sem: e.dma_start(out=out, in_=in_).then_inc(sem, 16))
        self._mark(tok, reads, writes)
        return tok

    def barrier(self):
        toks = [(k, v) for k, v in self.cnt.items() if v > 0]
        for eng in self.ENGS:
            for t in toks:
                self._wait(eng, t)

    def final_wait(self, eng, toks):
        for t in toks:
            self._wait(eng, t)

    def emit(self, block):
        q = self.q

        @block.tensor
        def _(e):
            for f in q["pe"]:
                f(e)

        @block.scalar
        def _(e):
            for f in q["act"]:
                f(e)

        @block.vector
        def _(e):
            for f in q["dve"]:
                f(e)

        @block.gpsimd
        def _(e):
            for f in q["pool"]:
                f(e)

        @block.sync
        def _(e):
            for f in q["sp"]:
                f(e)


class Arena:
    def __init__(self, nc, nbytes):
        self.t = nc.alloc_sbuf_tensor("arena", [128, nbytes // 4], F32)
        self.off = 0
        self.cap = nbytes

    def f32(self, name, *shape):
        n = int(np.prod(shape))
        ap = self.t[:, self.off // 4:self.off // 4 + n]
        self.off += n * 4
        assert self.off <= self.cap, (name, self.off, self.cap)
        return self._shape(ap, shape)

    def bf16(self, name, *shape):
        n = int(np.prod(shape))
        nb = (n * 2 + 3) // 4 * 4
        ap = self.t[:, self.off // 4:self.off // 4 + nb // 4].bitcast(BF16)[:, 0:n]
        self.off += nb
        assert self.off <= self.cap, (name, self.off, self.cap)
        return self._shape(ap, shape)

    @staticmethod
    def _shape(ap, shape):
        if len(shape) == 1:
            return ap
        if len(shape) == 2:
            return ap.rearrange("p (a b) -> p a b", a=shape[0])
        if len(shape) == 3:
            return ap.rearrange("p (a b c) -> p a b c", a=shape[0], b=shape[1])
        if len(shape) == 4:
            return ap.rearrange("p (a b c d) -> p a b c d", a=shape[0], b=shape[1], c=shape[2])
        raise ValueError


def build(kind):
    nc = bass.Bass("TRN2", target_bir_lowering=False)
    layer = 1 if kind == "L1" else 0

    def din(name, shape, dt=F32):
        return nc.dram_tensor(name, list(shape), dt, kind="ExternalInput").ap()

    def dout(name, shape, dt=F32):
        return nc.dram_tensor(name, list(shape), dt, kind="ExternalOutput").ap()

    if kind == "A":
        d_cv = din("cv", [128, NDC, 2])
        d_wm = [din("wm", [24, 128, NDC, 128]), din("wm1", [24, 128, NDC, 128])]
        d_bm = [din("bm", [128, 24]), din("bm1", [128, 24])]
        o_mod = [dout("mod0", [128, 24, 2]), dout("mod1", [128, 24, 2])]
    else:
        d_modin = din("modin", [128, 24, 2])
    d_win = din("win", [32, 128, NDC, 128])
    if kind in ("A", "B"):
        d_x = din("x", [128, NDC, TOK])
        d_xh = din("xh", [128, NDC, 2, 3])
        d_hmask = din("hmask", [128, 2, 3])
        d_ctx = din("ctx", [128, NDC, CTX])
        d_cw = din("cw", [128, NCH, 4])
        d_cb = din("cb", [128, NCH])
        d_gw = din("gw", [NCH, 128, 4, 128])
        d_gb = din("gb", [128, 4, NCH])
        d_lam = din("lam", [128, 2, NCH])
    NCC = 4
    if kind == "A":
        o_ep = dout("ep", [128, 2, NCH, 4])
        o_hc = dout("hc", [128, NCC, 2])
        d_winc = din("winc", [NCC, 128, NDC, 128])
        d_gwc = din("gwc", [NCC, 128, 4, 128])
        d_cwc = din("cwc", [128, NCC, 4])
        d_cbc = din("cbc", [128, NCC])
        d_gbc = din("gbc", [128, 4, NCC])
        d_lamc = din("lamc", [128, 2, NCC])
    if kind == "B":
        d_epf = din("epf", [128, 2, 7, NCH, 2])
        d_epb = din("epb", [128, 2, 7, NCH, 2])
        d_hc = din("hcin", [128, NCH, 2])
    if kind == "L1":
        d_x = din("x", [128, NDC, TOK])
        d_xhalo = din("xhalo", [128, NDC, 960])
        d_icnt = din("icnt", [4, 128, TOK])
        d_pw = din("pw", [4, 128, 4, 512])
        d_ps = din("ps", [128, NCH])
        d_hm1 = din("hm1", [128, 2])
    if kind in ("B", "L1"):
        d_wout = din("wout", [NCH, 128, D])
        d_lng = din("lng", [128, NDC])
        d_lnb = din("lnb", [128, NDC])
        o_x = dout("xo", [128, NDC, TOK])

    stack = contextlib.ExitStack()
    with stack:
        S = Sched(nc, stack)
        AR = Arena(nc, 206 * 1024)
        psum = nc.alloc_psum_tensor("psum", [128, 8, 512], F32)
        P01 = psum[:, 0:2, :].rearrange("p a b -> p (a b)")
        P23 = psum[:, 2:4, :].rearrange("p a b -> p (a b)")
        P45 = psum[:, 4:6, :].rearrange("p a b -> p (a b)")
        P6 = psum[:, 6, :]
        P7 = psum[:, 7, :]
        bP01, bP23, bP45, bP6, bP7 = (Buf(n) for n in ("P01", "P23", "P45", "P6", "P7"))

        CV = AR.f32("CV", NDC, 2)
        BM = AR.f32("BM", 24)
        MOD = AR.f32("MOD", 24, 2)
        SC1 = AR.f32("SC1", NDC, 2)
        SCV = AR.f32("SCV", NDC, 2)
        TMPC = AR.f32("TMPC", NDC, 2)
        NWM = 6 if kind == "A" else 2
        WM = [AR.f32("WM%d" % i, NDC, 128) for i in range(NWM)] if kind != "B" else []
        bWM = [Buf("WM%d" % i) for i in range(NWM)]
        bCV, bBM, bMOD, bSC1, bSCV, bTMPC = (Buf(n) for n in ("CV", "BM", "MOD", "SC1", "SCV", "TMPC"))
        MOD1 = AR.f32("MOD1", 24, 2); bMOD1 = Buf("MOD1")

        def adaln_step(l, jc):
            sl = jc % NWM
            S.dma("sp", "wm%d" % sl, WM[sl], d_wm[l][jc], writes=[bWM[sl]])

            def mm(e, jc=jc, sl=sl):
                for dc in range(NDC):
                    r = e.matmul(P7[:, 2 * jc:2 * jc + 2], lhsT=WM[sl][:, dc, :], rhs=SCV[:, dc, :],
                                 start=(dc == 0), stop=(dc == NDC - 1))
                return r
            S.op("pe", mm, [bWM[sl], bSCV], [bP7])

        def adaln_finish(l, MODd, bMODd, lo=0, hi=24, load_bm=True):
            if load_bm:
                S.dma("sp", "bm", BM, d_bm[l], writes=[bBM])
            P7m = P7[:, 0:48].rearrange("p (a b) -> p a b", b=2)
            for col in range(2):
                S.op("dve", lambda e, col=col: e.tensor_tensor(out=MODd[:, lo:hi, col], in0=P7m[:, lo:hi, col], in1=BM[:, lo:hi], op=ALU.add),
                     [bP7, bBM], [bMODd])

        def adaln(l, MODd, bMODd):
            for jc in range(24):
                adaln_step(l, jc)
            adaln_finish(l, MODd, bMODd)

        if kind == "A":
            S.dma("sp", "cv", CV, d_cv, writes=[bCV])
            S.op("act", lambda e: e.activation(out=TMPC, in_=CV, func=AF.Tanh, scale=0.5), [bCV], [bTMPC])
            S.op("dve", lambda e: e.scalar_tensor_tensor(out=TMPC, in0=TMPC, scalar=1.0, in1=CV, op0=ALU.add, op1=ALU.mult),
                 [bCV, bTMPC], [bTMPC])
            S.op("dve", lambda e: e.tensor_scalar(out=SCV, in0=TMPC, scalar1=0.5, scalar2=None, op0=ALU.mult), [bTMPC], [bSCV])
            for jc in range(16):
                adaln_step(0, jc)
            adaln_finish(0, MOD, bMOD, 0, 16)
        else:
            S.dma("sp", "modin", MOD, d_modin, writes=[bMOD])
        S.op("dve", lambda e: e.tensor_scalar(out=SC1, in0=MOD[:, 8:16, :], scalar1=1.0, scalar2=None, op0=ALU.add),
             [bMOD], [bSC1])
        SH = MOD[:, 0:8, :]
        GT = MOD[:, 16:24, :]

        XACC = AR.f32("XACC", NDC, TOK if kind != "A" else TT)
        bXc = [[Buf("X%d_%d" % (dc, t)) for t in range(4)] for dc in range(NDC)]

        def bXs(dcs=range(NDC), tcs=range(4)):
            return [bXc[dc][t] for dc in dcs for t in tcs]
        def load_x(j_):
            if kind == "A":
                S.dma("sp", "xload%d" % j_, XACC, d_x[:, :, j_ * TT:(j_ + 1) * TT], writes=bXs(tcs=[0, 1]))
            else:
                S.dma("sp", "xload%d" % j_, XACC[:, :, j_ * TT:(j_ + 1) * TT], d_x[:, :, j_ * TT:(j_ + 1) * TT], writes=bXs(tcs=[2 * j_, 2 * j_ + 1]))
        load_x(0)
        if kind != "A":
            load_x(1)

        out_toks = []

        if kind in ("A", "B"):
            CW = AR.f32("CW", NCH, 4); bCW = Buf("CW")
            CB = AR.f32("CB", NCH); bCB = Buf("CB")
            GB = AR.f32("GB", 4, NCH); bGB = Buf("GB")
            LAM = AR.f32("LAM", 2, NCH); bLAM = Buf("LAM")
            HSs = AR.f32("HS", 2, NCH); bHS = Buf("HSs")
            SS = AR.f32("SS", 2, NCH)
            LT = AR.f32("LT", 2, NCH); LT2 = AR.f32("LT2", 2, NCH)
            HMASK = AR.f32("HMASK", 2, 3); bHM = Buf("HMASK")
            if kind == "A":
                ZERO = AR.f32("ZERO", TT); bZERO = Buf("ZERO")
            S.dma("sp", "c0", CW, d_cw, writes=[bCW])
            S.dma("sp", "c1", CB, d_cb, writes=[bCB])
            S.dma("sp", "c2", GB, d_gb, writes=[bGB])
            S.dma("sp", "c3", LAM, d_lam, writes=[bLAM])
            S.dma("sp", "c4", HMASK, d_hmask, writes=[bHM])
            if kind == "A":
                S.op("pool", lambda e: e.memset(ZERO, 0.0), [], [bZERO])
            def prep_consts(GB_, bGB_, LAM_, bLAM_, LT_, LT2_, SS_, HS_, bHS_):
                S.op("dve", lambda e: e.tensor_scalar(out=GB_, in0=GB_, scalar1=0.5, scalar2=None, op0=ALU.mult), [bGB_], [bGB_])
                S.op("act", lambda e: e.activation(out=LT_, in_=LAM_, func=AF.Exp, scale=-1.0), [bLAM_], [bHS_])
                S.op("dve", lambda e: e.tensor_scalar(out=LT2_, in0=LT_, scalar1=-0.2, scalar2=0.25, op0=ALU.mult, op1=ALU.add), [bHS_], [bHS_])
                for cst in (1.0 / 3.0, 0.5, 1.0):
                    S.op("dve", lambda e: e.tensor_tensor(out=LT2_, in0=LT2_, in1=LT_, op=ALU.mult), [bHS_], [bHS_])
                    S.op("dve", lambda e, cst=cst: e.tensor_scalar(out=LT2_, in0=LT2_, scalar1=-1.0, scalar2=cst, op0=ALU.mult, op1=ALU.add), [bHS_], [bHS_])
                S.op("dve", lambda e: e.tensor_tensor(out=LT2_, in0=LT2_, in1=LT_, op=ALU.mult), [bHS_], [bHS_])
                S.op("dve", lambda e: e.tensor_scalar(out=SS_, in0=LT2_, scalar1=-8.0, scalar2=None, op0=ALU.mult), [bHS_], [bHS_])
                S.op("dve", lambda e: e.tensor_scalar(out=HS_, in0=LT2_, scalar1=-4.0, scalar2=None, op0=ALU.mult), [bHS_], [bHS_])

            prep_consts(GB, bGB, LAM, bLAM, LT, LT2, SS, HSs, bHS)
            main_cs = dict(CW=CW, CB=CB, GB=GB, HSs=HSs, SS=SS, bCW=bCW, bCB=bCB, bGB=bGB, bHS=bHS, win=d_win, gw=d_gw, nch=NCH)
            cur = dict(main_cs)
            if kind == "A":
                CWc = AR.f32("CWc", NCC, 4); CBc = AR.f32("CBc", NCC); GBc = AR.f32("GBc", 4, NCC); LAMc = AR.f32("LAMc", 2, NCC)
                HSc = AR.f32("HSc", 2, NCC); SSc = AR.f32("SSc", 2, NCC); LTc = AR.f32("LTc", 2, NCC); LT2c = AR.f32("LT2c", 2, NCC)
                bCWc, bCBc, bGBc, bLAMc, bHSc = (Buf(x) for x in ("CWc", "CBc", "GBc", "LAMc", "HSc"))
                S.dma("sp", "cc0", CWc, d_cwc, writes=[bCWc])
                S.dma("sp", "cc1", CBc, d_cbc, writes=[bCBc])
                S.dma("sp", "cc2", GBc, d_gbc, writes=[bGBc])
                S.dma("sp", "cc3", LAMc, d_lamc, writes=[bLAMc])
                prep_consts(GBc, bGBc, LAMc, bLAMc, LTc, LT2c, SSc, HSc, bHSc)
                ctx_cs = dict(CW=CWc, CB=CBc, GB=GBc, HSs=HSc, SS=SSc, bCW=bCWc, bCB=bCBc, bGB=bGBc, bHS=bHSc, win=d_winc, gw=d_gwc, nch=NCC)

            work_mark = AR.off
            H = AR.bf16("H", NDC, TT + 3); bH = Buf("H")
            XH = AR.f32("XH", NDC, 2, 3); bXH = Buf("XH")
            S.dma("sp", "c5", XH, d_xh, writes=[bXH])
            WU = [AR.bf16("WU%d" % i, NDC, 128) for i in range(3)]; bWU = [Buf("WU0"), Buf("WU1"), Buf("WU2")]
            GW = [AR.bf16("GW%d" % i, 4, 128) for i in range(2)]; bGW = [Buf("GW0"), Buf("GW1")]
            U = AR.f32("U", TT); bU = Buf("U")
            US = AR.f32("US", 4); bUS = Buf("US")
            UCs = [AR.f32("UC%d" % i, TT) for i in range(2)]; bUCs = [Buf("UC0"), Buf("UC1")]
            UCBs = [AR.bf16("UCB%d" % i, TT) for i in range(2)]; bUCBs = [Buf("UCB0"), Buf("UCB1")]
            THR = [AR.f32("THR%d" % i, TT) for i in range(2)]; bTHR = [Buf("THR0"), Buf("THR1")]
            THIs = [[AR.f32("THI%d%d" % (p, i), TT) for i in range(2)] for p in range(2)]
            bTHIs = [[Buf("THI%d%d" % (p, i)) for i in range(2)] for p in range(2)]
            AAs = [[AR.f32("AA%d%d" % (p, i), TT) for i in range(2)] for p in range(2)]
            bAAs = [[Buf("AA%d%d" % (p, i)) for i in range(2)] for p in range(2)]
            VV = [AR.f32("VV%d" % i, TT) for i in range(2)]; bVV = [Buf("VV0"), Buf("VV1")]
            HSC = AR.f32("HSC", TT); bHSC = Buf("HSC")
            if kind == "A":
                RSUM = AR.f32("RSUM", 2); bRSUM = [Buf("RSUM0"), Buf("RSUM1")]
                HST = AR.f32("HST", 2, NCH)
                S.op("dve", lambda e: e.tensor_scalar(out=HST, in0=HSs, scalar1=float(TT), scalar2=None, op0=ALU.mult), [bHS], [bHS])

            if kind == "A":
                CTXT = AR.f32("CTXT", NDC, CTX); bCTXT = Buf("CTXT")
                S.dma("sp", "ctxl", CTXT, d_ctx, writes=[bCTXT])
                EP = AR.f32("EP", 2, NCH, 4); bEP = Buf("EP")
                EPP = AR.f32("EPP", 2, NCH, 2); bEPP = Buf("EPP")
                HCO = AR.f32("HCO", NCC, 2); bHCO = Buf("HCO")
            if kind == "B":
                WG = [AR.bf16("WG%d" % i, NDC, 128) for i in range(2)]; bWG = [Buf("WG0"), Buf("WG1")]
                HF = AR.f32("HF", TT); bHF = Buf("HF")
                HB = AR.f32("HB", TT); bHB = Buf("HB")
                TG = AR.f32("TG", TT); bTG = Buf("TG")
                T1 = AR.f32("T1", TT); bT1 = Buf("T1")
                Z = AR.bf16("Z", 4, TT); bZ = Buf("Z")
                WO = AR.bf16("WO", 4, D); bWO = Buf("WO")
                EPF = AR.f32("EPF", 2, 7, NCH, 2); bEPF = Buf("EPF")
                EPB = AR.f32("EPB", 2, 7, NCH, 2); bEPB = Buf("EPB")
                HCI = AR.f32("HCI", NCH, 2); bHCI = Buf("HCI")
                CAR = AR.f32("CAR", 2, 2, NCH); bCAR = Buf("CAR")
                S.dma("sp", "c6", EPF, d_epf, writes=[bEPF])
                S.dma("sp", "c7", EPB, d_epb, writes=[bEPB])
                S.dma("sp", "c8", HCI, d_hc, writes=[bHCI])
                for j in range(2):
                    for dr, EPX, bEPX in ((0, EPF, bEPF), (1, EPB, bEPB)):
                        S.op("dve", lambda e, j=j, dr=dr: e.tensor_copy(out=CAR[:, j, dr, :], in_=HCI[:, :, dr]), [bHCI], [bCAR])
                        for s_ in range(7):
                            S.op("dve", lambda e, j=j, dr=dr, s_=s_, EPX=EPX: e.tensor_tensor(
                                out=CAR[:, j, dr, :], in0=CAR[:, j, dr, :], in1=EPX[:, j, s_, :, 0], op=ALU.mult), [bEPX, bCAR], [bCAR])
                            S.op("dve", lambda e, j=j, dr=dr, s_=s_, EPX=EPX: e.tensor_tensor(
                                out=CAR[:, j, dr, :], in0=CAR[:, j, dr, :], in1=EPX[:, j, s_, :, 1], op=ALU.add), [bEPX, bCAR], [bCAR])

            def load_wu(n):
                S.dma("pool", "wu%d" % (n % 3), WU[n % 3], cur["win"][n], writes=[bWU[n % 3]])

            def load_gw(n, with_g):
                slot = n % 2
                S.dma("pool", "gw%d" % slot, GW[slot], cur["gw"][n], writes=[bGW[slot]])
                if with_g:
                    S.dma("pool", "wg%d" % slot, WG[slot], d_win[NCH + n], writes=[bWG[slot]])

            def run_fronts(Hsrc, bHsrc, T, halo, j, with_g, tail, pre=None, extra=None):
                nch = cur["nch"]
                load_wu(0)
                load_wu(1)
                load_gw(0, with_g)
                front1(0, Hsrc, bHsrc, T, halo, j)
                front1b(0, T)
                for n in range(nch):
                    if n + 2 < nch:
                        load_wu(n + 2)
                    if n + 1 < nch:
                        load_gw(n + 1, with_g)
                        front1(n + 1, Hsrc, bHsrc, T, halo, j)
                    if pre is not None:
                        pre(n)
                    if extra is not None:
                        extra(n)
                    front2(n, T)
                    if n + 1 < nch:
                        front1b(n + 1, T)
                    tail(n)

            def front1(n, Hsrc, bHsrc, T, halo, j):
                nchunks = (T + 511) // 512
                slot = n % 3
                UC, bUC, UCB, bUCB = UCs[n % 2], bUCs[n % 2], UCBs[n % 2], bUCBs[n % 2]

                def mm_u(e):
                    for c in range(nchunks):
                        w = min(512, T - c * 512)
                        for dc in range(NDC):
                            r = e.matmul(P01[:, c * 512:c * 512 + w], lhsT=WU[slot][:, dc, :], rhs=Hsrc[:, dc, c * 512:c * 512 + w],
                                         start=(dc == 0), stop=(dc == NDC - 1))
                    return r
                S.op("pe", mm_u, [bWU[slot], bHsrc], [bP01])
                if halo:
                    def mm_s(e):
                        for dc in range(NDC):
                            r = e.matmul(P6[:, 0:3], lhsT=WU[slot][:, dc, :], rhs=Hsrc[:, dc, T:T + 3],
                                         start=(dc == 0), stop=(dc == NDC - 1))
                        return r
                    S.op("pe", mm_s, [bWU[slot], bHsrc], [bP6])
                    S.op("dve", lambda e: e.tensor_tensor(out=US[:, 0:3], in0=P6[:, 0:3], in1=HMASK[:, j, :], op=ALU.mult),
                         [bP6, bHM], [bUS])
                S.op("act", lambda e: e.activation(out=U[:, 0:T], in_=P01[:, 0:T], func=AF.Identity), [bP01], [bU])
                w0, w1, w2, w3 = (cur["CW"][:, n, k:k + 1] for k in range(4))
                cbn = cur["CB"][:, n:n + 1]
                S.op("dve", lambda e: e.tensor_scalar(out=UC[:, 0:T], in0=U[:, 0:T], scalar1=w2, scalar2=cbn,
                                                      op0=ALU.mult, op1=ALU.add), [bU, cur["bCW"], cur["bCB"]], [bUC])
                S.op("dve", lambda e: e.scalar_tensor_tensor(out=UC[:, 2:T], in0=U[:, 0:T - 2], scalar=w0, in1=UC[:, 2:T],
                                                             op0=ALU.mult, op1=ALU.add), [bU, bUC], [bUC])
                S.op("dve", lambda e: e.scalar_tensor_tensor(out=UC[:, 1:T], in0=U[:, 0:T - 1], scalar=w1, in1=UC[:, 1:T],
                                                             op0=ALU.mult, op1=ALU.add), [bU, bUC], [bUC])
                S.op("dve", lambda e: e.scalar_tensor_tensor(out=UC[:, 0:T - 1], in0=U[:, 1:T], scalar=w3, in1=UC[:, 0:T - 1],
                                                             op0=ALU.mult, op1=ALU.add), [bU, bUC], [bUC])
                if halo:
                    S.op("dve", lambda e: e.scalar_tensor_tensor(out=UC[:, 0:2], in0=US[:, 0:2], scalar=w0, in1=UC[:, 0:2],
                                                                 op0=ALU.mult, op1=ALU.add), [bUS, bUC], [bUC])
                    S.op("dve", lambda e: e.scalar_tensor_tensor(out=UC[:, 0:1], in0=US[:, 1:2], scalar=w1, in1=UC[:, 0:1],
                                                                 op0=ALU.mult, op1=ALU.add), [bUS, bUC], [bUC])
                    S.op("dve", lambda e: e.scalar_tensor_tensor(out=UC[:, T - 1:T], in0=US[:, 2:3], scalar=w3, in1=UC[:, T - 1:T],
                                                                 op0=ALU.mult, op1=ALU.add), [bUS, bUC], [bUC])

            def front1b(n, T):
                UC, bUC, UCB, bUCB = UCs[n % 2], bUCs[n % 2], UCBs[n % 2], bUCBs[n % 2]
                S.op("act", lambda e: e.activation(out=UCB[:, 0:T], in_=UC[:, 0:T], func=AF.Identity), [bUC], [bUCB])

            def front2(n, T):
                nchunks = (T + 511) // 512
                slot = n % 2
                UC, bUC, UCB, bUCB = UCs[n % 2], bUCs[n % 2], UCBs[n % 2], bUCBs[n % 2]
                THI, bTHI, AA, bAA = THIs[n % 2], bTHIs[n % 2], AAs[n % 2], bAAs[n % 2]
                for dr in range(2):
                    gbr, gbi = cur["GB"][:, dr * 2, n:n + 1], cur["GB"][:, dr * 2 + 1, n:n + 1]
                    hsn, ssn = cur["HSs"][:, dr, n:n + 1], cur["SS"][:, dr, n:n + 1]
                    bGBx, bHSx = cur["bGB"], cur["bHS"]

                    def mm_g(e, dr=dr):
                        for gate, PP in ((0, P23), (1, P45)):
                            for c in range(nchunks):
                                w = min(512, T - c * 512)
                                r = e.matmul(PP[:, c * 512:c * 512 + w], lhsT=GW[slot][:, dr * 2 + gate, :],
                                             rhs=UCB[:, c * 512:c * 512 + w], start=True, stop=True)
                        return r
                    S.op("pe", mm_g, [bGW[slot], bUCB], [bP23, bP45])
                    if kind == "A":
                        S.op("act", lambda e, dr=dr, gbr=gbr: e.activation(out=THR[dr][:, 0:T], in_=P23[:, 0:T], func=AF.Tanh,
                                                                           bias=gbr, scale=0.5, accum_out=RSUM[:, dr:dr + 1]),
                             [bP23, bGBx], [bTHR[dr], bRSUM[dr]])
                    else:
                        S.op("act", lambda e, dr=dr, gbr=gbr: e.activation(out=THR[dr][:, 0:T], in_=P23[:, 0:T], func=AF.Tanh,
                                                                           bias=gbr, scale=0.5), [bP23, bGBx], [bTHR[dr]])
                    S.op("act", lambda e, dr=dr, gbi=gbi: e.activation(out=THI[dr][:, 0:T], in_=P45[:, 0:T], func=AF.Tanh,
                                                                       bias=gbi, scale=0.5), [bP45, bGBx], [bTHI[dr]])
                    S.op("act", lambda e, dr=dr, hsn=hsn: e.activation(out=AA[dr][:, 0:T], in_=THR[dr][:, 0:T], func=AF.Exp,
                                                                       bias=hsn, scale=hsn), [bTHR[dr], bHSx], [bAA[dr]])
                    S.op("act", lambda e, dr=dr, ssn=ssn: e.activation(out=VV[dr][:, 0:T], in_=THR[dr][:, 0:T], func=AF.Exp,
                                                                       bias=ssn, scale=ssn), [bTHR[dr], bHSx], [bVV[dr]])
                    S.op("dve", lambda e, dr=dr: e.tensor_scalar(out=VV[dr][:, 0:T], in0=VV[dr][:, 0:T], scalar1=1.0, scalar2=None, op0=ALU.min),
                         [bVV[dr]], [bVV[dr]])
                    S.op("dve", lambda e, dr=dr: e.scalar_tensor_tensor(out=THI[dr][:, 0:T], in0=THI[dr][:, 0:T], scalar=1.0, in1=UC[:, 0:T],
                                                                        op0=ALU.add, op1=ALU.mult), [bTHI[dr], bUC], [bTHI[dr]])
                for dr in range(2):
                    S.op("act", lambda e, dr=dr: e.activation(out=VV[dr][:, 0:T], in_=VV[dr][:, 0:T], func=AF.Sqrt, bias=1.0, scale=-1.0),
                         [bVV[dr]], [bVV[dr]])
                for dr in range(2):
                    S.op("dve", lambda e, dr=dr: e.scalar_tensor_tensor(out=THI[dr][:, 0:T], in0=THI[dr][:, 0:T], scalar=0.5, in1=VV[dr][:, 0:T],
                                                                        op0=ALU.mult, op1=ALU.mult), [bTHI[dr], bVV[dr]], [bTHI[dr]])

            def scan(eng, out, a, d, init, T, rev, reads, writes):
                if rev:
                    o_, a_, d_ = out[:, 0:T][:, ::-1], a[:, 0:T][:, ::-1], d[:, 0:T][:, ::-1]
                else:
                    o_, a_, d_ = out[:, 0:T], a[:, 0:T], d[:, 0:T]
                return S.op(eng, lambda e: e.tensor_tensor_scan(out=o_, data0=a_, data1=d_, initial=init, op0=ALU.mult, op1=ALU.add),
                            reads, writes)

            def make_h(j, Hb=None, bHb=None):
                Hb = H if Hb is None else Hb
                bHb = bH if bHb is None else bHb
                js = 0 if kind == "A" else j
                for dc in range(NDC):
                    S.op("act", lambda e, dc=dc: e.activation(out=Hb[:, dc, 0:TT], in_=XACC[:, dc, js * TT:(js + 1) * TT], func=AF.Identity,
                                                              bias=SH[:, dc, 0:1], scale=SC1[:, dc, 0:1]), bXs([dc], [2 * js, 2 * js + 1]) + [bMOD, bSC1], [bHb])
                    S.op("act", lambda e, dc=dc: e.activation(out=Hb[:, dc, TT:TT + 3], in_=XH[:, dc, j, :], func=AF.Identity,
                                                              bias=SH[:, dc, 0:1], scale=SC1[:, dc, 0:1]), [bXH, bMOD, bSC1], [bHb])

            if kind == "A":
                HC = AR.bf16("HC", NDC, CTX); bHC = Buf("HC")
                for dc in range(NDC):
                    S.op("act", lambda e, dc=dc: e.activation(out=HC[:, dc, :], in_=CTXT[:, dc, :], func=AF.Identity,
                                                              bias=SH[:, dc, 1:2], scale=SC1[:, dc, 1:2]), [bCTXT, bMOD, bSC1], [bHC])
                def tail_ctx(n):
                    THI, bTHI, AA, bAA = THIs[n % 2], bTHIs[n % 2], AAs[n % 2], bAAs[n % 2]
                    scan("dve", HSC, AA[0], THI[0], 0.0, CTX, False, [bAA[0], bTHI[0]], [bHSC])
                    S.op("dve", lambda e, n=n: e.tensor_copy(out=HCO[:, n, 0:1], in_=HSC[:, CTX - 1:CTX]), [bHSC], [bHCO])
                    scan("dve", HSC, AA[1], THI[1], 0.0, CTX, True, [bAA[1], bTHI[1]], [bHSC])
                    S.op("dve", lambda e, n=n: e.tensor_copy(out=HCO[:, n, 1:2], in_=HSC[:, 0:1]), [bHSC], [bHCO])
                cur.update(ctx_cs)
                def extra_ctx(n):
                    adaln_step(0, 16 + 2 * n)
                    adaln_step(0, 16 + 2 * n + 1)
                run_fronts(HC, bHC, CTX, False, 0, False, tail_ctx, extra=extra_ctx)
                adaln_finish(0, MOD, bMOD, 16, 24, load_bm=False)
                cur.update(main_cs)
                H2 = AR.bf16("H2", NDC, TT + 3); bH2 = Buf("H2")
                Hj = [(H, bH), (H2, bH2)]
                make_h(0, H, bH)
                load_x(1)
                for j in range(2):

                    def tail_a(n, j=j):
                        THI, bTHI, AA, bAA = THIs[n % 2], bTHIs[n % 2], AAs[n % 2], bAAs[n % 2]
                        for dr in range(2):
                            rev = dr == 1
                            col = 0 if rev else TT - 1
                            scan("dve", HSC, AA[dr], THI[dr], 0.0, TT, rev, [bAA[dr], bTHI[dr]], [bHSC])
                            S.op("dve", lambda e, n=n, dr=dr, col=col, j=j: e.tensor_copy(out=EP[:, j, n, 2 * dr:2 * dr + 1], in_=HSC[:, col:col + 1]),
                                 [bHSC], [bEP])
                            S.op("act", lambda e, n=n, dr=dr, j=j: e.activation(out=EPP[:, j, n, dr:dr + 1], in_=RSUM[:, dr:dr + 1], func=AF.Exp,
                                                                             bias=HST[:, dr, n:n + 1], scale=HSs[:, dr, n:n + 1]),
                                 [bRSUM[dr], bHS], [bEPP])
                    def extra_a(n, j=j):
                        if n < 12:
                            adaln_step(1, 12 * j + n)
                        if j == 0 and n == 10:
                            make_h(1, H2, bH2)
                    run_fronts(Hj[j][0], Hj[j][1], TT, True, j, False, tail_a, extra=extra_a)
                for dr in range(2):
                    S.op("dve", lambda e, dr=dr: e.tensor_copy(out=EP[:, :, :, 2 * dr + 1], in_=EPP[:, :, :, dr]), [bEPP], [bEP])
                adaln_finish(1, MOD1, bMOD1)
                out_toks.append(S.dma("sp", "omod0", o_mod[0], MOD, reads=[bMOD]))
                out_toks.append(S.dma("sp", "omod1", o_mod[1], MOD1, reads=[bMOD1]))
                out_toks.append(S.dma("sp", "oep", o_ep, EP, reads=[bEP]))
                out_toks.append(S.dma("sp", "ohc", o_hc, HCO, reads=[bHCO]))

            if kind == "B":
                def outproj_burst(G):
                    jp = G // 4
                    for dc in range(NDC):
                        for c in range(2):
                            PP, bPP = (P6, bP6) if c == 0 else (P7, bP7)

                            def mm_o(e, dc=dc, c=c, PP=PP):
                                for i in range(4):
                                    r = e.matmul(PP, lhsT=WO[:, i, dc * 128:(dc + 1) * 128],
                                                 rhs=Z[:, i, c * 512:(c + 1) * 512], start=(i == 0), stop=(i == 3))
                                return r
                            S.op("pe", mm_o, [bWO, bZ], [bPP])
                            cs = slice(jp * TT + c * 512, jp * TT + (c + 1) * 512)
                            S.op("dve", lambda e, dc=dc, PP=PP, cs=cs: e.scalar_tensor_tensor(
                                out=XACC[:, dc, cs], in0=PP, scalar=GT[:, dc, 0:1], in1=XACC[:, dc, cs],
                                op0=ALU.mult, op1=ALU.add), [bPP, bMOD] + bXs([dc], [2 * jp + c]), bXs([dc], [2 * jp + c]))

                for j in range(2):
                    make_h(j)
                    S.op("dve", lambda e, j=j: e.tensor_scalar(out=XACC[:, :, j * TT:(j + 1) * TT], in0=XACC[:, :, j * TT:(j + 1) * TT],
                                                               scalar1=ALPHA, scalar2=None, op0=ALU.mult), bXs(tcs=[2 * j, 2 * j + 1]), bXs(tcs=[2 * j, 2 * j + 1]))

                    def pre_b(n):
                        slot = n % 2

                        def mm_gg(e, slot=slot):
                            for c in range(2):
                                for dc in range(NDC):
                                    r = e.matmul(P01[:, c * 512:(c + 1) * 512], lhsT=WG[slot][:, dc, :], rhs=H[:, dc, c * 512:(c + 1) * 512],
                                                 start=(dc == 0), stop=(dc == NDC - 1))
                            return r
                        S.op("pe", mm_gg, [bWG[slot], bH], [bP01])
                        S.op("act", lambda e: e.activation(out=TG, in_=P01, func=AF.Tanh, scale=0.5), [bP01], [bTG])
                        S.op("dve", lambda e: e.scalar_tensor_tensor(out=T1, in0=TG, scalar=1.0, in1=P01, op0=ALU.add, op1=ALU.mult),
                             [bTG, bP01], [bT1])

                    def tail_b(n, j=j):
                        THI, bTHI, AA, bAA = THIs[n % 2], bTHIs[n % 2], AAs[n % 2], bAAs[n % 2]
                        slot = n % 2
                        gi = n % 4
                        if gi == 0:
                            if j * 4 + n // 4 >= 1:
                                outproj_burst(j * 4 + n // 4 - 1)
                            S.dma("pool", "wo", WO, d_wout[n:n + 4].rearrange("a p d -> p a d"), writes=[bWO])
                        scan("dve", HF, AA[0], THI[0], CAR[:, j, 0, n:n + 1], TT, False, [bAA[0], bTHI[0], bCAR], [bHF])
                        scan("dve", HB, AA[1], THI[1], CAR[:, j, 1, n:n + 1], TT, True, [bAA[1], bTHI[1], bCAR], [bHB])
                        S.op("pool", lambda e: e.tensor_tensor(out=HF, in0=HF, in1=HB, op=ALU.add), [bHF, bHB], [bHF])

                        S.op("dve", lambda e, gi=gi: e.scalar_tensor_tensor(out=Z[:, gi, :], in0=T1, scalar=0.5, in1=HF, op0=ALU.mult, op1=ALU.mult),
                             [bT1, bHF], [bZ])
                    run_fronts(H, bH, TT, True, j, True, tail_b, pre=pre_b)
                outproj_burst(7)


        ln_alias = []
        ln_mark = None

        def L1_body():
            nonlocal ln_mark
            PS_ = AR.f32("PS", NCH); bPS = Buf("PS")
            HM1 = AR.f32("HM1", 2); bHM1 = Buf("HM1")
            S.dma("sp", "c0", PS_, d_ps, writes=[bPS])
            S.dma("sp", "c1", HM1, d_hm1, writes=[bHM1])
            NR = 47
            H1 = AR.bf16("H1", NDC, NR * 64); bH1 = Buf("H1")
            XHS = AR.f32("XHS", 960); bXHS = Buf("XHS")
            UW = 31 * 80 + 16
            ln_mark = AR.off
            U32 = AR.f32("U32", UW); bU32 = Buf("U32")
            ln_alias.append(bU32)
            B1 = AR.f32("B1", UW); bB1 = Buf("B1")
            B2 = AR.f32("B2", UW); bB2 = Buf("B2")
            ICNT = AR.f32("ICNT", TT); bICNT = Buf("ICNT")
            TMP = WM[1].rearrange("p a b -> p (a b)"); bTMP = bWM[1]
            DD = AR.bf16("DD", 4, TT); bDD = Buf("DD")
            WU = [AR.bf16("WU%d" % i, NDC, 128) for i in range(2)]; bWU = [Buf("WU0"), Buf("WU1")]
            WG = [AR.bf16("WG%d" % i, NDC, 128) for i in range(2)]; bWG = [Buf("WG0"), Buf("WG1")]
            PWb = AR.bf16("PWb", 4, 512); bPW = Buf("PWb")
            TG = WM[0].rearrange("p a b -> p (a b)"); bTG = bWM[0]
            Z = AR.bf16("Z", 4, TT); bZ = Buf("Z")
            WO = AR.bf16("WO", 4, D); bWO = Buf("WO")
            S.op("pool", lambda e: e.memset(U32, 0.0), [], [bU32])
            S.op("pool", lambda e: e.memset(B1, 0.0), [], [bB1])
            S.op("pool", lambda e: e.memset(B2, 0.0), [], [bB2])
            for dc in range(NDC):
                S.op("act", lambda e, dc=dc: e.activation(out=H1[:, dc, 512:512 + TOK], in_=XACC[:, dc, :], func=AF.Identity,
                                                          bias=SH[:, dc, 0:1], scale=SC1[:, dc, 0:1]), bXs([dc]) + [bMOD, bSC1], [bH1])
                XHb, bXHb = (XHS, bXHS) if dc % 2 == 0 else (T1[:, 0:960], bT1)
                S.dma("sp", "xhs%d" % (dc % 2), XHb, d_xhalo[:, dc, :], writes=[bXHb])
                S.op("act", lambda e, dc=dc, XHb=XHb: e.activation(out=H1[:, dc, 0:512], in_=XHb[:, 0:512], func=AF.Identity,
                                                                   bias=SH[:, dc, 0:1], scale=SC1[:, dc, 0:1]), [bXHb, bMOD, bSC1], [bH1])
                S.op("act", lambda e, dc=dc, XHb=XHb: e.activation(out=H1[:, dc, 512 + TOK:512 + TOK + 448], in_=XHb[:, 512:960], func=AF.Identity,
                                                                   bias=SH[:, dc, 0:1], scale=SC1[:, dc, 0:1]), [bXHb, bMOD, bSC1], [bH1])
            S.op("dve", lambda e: e.tensor_scalar(out=H1[:, :, 0:512], in0=H1[:, :, 0:512], scalar1=HM1[:, 0:1], scalar2=None, op0=ALU.mult),
                 [bH1, bHM1], [bH1])
            S.op("dve", lambda e: e.tensor_scalar(out=H1[:, :, 512 + TOK:512 + TOK + 448], in0=H1[:, :, 512 + TOK:512 + TOK + 448],
                                                  scalar1=HM1[:, 1:2], scalar2=None, op0=ALU.mult), [bH1, bHM1], [bH1])
            S.op("dve", lambda e: e.tensor_scalar(out=XACC, in0=XACC, scalar1=ALPHA, scalar2=None, op0=ALU.mult), bXs(), bXs())
            P03 = psum[:, 0:4, :].rearrange("p a b -> p (a b)")
            SHIFTS = [(1, 0), (1, 1), (2, 2), (4, 4)]
            P67 = psum[:, 6:8, :].rearrange("p a b -> p (a b)")
            bP67 = [bP6, bP7]
            DDs = [DD, AR.bf16("DD1", 4, TT)]; bDDs = [bDD, Buf("DD1")]

            pf = {"a": 0, "b": 0}

            def stage_a_chunk(j, k, gi, par):
                hw = 1 << k
                R = 15 + 2 * hw
                L = R * 64
                r0 = 8 + 16 * j - hw
                n = 4 * k + gi
                slot = n % 2
                if gi == 0:
                    S.dma("sp", "icnt", ICNT, d_icnt[k][:, j * TT:(j + 1) * TT], writes=[bICNT])
                if pf["a"] == 0:
                    S.dma("pool", "wu%d" % slot, WU[slot], d_win[n], writes=[bWU[slot]])
                pf["a"] += 1
                if pf["a"] < 32:
                    n2 = (n + 1) % NCH
                    S.dma("pool", "wu%d" % (n2 % 2), WU[n2 % 2], d_win[n2], writes=[bWU[n2 % 2]])

                def mm_u(e):
                    for c in range((L + 511) // 512):
                        w = min(512, L - c * 512)
                        for dc in range(NDC):
                            r = e.matmul(P03[:, c * 512:c * 512 + w], lhsT=WU[slot][:, dc, :],
                                         rhs=H1[:, dc, r0 * 64 + c * 512:r0 * 64 + c * 512 + w],
                                         start=(dc == 0), stop=(dc == NDC - 1))
                    return r
                S.op("pe", mm_u, [bWU[slot], bH1], [bP01, bP23])
                Uv = U32[:, 8:8 + R * 80].rearrange("p (r c) -> p r c", c=80)[:, :, 8:72]
                S.op("act", lambda e: e.activation(out=Uv, in_=P03[:, 0:L].rearrange("p (r c) -> p r c", c=64), func=AF.Identity),
                     [bP01, bP23], [bU32])
                src, bsrc = U32, bU32
                dsts = [(B1, bB1), (B2, bB2)]
                rng = [(hw, hw + 16)]
                for l in range(k, 0, -1):
                    a, b_ = SHIFTS[l]
                    rng.insert(0, (rng[0][0] - a, rng[0][1] + b_))
                for l in range(k + 1):
                    a, b_ = SHIFTS[l]
                    rlo, rhi = rng[l]
                    dst, bdst = dsts[l % 2]
                    lo, hi = 8 + rlo * 80, 8 + rhi * 80
                    S.op("dve", lambda e, dst=dst, src=src, a=a, b_=b_, lo=lo, hi=hi: e.tensor_tensor(
                        out=dst[:, lo:hi], in0=src[:, lo - 80 * a:hi - 80 * a], in1=src[:, lo + 80 * b_:hi + 80 * b_], op=ALU.add),
                        [bsrc], [bdst])
                    src, bsrc = dst, bdst
                lo, hi = 8 + hw * 80, 8 + (hw + 16) * 80
                for l in range(k + 1):
                    a, b_ = SHIFTS[l]
                    dst, bdst = dsts[(k + 1 + l) % 2]
                    S.op("dve", lambda e, dst=dst, src=src, a=a, b_=b_, lo=lo, hi=hi: e.tensor_tensor(
                        out=dst[:, lo:hi], in0=src[:, lo - a:hi - a], in1=src[:, lo + b_:hi + b_], op=ALU.add), [bsrc], [bdst])
                    src, bsrc = dst, bdst
                own = slice(8 + hw * 80, 8 + (hw + 16) * 80)
                RSv = src[:, own].rearrange("p (r c) -> p r c", c=80)[:, :, 8:72]
                UOv = U32[:, own].rearrange("p (r c) -> p r c", c=80)[:, :, 8:72]
                S.op("dve", lambda e: e.tensor_tensor(out=TMP.rearrange("p (r c) -> p r c", c=64), in0=RSv,
                                                      in1=ICNT.rearrange("p (r c) -> p r c", c=64), op=ALU.mult),
                     [bsrc, bICNT], [bTMP])
                S.op("dve", lambda e: e.tensor_tensor(out=DDs[par][:, gi, :].rearrange("p (r c) -> p r c", c=64),
                                                      in0=TMP.rearrange("p (r c) -> p r c", c=64), in1=UOv, op=ALU.subtract),
                     [bTMP, bU32], [bDDs[par]])

            def stage_b_chunk(j, k, hc, par):
                n = 4 * k + hc
                slot = n % 2
                if hc == 0:
                    if pf["b"] == 0:
                        S.dma("pool", "pw", PWb, d_pw[k], writes=[bPW])
                    S.dma("pool", "wo", WO, d_wout[4 * k:4 * k + 4].rearrange("a p d -> p a d"), writes=[bWO])
                if pf["b"] == 0:
                    S.dma("pool", "wg%d" % slot, WG[slot], d_win[NCH + n], writes=[bWG[slot]])
                pf["b"] += 1
                if pf["b"] < 32:
                    n2 = (n + 1) % NCH
                    S.dma("pool", "wg%d" % (n2 % 2), WG[n2 % 2], d_win[NCH + n2], writes=[bWG[n2 % 2]])

                def mm_gg(e):
                    for c in range(2):
                        for dc in range(NDC):
                            c0 = 512 + j * TT + c * 512
                            r = e.matmul(P45[:, c * 512:(c + 1) * 512], lhsT=WG[slot][:, dc, :], rhs=H1[:, dc, c0:c0 + 512],
                                         start=(dc == 0), stop=(dc == NDC - 1))
                    return r
                S.op("pe", mm_gg, [bWG[slot], bH1], [bP45])
                S.op("act", lambda e: e.activation(out=T1, in_=P45, func=AF.Silu), [bP45], [bT1])

                def mm_p(e):
                    for c in range(2):
                        for gc in range(4):
                            r = e.matmul(P67[:, c * 512:(c + 1) * 512], lhsT=PWb[:, gc, hc * 128:(hc + 1) * 128],
                                         rhs=DDs[par][:, gc, c * 512:(c + 1) * 512], start=(gc == 0), stop=(gc == 3))
                    return r
                S.op("pe", mm_p, [bPW, bDDs[par]], bP67)
                if hc == 3 and pf["b"] < 32:
                    S.dma("pool", "pw", PWb, d_pw[(k + 1) % 4], writes=[bPW])
                S.op("dve", lambda e: e.scalar_tensor_tensor(out=Z[:, hc, :], in0=P67, scalar=PS_[:, n:n + 1], in1=T1,
                                                             op0=ALU.mult, op1=ALU.mult), bP67 + [bPS, bT1], [bZ])

            def stage_b_out(j):
                for dc in range(NDC):
                    PP, bPP = (P45, [bP45]) if dc % 2 == 0 else (P67, bP67)

                    def mm_o(e, dc=dc, PP=PP):
                        for c in range(2):
                            for i in range(4):
                                r = e.matmul(PP[:, c * 512:(c + 1) * 512], lhsT=WO[:, i, dc * 128:(dc + 1) * 128],
                                             rhs=Z[:, i, c * 512:(c + 1) * 512], start=(i == 0), stop=(i == 3))
                        return r
                    S.op("pe", mm_o, [bWO, bZ], bPP)
                    S.op("dve", lambda e, dc=dc, PP=PP: e.scalar_tensor_tensor(
                        out=XACC[:, dc, j * TT:(j + 1) * TT], in0=PP, scalar=GT[:, dc, 0:1], in1=XACC[:, dc, j * TT:(j + 1) * TT],
                        op0=ALU.mult, op1=ALU.add), bPP + [bMOD] + bXs([dc], [2 * j, 2 * j + 1]), bXs([dc], [2 * j, 2 * j + 1]))

            groups = [(j, k) for j in range(2) for k in range(4)]
            for gi in range(4):
                stage_a_chunk(groups[0][0], groups[0][1], gi, 0)
            for g, (j, k) in enumerate(groups):
                par = g % 2
                for i in range(4):
                    if g + 1 < len(groups):
                        stage_a_chunk(groups[g + 1][0], groups[g + 1][1], i, 1 - par)
                    stage_b_chunk(j, k, i, par)
                stage_b_out(j)

        if kind == "L1":
            T1 = AR.f32("T1", TT); bT1 = Buf("T1")
            L1_body()

        if kind in ("B", "L1"):
            S.barrier()
            AR.off = ln_mark if kind == "L1" else work_mark
            P67 = psum[:, 6:8, :].rearrange("p a b -> p (a b)")
            ONES = AR.f32("ONES", 128); bONES = Buf("ONES")
            LNG = AR.f32("LNG", NDC); LNB = AR.f32("LNB", NDC); bLN = Buf("LN")
            SQT = [AR.f32("SQT%d" % i, 512) for i in range(2)]; bSQT = [Buf("SQT0"), Buf("SQT1")]
            MEANs = [AR.f32("MEAN%d" % i, 512) for i in range(2)]; bMEANs = [Buf("MEAN0"), Buf("MEAN1")]
            RSTDs = [AR.f32("RSTD%d" % i, 512) for i in range(2)]; bRSTDs = [Buf("RSTD0"), Buf("RSTD1")]
            S.op("pool", lambda e: e.memset(ONES, 1.0), [], [bONES])
            S.dma("sp", "c9", LNG, d_lng, writes=[bLN])
            S.dma("sp", "c10", LNB, d_lnb, writes=[bLN])
            for tc_ in range(4):
                cs = slice(tc_ * 512, (tc_ + 1) * 512)
                par = tc_ % 2
                MEAN, bMEAN, RSTD, bRSTD = MEANs[par], bMEANs[par], RSTDs[par], bRSTDs[par]
                PS1, bPS1, PS2, bPS2 = (P01, [bP01], P23, [bP23]) if par == 0 else (P45, [bP45], P67, [bP6, bP7])

                def mm_sum(e, cs=cs, PS1=PS1):
                    for dc in range(NDC):
                        r = e.matmul(PS1[:, 0:512], lhsT=ONES, rhs=XACC[:, dc, cs], start=(dc == 0), stop=(dc == NDC - 1))
                    return r
                S.op("pe", mm_sum, [bONES] + bXs(tcs=[tc_]), bPS1)
                for dc in range(NDC):
                    S.op("act", lambda e, dc=dc, cs=cs: e.activation(out=SQT[dc % 2], in_=XACC[:, dc, cs], func=AF.Square),
                         bXs([dc], [tc_]), [bSQT[dc % 2]])
                    S.op("pe", lambda e, dc=dc, PS2=PS2: e.matmul(PS2[:, 0:512], lhsT=ONES, rhs=SQT[dc % 2],
                                                                  start=(dc == 0), stop=(dc == NDC - 1)), [bONES, bSQT[dc % 2]], bPS2)
                S.op("dve", lambda e, MEAN=MEAN, PS1=PS1: e.tensor_scalar(out=MEAN, in0=PS1[:, 0:512], scalar1=1.0 / D, scalar2=None, op0=ALU.mult),
                     bPS1, [bMEAN])
                S.op("dve", lambda e, MEAN=MEAN, RSTD=RSTD: e.tensor_tensor(out=RSTD, in0=MEAN, in1=MEAN, op=ALU.mult), [bMEAN], [bRSTD])
                S.op("dve", lambda e, RSTD=RSTD, PS2=PS2: e.scalar_tensor_tensor(out=RSTD, in0=PS2[:, 0:512], scalar=1.0 / D, in1=RSTD,
                                                                                 op0=ALU.mult, op1=ALU.subtract), bPS2 + [bRSTD], [bRSTD])
                S.op("dve", lambda e, RSTD=RSTD: e.tensor_scalar(out=RSTD, in0=RSTD, scalar1=LN_EPS, scalar2=None, op0=ALU.add), [bRSTD], [bRSTD])
                S.op("act", lambda e, RSTD=RSTD: e.activation(out=RSTD, in_=RSTD, func=AF.Sqrt), [bRSTD], [bRSTD])
                S.op("dve", lambda e, RSTD=RSTD: e.reciprocal(out=RSTD, in_=RSTD), [bRSTD], [bRSTD])
                for dc in range(NDC):
                    eng = "dve" if dc % 3 != 2 else "pool"
                    bx = bXs([dc], [tc_])
                    S.op(eng, lambda e, dc=dc, cs=cs, MEAN=MEAN: e.tensor_tensor(out=XACC[:, dc, cs], in0=XACC[:, dc, cs], in1=MEAN, op=ALU.subtract),
                         bx + [bMEAN], bx)
                    S.op(eng, lambda e, dc=dc, cs=cs, RSTD=RSTD: e.tensor_tensor(out=XACC[:, dc, cs], in0=XACC[:, dc, cs], in1=RSTD, op=ALU.mult),
                         bx + [bRSTD], bx)
                    S.op("act", lambda e, dc=dc, cs=cs: e.activation(out=XACC[:, dc, cs], in_=XACC[:, dc, cs], func=AF.Identity,
                                                                     bias=LNB[:, dc:dc + 1], scale=LNG[:, dc:dc + 1]), bx + [bLN], bx)
                out_toks.append(S.dma("sp", "ox%d" % tc_, o_x[:, :, cs], XACC[:, :, cs], reads=bXs(tcs=[tc_])))


        S.final_wait("sp", out_toks)
        with nc.Block() as block:
            S.emit(block)
    return nc


def _fm(v):
    v = np.asarray(v, np.float32)
    n = v.shape[-1] // 128
    lead = v.shape[:-1]
    v = v.reshape(*lead, n, 128)
    perm = (len(lead) + 1, len(lead)) + tuple(range(len(lead)))
    return np.ascontiguousarray(v.transpose(perm))


_PROGS = {}


def _prog(kind):
    if kind not in _PROGS:
        _PROGS[kind] = build(kind)
    return _PROGS[kind]


def _common_maps(layer, c, c_ctx, w_mod, b_mod, w_in, with_mod=False):
    win = np.ascontiguousarray(np.asarray(w_in[layer], np.float32).reshape(NDC, 128, 32, 128).transpose(2, 1, 0, 3))
    maps = []
    extra = {}
    if with_mod:
        for l, sfx in ((0, ""), (1, "1")):
            extra["wm" + sfx] = np.ascontiguousarray(np.asarray(w_mod[l], np.float32).reshape(NDC, 128, 24, 128).transpose(2, 1, 0, 3))
            extra["bm" + sfx] = _fm(b_mod[l])
    for core in range(NCORES):
        b = core // 4
        m = {"win": win}
        if with_mod:
            m["cv"] = np.ascontiguousarray(np.stack([_fm(c[b]), _fm(c_ctx)], axis=-1))
            m.update(extra)
        maps.append(m)
    return maps


def layer0_inputs(x, ctx, conv_w, conv_b, lru_wa, lru_ba, lru_wx, lru_bx, lru_lam):
    cw = np.ascontiguousarray(_fm(conv_w[0]))
    cb = _fm(conv_b[0])
    gw = np.stack([lru_wa[0, 0], lru_wx[0, 0], lru_wa[0, 1], lru_wx[0, 1]], axis=0)
    gw = np.ascontiguousarray(np.asarray(gw, np.float32).transpose(1, 2, 0, 3))
    gb = np.stack([lru_ba[0, 0], lru_bx[0, 0], lru_ba[0, 1], lru_bx[0, 1]], axis=0)
    gb = np.ascontiguousarray(_fm(gb).transpose(0, 2, 1))
    lam = np.ascontiguousarray(_fm(lru_lam[0]).transpose(0, 2, 1))
    xs = np.asarray(x, np.float32)
    S_ = xs.shape[1]
    maps = []
    for core in range(NCORES):
        b, q = core // 4, core % 4
        t0 = q * TOK
        xt = _fm(xs[b, t0:t0 + TOK])
        xh = np.zeros((128, NDC, 2, 3), np.float32)
        hm = np.zeros((128, 2, 3), np.float32)
        for j in range(2):
            s = t0 + j * TT
            for k, t in enumerate((s - 2, s - 1, s + TT)):
                if 0 <= t < S_:
                    xh[:, :, j, k] = _fm(xs[b, t])
                    hm[:, j, k] = 1.0
        maps.append({"x": xt, "xh": xh, "hmask": hm, "ctx": _fm(np.asarray(ctx, np.float32)[b]),
                     "cw": cw, "cb": cb, "gw": gw, "gb": gb, "lam": lam})
    return maps


def run_A(inputs):
    cm = _common_maps(0, inputs["c"], inputs["c_ctx"], inputs["w_mod"], inputs["b_mod"], inputs["w_in"], with_mod=True)
    lm = layer0_inputs(inputs["x"], inputs["ctx"], inputs["conv_w"], inputs["conv_b"], inputs["lru_wa"], inputs["lru_ba"],
                       inputs["lru_wx"], inputs["lru_bx"], inputs["lru_lam"])
    maps = [dict(a, **b) for a, b in zip(cm, lm)]
    for core, m in enumerate(maps):
        q = core % 4
        sl = slice(4 * q, 4 * q + 4)
        m["winc"] = np.ascontiguousarray(m["win"][sl])
        m["gwc"] = np.ascontiguousarray(m["gw"][sl])
        m["cwc"] = np.ascontiguousarray(m["cw"][:, sl, :])
        m["cbc"] = np.ascontiguousarray(m["cb"][:, sl])
        m["gbc"] = np.ascontiguousarray(m["gb"][:, :, sl])
        m["lamc"] = np.ascontiguousarray(m["lam"][:, :, sl])
    res = run_bass_kernel_spmd(_prog("A"), maps, core_ids=list(range(NCORES)))
    mods = [(r["mod0"], r["mod1"]) for r in res.results]
    return maps, [r["ep"] for r in res.results], [r["hc"] for r in res.results], mods


def carries_layout(eps, hcs):
    outs = []
    for core in range(NCORES):
        b, q = core // 4, core % 4
        epf = np.zeros((128, 2, 7, NCH, 2), np.float32)
        epb = np.zeros((128, 2, 7, NCH, 2), np.float32)
        epf[..., 0] = 1.0
        epb[..., 0] = 1.0
        for j in range(2):
            v = 2 * q + j
            for s_, r in enumerate(range(0, v)):
                src = eps[b * 4 + r // 2][:, r % 2]
                epf[:, j, s_, :, 0] = src[:, :, 1]
                epf[:, j, s_, :, 1] = src[:, :, 0]
            for s_, r in enumerate(range(7, v, -1)):
                src = eps[b * 4 + r // 2][:, r % 2]
                epb[:, j, s_, :, 0] = src[:, :, 3]
                epb[:, j, s_, :, 1] = src[:, :, 2]
        hcin = np.concatenate([hcs[b * 4 + qq] for qq in range(4)], axis=1)
        outs.append({"epf": epf, "epb": epb, "hcin": np.ascontiguousarray(hcin)})
    return outs


def run_B(inputs, maps, eps, hcs, mods):
    wout = np.ascontiguousarray(np.asarray(inputs["w_out"][0], np.float32).reshape(NCH, 128, D))
    lng, lnb = _fm(inputs["ln_g"][0]), _fm(inputs["ln_b"][0])
    cl = carries_layout(eps, hcs)
    m2 = []
    for core in range(NCORES):
        m = {k: v for k, v in maps[core].items() if k not in ("cv", "wm", "wm1", "bm", "bm1", "winc", "gwc", "cwc", "cbc", "gbc", "lamc")}
        m["modin"] = np.ascontiguousarray(mods[core][0])
        m.update(cl[core])
        m.update({"wout": wout, "lng": lng, "lnb": lnb})
        m2.append(m)
    res = run_bass_kernel_spmd(_prog("B"), m2, core_ids=list(range(NCORES)))
    return [r["xo"] for r in res.results]


def run_L1(inputs, xo, mods):
    cm = _common_maps(1, inputs["c"], inputs["c_ctx"], inputs["w_mod"], inputs["b_mod"], inputs["w_in"])
    wout = np.ascontiguousarray(np.asarray(inputs["w_out"][1], np.float32).reshape(NCH, 128, D))
    lng, lnb = _fm(inputs["ln_g"][1]), _fm(inputs["ln_b"][1])
    pw = np.ascontiguousarray(np.asarray(inputs["pool_w"][0], np.float32).reshape(4, 4, 128, 512).transpose(0, 2, 1, 3))
    ps = _fm(inputs["pool_scale"][0])
    cols = np.arange(64)
    maps = []
    for core in range(NCORES):
        b, q = core // 4, core % 4
        xhalo = np.zeros((128, NDC, 960), np.float32)
        hm1 = np.zeros((128, 2), np.float32)
        if q > 0:
            xhalo[:, :, 0:512] = xo[core - 1][:, :, TOK - 512:TOK]
            hm1[:, 0] = 1.0
        if q < 3:
            xhalo[:, :, 512:960] = xo[core + 1][:, :, 0:448]
            hm1[:, 1] = 1.0
        icnt = np.zeros((4, 128, TOK), np.float32)
        rows = q * 32 + np.arange(32)
        for k, w in enumerate((2, 4, 8, 16)):
            cc = (np.minimum(cols + w // 2, 64) - np.maximum(cols - w // 2, 0)).astype(np.float32)
            cr = (np.minimum(rows + w // 2, 128) - np.maximum(rows - w // 2, 0)).astype(np.float32)
            icnt[k] = (np.float32(1.0) / (cr[:, None] * cc[None, :])).reshape(-1)[None, :]
        m = dict(cm[core])
        m["modin"] = np.ascontiguousarray(mods[core][1])
        m.update({"x": np.ascontiguousarray(xo[core]), "xhalo": xhalo, "hm1": hm1, "icnt": icnt, "pw": pw, "ps": ps,
                  "wout": wout, "lng": lng, "lnb": lnb})
        maps.append(m)
    res = run_bass_kernel_spmd(_prog("L1"), maps, core_ids=list(range(NCORES)))
    return [r["xo"] for r in res.results]


def kernel(**inputs):
    maps, eps, hcs, mods = run_A(inputs)
    xo = run_B(inputs, maps, eps, hcs, mods)
    xo = run_L1(inputs, xo, mods)
    x = np.asarray(inputs["x"])
    out = np.zeros(x.shape, np.float32)
    for core in range(NCORES):
        b, q = core // 4, core % 4
        out[b, q * TOK:(q + 1) * TOK] = xo[core].transpose(2, 1, 0).reshape(TOK, D)
    return out
```
